# Optimizing a Trainium2 kernel written in Bass

```python
import math
import jax, jax.numpy as jnp
from jax import lax
import numpy as np

D_MODEL = 4096
BATCH = 32
SEQ = 256
DEPTH = 1
DEC_BATCH = 4
DEC_SEQ = 1024
PAST_LEN = 256

GRID_W = 64
N_HEADS = D_MODEL // 512
QK_DIM = 128
V_DIM = 2 * QK_DIM
D_ATT = N_HEADS * V_DIM
D_CONV = D_MODEL - D_ATT
D_MIX = D_ATT + D_CONV
PROJ_W = 3 * D_ATT + 3 * D_CONV
CONV_K = 3
D_FF = 4 * D_MODEL
N_MOD = 6
ROPE_AXIS_DIM = QK_DIM // 2
ROPE_THETA = 10000.0
Q_BLOCK = 128
EPS = 1e-6

kernel_name = 'hybrid_diffattn_shortconv_dit_step'


def rmsnorm(x, g):
    xf = x.astype(jnp.float32)
    y = xf * lax.rsqrt(jnp.mean(xf * xf, axis=-1, keepdims=True) + EPS)
    return (y * g.astype(jnp.float32)).astype(x.dtype)


def axial_angles(n_tokens):
    rows = n_tokens // GRID_W
    row = jnp.repeat(jnp.arange(rows, dtype=jnp.float32), GRID_W)
    col = jnp.tile(jnp.arange(GRID_W, dtype=jnp.float32), rows)
    inv = jnp.power(ROPE_THETA, -jnp.arange(0, ROPE_AXIS_DIM, 2, dtype=jnp.float32) / ROPE_AXIS_DIM)
    return row[:, None] * inv, col[:, None] * inv


def rotate_axis(xp, ang):
    cos = jnp.cos(ang)[:, None, None, :].astype(xp.dtype)
    sin = jnp.sin(ang)[:, None, None, :].astype(xp.dtype)
    x1, x2 = jnp.split(xp, 2, axis=-1)
    return jnp.concatenate([x1 * cos - x2 * sin, x1 * sin + x2 * cos], axis=-1)


def apply_axial_rope(x, row_ang, col_ang):
    return jnp.concatenate([rotate_axis(x[..., :ROPE_AXIS_DIM], row_ang),
                            rotate_axis(x[..., ROPE_AXIS_DIM:], col_ang)], axis=-1)


def modulation(cond, w_ada, b_ada):
    s = jax.nn.silu(cond) @ w_ada + b_ada
    return s.reshape(cond.shape[:-1] + (N_MOD, D_MODEL))


def project_heads(h, w_in, q_norm_g, k_norm_g):
    b, s, _ = h.shape
    proj = h @ w_in
    cuts = [D_ATT, 2 * D_ATT, 3 * D_ATT, 3 * D_ATT + D_CONV, 3 * D_ATT + 2 * D_CONV]
    q, k, v, b_gate, c_gate, u = jnp.split(proj, cuts, axis=-1)
    q = rmsnorm(q.reshape(b, s, N_HEADS, 2, QK_DIM), q_norm_g)
    k = rmsnorm(k.reshape(b, s, N_HEADS, 2, QK_DIM), k_norm_g)
    v = v.reshape(b, s, N_HEADS, V_DIM)
    return q, k, v, b_gate, c_gate, u


def short_conv(b_gate, c_gate, u, conv_w):
    z = c_gate * u
    zp = jnp.pad(z, ((0, 0), (1, 1), (0, 0)))
    y = conv_w[0] * zp[:, :-2] + conv_w[1] * zp[:, 1:-1] + conv_w[2] * zp[:, 2:]
    return b_gate * y


def diff_attention(q, k, v, lam, lam_init, subln_g):
    b, sq = q.shape[0], q.shape[1]
    nb = sq // Q_BLOCK
    qb = q.reshape(b, nb, Q_BLOCK, N_HEADS, 2, QK_DIM).swapaxes(0, 1)
    scale = QK_DIM ** -0.5

    def block(qi):
        s = jnp.einsum('bqhcd,bkhcd->bchqk', qi, k).astype(jnp.float32) * scale
        p = jax.nn.softmax(s, axis=-1)
        a = (p[:, 0] - lam * p[:, 1]).astype(v.dtype)
        return jnp.einsum('bhqk,bkhe->bqhe', a, v)

    o = lax.map(block, qb)
    o = o.swapaxes(0, 1).reshape(b, sq, N_HEADS, V_DIM)
    o = rmsnorm(o, subln_g) * (1.0 - lam_init)
    return o.reshape(b, sq, D_ATT)


def trunk_layer(x, mod, cached_k, cached_v, is_latent, layer_idx,
                norm_attn_g, w_in, q_norm_g, k_norm_g, lam_q1, lam_k1, lam_q2, lam_k2,
                subln_g, conv_w, w_out, norm_mlp_g, w_mlp_in, w_mlp_out):
    shift1, scale1, gate1, shift2, scale2, gate2 = [mod[:, None, i] for i in range(N_MOD)]
    h = rmsnorm(x, norm_attn_g) * (1.0 + scale1) + shift1
    q, k, v, b_gate, c_gate, u = project_heads(h, w_in, q_norm_g, k_norm_g)
    lam_init = 0.8 - 0.6 * math.exp(-0.3 * layer_idx)
    f32 = jnp.float32
    lam = (jnp.exp(jnp.sum(lam_q1.astype(f32) * lam_k1.astype(f32)))
           - jnp.exp(jnp.sum(lam_q2.astype(f32) * lam_k2.astype(f32))) + lam_init)
    if is_latent:
        row_ang, col_ang = axial_angles(x.shape[1])
        q = apply_axial_rope(q, row_ang, col_ang)
        k = apply_axial_rope(k, row_ang, col_ang)
        keys = jnp.concatenate([k, cached_k], axis=1)
        vals = jnp.concatenate([v, cached_v], axis=1)
    else:
        keys, vals = k, v
    attn = diff_attention(q, keys, vals, lam, lam_init, subln_g)
    conv = short_conv(b_gate, c_gate, u, conv_w)
    x = x + gate1 * (jnp.concatenate([attn, conv], axis=-1) @ w_out)
    h2 = rmsnorm(x, norm_mlp_g) * (1.0 + scale2) + shift2
    x = x + gate2 * (jnp.square(jax.nn.relu(h2 @ w_mlp_in)) @ w_mlp_out)
    return x, k, v


def setup_inputs(seed: int = 0) -> dict:
    key = jax.random.key(seed)
    ks = jax.random.split(key, 24)
    f32 = jnp.float32
    nrm = lambda k, shp, s: jax.random.normal(k, shp, f32) * s
    return {
        'x_prompt': nrm(ks[0], (BATCH, SEQ, D_MODEL), 1.0),
        'x_sample': nrm(ks[1], (DEC_BATCH, DEC_SEQ, D_MODEL), 1.0),
        'cache_k': nrm(ks[2], (DEC_BATCH, DEPTH, PAST_LEN, N_HEADS, 2, QK_DIM), 1.0),
        'cache_v': nrm(ks[3], (DEC_BATCH, DEPTH, PAST_LEN, N_HEADS, V_DIM), 1.0),
        'c': nrm(ks[4], (DEC_BATCH, D_MODEL), 1.0),
        'c_ctx': nrm(ks[5], (D_MODEL,), 1.0),
        'w_ada': nrm(ks[6], (DEPTH, D_MODEL, N_MOD * D_MODEL), D_MODEL ** -0.5),
        'b_ada': nrm(ks[7], (DEPTH, N_MOD * D_MODEL), 0.01),
        'norm_attn_g': 1.0 + nrm(ks[8], (DEPTH, D_MODEL), 0.02),
        'w_in': nrm(ks[9], (DEPTH, D_MODEL, PROJ_W), D_MODEL ** -0.5),
        'q_norm_g': 1.0 + nrm(ks[10], (DEPTH, QK_DIM), 0.02),
        'k_norm_g': 1.0 + nrm(ks[11], (DEPTH, QK_DIM), 0.02),
        'lambda_q1': nrm(ks[12], (DEPTH, QK_DIM), 0.1),
        'lambda_k1': nrm(ks[13], (DEPTH, QK_DIM), 0.1),
        'lambda_q2': nrm(ks[14], (DEPTH, QK_DIM), 0.1),
        'lambda_k2': nrm(ks[15], (DEPTH, QK_DIM), 0.1),
        'subln_g': 1.0 + nrm(ks[16], (DEPTH, V_DIM), 0.02),
        'conv_w': nrm(ks[17], (DEPTH, CONV_K, D_CONV), CONV_K ** -0.5),
        'w_out': nrm(ks[18], (DEPTH, D_MIX, D_MODEL), D_MIX ** -0.5),
        'norm_mlp_g': 1.0 + nrm(ks[19], (DEPTH, D_MODEL), 0.02),
        'w_mlp_in': nrm(ks[20], (DEPTH, D_MODEL, D_FF), D_MODEL ** -0.5),
        'w_mlp_out': nrm(ks[21], (DEPTH, D_FF, D_MODEL), D_FF ** -0.5),
    }


def reference(x_prompt, x_sample, cache_k, cache_v, c, c_ctx, w_ada, b_ada, norm_attn_g, w_in,
              q_norm_g, k_norm_g, lambda_q1, lambda_k1, lambda_q2, lambda_k2, subln_g, conv_w,
              w_out, norm_mlp_g, w_mlp_in, w_mlp_out):
    y_p = x_prompt
    y_s = x_sample
    ctx_keys = []
    ctx_vals = []
    for l in range(DEPTH):
        layer_w = (norm_attn_g[l], w_in[l], q_norm_g[l], k_norm_g[l], lambda_q1[l], lambda_k1[l],
                   lambda_q2[l], lambda_k2[l], subln_g[l], conv_w[l], w_out[l], norm_mlp_g[l],
                   w_mlp_in[l], w_mlp_out[l])
        mod_ctx = modulation(c_ctx[None], w_ada[l], b_ada[l])
        y_p, k_l, v_l = trunk_layer(y_p, mod_ctx, None, None, False, l, *layer_w)
        ctx_keys.append(k_l)
        ctx_vals.append(v_l)
        mod_lat = modulation(c, w_ada[l], b_ada[l])
        y_s, _, _ = trunk_layer(y_s, mod_lat, cache_k[:, l], cache_v[:, l], True, l, *layer_w)
    new_k = jnp.stack(ctx_keys, axis=1)
    new_v = jnp.stack(ctx_vals, axis=1)
    return (y_p, y_s, new_k, new_v)
```

```python
import math
from contextlib import ExitStack

import numpy as np
import concourse.bass as bass
import concourse.mybir as mybir
from concourse.bass_utils import run_bass_kernel_spmd

F32 = mybir.dt.float32
BF16 = mybir.dt.bfloat16
AF = mybir.ActivationFunctionType
ALU = mybir.AluOpType

NCORES = 8
DM = 4096
NH = 8
EPS = 1e-6
LAM_INIT = 0.8 - 0.6 * math.exp(-0.3 * 0)
NW = 4
STAGES = {"all"}
NADA = 192
P2LIM = None
DBGQ = 0
WMOD = None

ADA0 = 0
WIN0 = 192
WOUT0 = WIN0 + 96
MIN0 = WOUT0 + 32
MOUT0 = MIN0 + 128
NBLK = MOUT0 + 128

V_BADA = 0
V_GATT = 192
V_GMLP = 224
V_QG = 256
V_KG = 257
V_CONV = 258
V_SUB = 306
V_LAM = 308
V_MASK = 312
NV = 320


class Op:
    __slots__ = ("eng", "fn", "deps", "sig", "dma", "val", "idx")


class Em:
    def __init__(self):
        self.ops = []
        self.kw = {}
        self.kr = {}

    def op(self, eng, fn, r=(), w=(), dma=None, partial=False):
        o = Op()
        o.eng, o.fn, o.dma, o.sig, o.val, o.idx = eng, fn, dma, dma is not None, None, len(self.ops)
        deps = {}
        psr = [k for k in r if k[0] == "ps" and k not in w]
        r = [k for k in r if k[0] != "ps"]
        for k in psr:
            for p in self.kw.get(k, ()):
                deps[p.idx] = p
            for p in self.kr.get(k, ()):
                deps[p.idx] = p
        for k in r:
            for p in self.kw.get(k, ()):
                deps[p.idx] = p
        for k in w:
            for p in self.kw.get(k, ()):
                deps[p.idx] = p
            for p in self.kr.get(k, ()):
                deps[p.idx] = p
        o.deps = [p for p in deps.values() if not (p.eng == "pe" and eng == "pe")]
        for p in o.deps:
            p.sig = True
        for k in r:
            lst = self.kr.setdefault(k, [])
            lst[:] = [q for q in lst if q.dma is not None or q.eng != eng]
            lst.append(o)
        for k in psr:
            self.kr[k] = []
            lst = self.kw.setdefault(k, [])
            lst[:] = [q for q in lst if q.dma is not None or q.eng != eng]
            lst.append(o)
        for k in w:
            self.kr[k] = []
            if partial:
                lst = self.kw.setdefault(k, [])
                lst[:] = [q for q in lst if q.dma is not None or q.eng != eng]
                lst.append(o)
            else:
                self.kw[k] = [o]
        self.ops.append(o)
        return o


def build_program():
    nc = bass.Bass("TRN2", target_bir_lowering=False)

    def din(name, shape):
        return nc.dram_tensor(name, shape, F32, kind="ExternalInput").ap()

    def dout(name, shape):
        return nc.dram_tensor(name, shape, F32, kind="ExternalOutput").ap()

    wblk = din("wblk", [WMOD or NBLK, 128, 4096])
    xp = din("xp", [1024, DM])
    xso = din("xso", [512, DM])
    xsx = din("xsx", [512, DM])
    ckT = din("ckT", [128, 4096])
    cv = din("cv", [256, 2048])
    condT = din("condT", [128, 64])
    vecs_d = din("vecs", [128, NV])
    ropec_d = din("ropec", [128, 1024])
    ropes_d = din("ropes", [128, 1024])
    consts_d = din("consts", [128, 384])
    yp = dout("yp", [1024, DM])
    ys = dout("ys", [512, DM])
    nk = dout("nk", [1024, 2048])
    nv = dout("nv", [1024, 2048])

    E = Em()
    es = ExitStack()

    def sb(name, shape, dt):
        return es.enter_context(nc.sbuf_tensor("s_" + name, shape, dt))

    R1 = sb("R1", [128, 16384], F32)
    RH = sb("RH", [128, 16384], BF16)
    RM = sb("RM", [128, 16384], BF16)
    RT = sb("RT", [128, 8192], F32)
    WB = [sb(f"WB{i}", [128, 4096], BF16) for i in range(NW)]
    ident = sb("ident", [128, 128], F32)
    ones_f = sb("ones_f", [128, 128], F32)
    ones_b = sb("ones_b", [128, 128], BF16)
    rot_b = sb("rot_b", [128, 128], BF16)
    vecs = sb("vecs", [128, NV], F32)
    ropec = sb("ropec", [128, 1024], F32)
    ropes = sb("ropes", [128, 1024], F32)
    cond_s = sb("cond_s", [128, 64], F32)
    scT = sb("scT", [128, 64], BF16)
    modT = sb("modT", [128, 384], F32)
    dv = sb("dv", [128, 6 * 64], F32)
    sm = sb("sm", [128, 64], F32)
    hhalo = sb("hhalo", [128, 64], BF16)
    PS = [es.enter_context(nc.psum_tensor(f"ps{i}", [128, 512], F32)) for i in range(8)]

    xT = R1[:].rearrange("p (c t) -> p c t", c=32)
    R1b = R1[:].bitcast(BF16)
    kTo = R1b[:, 0:8192].rearrange("p (c t) -> p c t", c=16)
    vo = R1b[:, 8192:16384].rearrange("p (c e) -> p c e", c=4)
    kTc = R1b[:, 16384:20480].rearrange("p (c t) -> p c t", c=16)
    vc = R1b[:, 20480:24576].rearrange("p (c e) -> p c e", c=2)
    hT = RH[:].rearrange("p (c t) -> p c t", c=32)
    mixT = RM[:].rearrange("p (c t) -> p c t", c=32)
    RTb = RT[:].bitcast(BF16)

    def K1(lo, hi=None):
        return [("R1", i) for i in range(lo, (lo + 1) if hi is None else hi)]

    KTO = K1(0, 8)
    KVO = K1(8, 16)
    KTC = K1(16, 20)
    KVC = K1(20, 24)

    HM = ["RH", "RM"]

    def KH(kc):
        return (HM[0], kc)

    def KM(kc):
        return (HM[1], kc)

    def swap_hm():
        nonlocal hT, mixT
        hT, mixT = mixT, hT
        HM.reverse()

    def rt_f32(slot, n):
        return RT[:, slot * 256: slot * 256 + n]

    def rt_b16(slot, n):
        return RTb[:, slot * 512: slot * 512 + n]

    def KT(lo, n):
        return [("RT", i) for i in range(lo, lo + n)]

    xst = [RT[:, 0:4096], RT[:, 4096:8192]]
    KXST = [KT(0, 16), KT(16, 16)]
    t_qT = rt_b16(0, 1024).rearrange("p (c t) -> p c t", c=2); K_QT = KT(0, 2)
    t_kT = rt_b16(2, 1024).rearrange("p (c t) -> p c t", c=2); K_KT = KT(2, 2)
    t_vh = rt_b16(4, 1024).rearrange("p (c e) -> p c e", c=4); K_VH = KT(4, 2)
    QTs = [t_qT, R1b[:, 24576:25600].rearrange("p (c t) -> p c t", c=2)]
    KTs = [t_kT, R1b[:, 25600:26624].rearrange("p (c t) -> p c t", c=2)]
    VHs = [t_vh, R1b[:, 26624:27648].rearrange("p (c e) -> p c e", c=4)]
    KQs = [K_QT, K1(24)]
    KKs = [K_KT, K1(25)]
    KVs = [K_VH, K1(26)]
    t_E = [rt_b16(6, 512), rt_b16(7, 512)]; K_E = [KT(6, 1), KT(7, 1)]
    t_R0 = rt_f32(8, 1024).rearrange("p (c t) -> p c t", c=2); K_R0 = KT(8, 4)
    t_rz = rt_f32(12, 512); K_RZ = KT(12, 2)
    t_t1 = rt_f32(14, 512); K_T1 = KT(14, 2)
    t_sq = rt_b16(16, 512); K_SQ = KT(16, 1)
    t_qgb = rt_b16(17, 512); K_QGB = KT(17, 1)
    t_qg = rt_f32(18, 512); K_QG = KT(18, 2)
    t_rs = rt_f32(20, 512); K_RS = KT(20, 2)
    t_t2 = rt_f32(22, 512); K_T2 = KT(22, 2)
    t_vT = rt_f32(24, 512); K_VT = KT(24, 2)
    t_st = [rt_f32(26, 512), rt_f32(28, 512)]; K_ST = [KT(26, 2), KT(28, 2)]
    t_dsq = rt_b16(30, 1024).rearrange("p (c t) -> p c t", c=2); K_DSQ = KT(30, 2)

    def vcol(c, n=1):
        return vecs[:, c:c + n]

    def dvv(idx, kc, j):
        c = idx * 64 + kc * 2 + j
        return dv[:, c:c + 1]

    live = [False] * 8
    rr = [0]

    def balloc():
        for t in range(8):
            b = (rr[0] + t) % 8
            if not live[b]:
                live[b] = True
                rr[0] = (b + 1) % 8
                return b
        raise RuntimeError("no free PSUM bank")

    def bfree(b):
        live[b] = False

    def KP(b):
        return ("ps", b)

    def dma(q, out, in_, r, w, key, **kw):
        E.op(q, lambda e: e.dma_start(out=out, in_=in_, **kw), r, w, dma=key)

    def act(out, in_, func, r, w, scale=None, bias=None, partial=False):
        kw = {}
        if scale is not None:
            kw["scale"] = scale
        if bias is not None:
            kw["bias"] = bias
        E.op("act", lambda e: e.activation(out=out, in_=in_, func=func, **kw), r, w, partial=partial)

    def tt(out, in0, in1, op, r, w, partial=False):
        E.op("dve", lambda e: e.tensor_tensor(out=out, in0=in0, in1=in1, op=op), r, w, partial=partial)

    def ts(out, in0, s1, op0, r, w, s2=None, op1=None, partial=False):
        if op1 is None:
            E.op("dve", lambda e: e.tensor_scalar(out=out, in0=in0, scalar1=s1, scalar2=None, op0=op0), r, w,
                 partial=partial)
        else:
            E.op("dve", lambda e: e.tensor_scalar(out=out, in0=in0, scalar1=s1, scalar2=s2, op0=op0, op1=op1), r, w,
                 partial=partial)

    def stt(out, in0, scalar, in1, op0, op1, r, w, partial=False):
        E.op("dve", lambda e: e.scalar_tensor_tensor(out=out, in0=in0, scalar=scalar, in1=in1, op0=op0, op1=op1),
             r, w, partial=partial)

    def cp(eng, out, in_, r, w, partial=False):
        if eng == "dve":
            E.op("dve", lambda e: e.tensor_copy(out=out, in_=in_), r, w, partial=partial)
        else:
            E.op("act", lambda e: e.activation(out=out, in_=in_, func=AF.Copy), r, w, partial=partial)

    def mm(out, lhsT, rhs, start, stop, r, w):
        E.op("pe", lambda e: e.matmul(out, lhsT=lhsT, rhs=rhs, start=start, stop=stop), r, w, partial=True)

    def tr(out, in_, r, w):
        E.op("pe", lambda e: e.transpose(out, in_, ident[:]), r + [("c", "ident")], w, partial=True)

    CK = [("c", "k")]

    wcnt = [0]

    def wload(wid):
        b = wcnt[0] % NW
        wcnt[0] += 1
        dma("pool", WB[b][:], wblk[wid % WMOD if WMOD else wid], [], [("WB", b)], ("WB", b), max_dma_last_dim=8192)
        return b

    def main_mm(b, bank, rhs_list, rkeys, n):
        def fn(e):
            last = None
            for kc in range(32):
                last = e.matmul(PS[bank][:, 0:n], lhsT=WB[b][:, kc * 128:(kc + 1) * 128], rhs=rhs_list[kc],
                                start=(kc == 0), stop=(kc == 31))
            return last
        E.op("pe", fn, [("WB", b)] + rkeys, [KP(bank)], partial=True)

    def phase0():
        dma("sp", ident[:], consts_d[:, 0:128], [], [("c", "ident")], "c0a")
        dma("sp", ones_f[:], consts_d[:, 128:256], [], [("c", "onesf")], "c0b")
        dma("sp", vecs[:], vecs_d, [], CK, "c0c")
        dma("sp", ropec[:], ropec_d, [], [("c", "ropec")], "c1a")
        dma("sp", ropes[:], ropes_d, [], [("c", "ropes")], "c1b")
        dma("sp", cond_s[:], condT, [], [("c", "cond")], "c2")
        dma("pool", ones_b[:], consts_d[:, 128:256], [], [("c", "onesb")], "c3")
        dma("pool", rot_b[:], consts_d[:, 256:384], [], [("c", "rot")], "c4")
        act(scT[:], cond_s[:], AF.Silu, [("c", "cond")], [("c", "scT")])
        tt(sm[:, 0:1], vcol(V_LAM), vcol(V_LAM + 1), ALU.mult, CK, [("sm", 0)])
        tt(sm[:, 1:2], vcol(V_LAM + 2), vcol(V_LAM + 3), ALU.mult, CK, [("sm", 1)])
        b = balloc()
        E.op("pe", lambda e: e.matmul(PS[b][:, 0:2], lhsT=ones_f[:], rhs=sm[:, 0:2], start=True, stop=True),
             [("sm", 0), ("sm", 1), ("c", "onesf")], [KP(b)], partial=True)
        act(sm[:, 2:4], PS[b][:, 0:2], AF.Exp, [KP(b)], [("sm", 2)])
        bfree(b)
        tt(sm[:, 4:5], sm[:, 2:3], sm[:, 3:4], ALU.subtract, [("sm", 2)], [("sm", 4)])
        ts(sm[:, 5:6], sm[:, 4:5], -1.0, ALU.mult, [("sm", 4)], [("c", "neglam")], s2=-LAM_INIT, op1=ALU.add)
        ts(sm[:, 6:8], vcol(V_SUB, 2), 1.0 - LAM_INIT, ALU.mult, CK, [("c", "sg")])
        for oc in range(min(64, NADA)):
            ada_block(oc)
        derive(0)
        derive(1)

    def ada_block(oc):
        wb = wload(ADA0 + oc)
        bank = balloc()

        def fn(e, wb=wb, bank=bank):
            last = None
            for kc in range(32):
                last = e.matmul(PS[bank][:, 0:2], lhsT=WB[wb][:, kc * 128:(kc + 1) * 128],
                                rhs=scT[:, kc * 2:kc * 2 + 2], start=(kc == 0), stop=(kc == 31))
            return last
        E.op("pe", fn, [("WB", wb), ("c", "scT")], [KP(bank)], partial=True)
        ts(modT[:, oc * 2:oc * 2 + 2], PS[bank][:, 0:2], vcol(V_BADA + oc), ALU.add, [KP(bank)] + CK,
           [("c", "mod")], partial=True)
        bfree(bank)

    ada_next = [64]

    def ada_some(n=1):
        for _ in range(n):
            if ada_next[0] < NADA:
                oc = ada_next[0]
                ada_next[0] += 1
                ada_block(oc)
                if oc == 95:
                    derive(2)
                elif oc == 159:
                    derive(3)
                    derive(4)
                elif oc == 191:
                    derive(5)

    def derive(dst):
        m4 = modT[:].rearrange("p (i c j) -> p i c j", i=6, c=32)
        d4 = dv[:].rearrange("p (i c j) -> p i c j", i=6, c=32)
        for j in range(2):
            if dst in (0, 3):
                src, g0 = (1, V_GATT) if dst == 0 else (4, V_GMLP)
                stt(d4[:, dst, :, j], m4[:, src, :, j], 1.0, vcol(g0, 32), ALU.add, ALU.mult,
                    [("c", "mod")] + CK, [("c", "dv", dst)], partial=True)
            else:
                src = {1: 0, 2: 2, 4: 3, 5: 5}[dst]
                E.op("dve", lambda e, s_=src, d=dst, j=j: e.tensor_copy(out=d4[:, d, :, j], in_=m4[:, s_, :, j]),
                     [("c", "mod")], [("c", "dv", dst)], partial=True)

    def CDVI(*idx):
        return [("c", "dv", i) for i in idx]

    def p1(xrows, j):
        for t4 in range(4):
            p1_piece(xrows, j, t4)

    def p1_piece(xrows, j, t4, part="ab", sfix=None, dst=None):
        s = t4 % 2 if sfix is None else sfix
        hD, KD = (hT, KH) if dst is None else dst
        if "a" in part:
            dma("sp", xst[s], xrows[t4 * 128:(t4 + 1) * 128, :], [], KXST[s], ("xst", s))
            for q in range(8):
                E.op("dve", lambda e, q=q, s=s: e.bn_stats(out=sm[:, 8 + q * 6: 14 + q * 6],
                                                       in_=xst[s][:, q * 512:(q + 1) * 512]),
                     KXST[s], [("sm", "bn")], partial=True)
            E.op("dve", lambda e: e.bn_aggr(out=sm[:, 56:58], in_=sm[:, 8:56]), [("sm", "bn")], [("sm", "agg")])
            stt(sm[:, 58:59], sm[:, 56:57], sm[:, 56:57], sm[:, 57:58], ALU.mult, ALU.add, [("sm", "agg")],
                [("sm", "ms")])
            act(sm[:, 59:60], sm[:, 58:59], AF.Ln, [("sm", "ms")], [("sm", "ln")], bias=EPS)
            act(sm[:, 60:61], sm[:, 59:60], AF.Exp, [("sm", "ln")], [("sm", "rstd")], scale=-0.5)
            ts(xst[s][:, 0:2048], xst[s][:, 0:2048], sm[:, 60:61], ALU.mult, KXST[s] + [("sm", "rstd")],
               KXST[s][0:8], partial=True)
            act(xst[s][:, 2048:4096], xst[s][:, 2048:4096], AF.Copy, KXST[s] + [("sm", "rstd")], KXST[s][8:16],
                scale=sm[:, 60:61], partial=True)
        if "b" in part:
            for g in range(8):
                bank = balloc()
                for q in range(4):
                    kc = 4 * g + q
                    tr(PS[bank][:, q * 128:(q + 1) * 128], xst[s][:, kc * 128:(kc + 1) * 128],
                       [KXST[s][kc // 2]], [KP(bank)])
                for q in range(4):
                    kc = 4 * g + q
                    act(hD[:, kc, t4 * 128:(t4 + 1) * 128], PS[bank][:, q * 128:(q + 1) * 128], AF.Identity,
                        [KP(bank)] + CDVI(0, 1), [KD(kc)], scale=dvv(0, kc, j), bias=dvv(1, kc, j), partial=True)
                bfree(bank)

    def qk_stage0(bank, gcol, rope):
        if DBGQ == 1:
            cp("dve", t_qg, PS[bank][:, :], [KP(bank)], K_QG)
            bfree(bank)
            return
        if DBGQ == 3:
            act(t_sq, PS[bank][:, :], AF.Square, [KP(bank)], K_SQ)
            bfree(bank)
            return
        if DBGQ == 4:
            ts(t_qg, PS[bank][:, :], vcol(gcol), ALU.mult, [KP(bank)] + CK, K_QG)
            bfree(bank)
            return
        if DBGQ == 5:
            act(hT[:, 0, :], PS[bank][:, :], AF.Square, [KP(bank)], [KH(0)])
            bfree(bank)
            return
        act(t_sq, PS[bank][:, :], AF.Square, [KP(bank)], K_SQ)
        ts(t_qg, PS[bank][:, :], vcol(gcol), ALU.mult, [KP(bank)] + CK, K_QG)
        if rope:
            act(t_qgb, PS[bank][:, :], AF.Copy, [KP(bank)] + CK, K_QGB, scale=vcol(gcol))
        bfree(bank)

    def qk_stage1(rope, tcol0, out_bf, out_keys, out_partial, nk_dst=None):
        if DBGQ in (1, 2, 3, 4, 5):
            return
        b1 = balloc()
        mm(PS[b1][:, :], ones_b[:], t_sq, True, True, K_SQ + [("c", "onesb")], [KP(b1)])
        act(t_rs, PS[b1][:, :], AF.Ln, [KP(b1)], K_RS, scale=1.0 / 128.0, bias=EPS)
        bfree(b1)
        act(t_rs, t_rs, AF.Exp, K_RS, K_RS, scale=-0.5)
        if rope:
            b2 = balloc()
            mm(PS[b2][:, :], rot_b[:], t_qgb, True, True, K_QGB + [("c", "rot")], [KP(b2)])
            tt(t_t2, PS[b2][:, :], ropes[:, tcol0:tcol0 + 512], ALU.mult, [KP(b2), ("c", "ropes")], K_T2)
            bfree(b2)
            tt(t_qg, t_qg, ropec[:, tcol0:tcol0 + 512], ALU.mult, K_QG + [("c", "ropec")], K_QG)
            tt(t_qg, t_qg, t_t2, ALU.add, K_QG + K_T2, K_QG)
            tt(out_bf, t_qg, t_rs, ALU.mult, K_QG + K_RS, out_keys, partial=out_partial)
        else:
            tt(t_qg, t_qg, t_rs, ALU.mult, K_QG + K_RS, K_QG)
            cp("act", out_bf, t_qg, K_QG, out_keys, partial=out_partial)
            if nk_dst is not None:
                kv_out(t_qg, K_QG, nk_dst, None, None)

    stc = [0]

    def kv_out(src_f32, src_keys, dst, bf_out, bf_keys):
        b = balloc()
        for q in range(4):
            tr(PS[b][:, q * 128:(q + 1) * 128], src_f32[:, q * 128:(q + 1) * 128], src_keys, [KP(b)])
        if bf_out is not None:
            cp("act", bf_out, PS[b][:, :].rearrange("p (c e) -> p c e", c=4), [KP(b)], bf_keys, partial=True)
        if dst is not None:
            s = stc[0] % 2
            stc[0] += 1
            cp("dve", t_st[s], PS[b][:, :], [KP(b)], K_ST[s])
            dma("sp", dst, t_st[s].rearrange("p (c e) -> p c e", c=4), K_ST[s], [], ("st", s))
        bfree(b)

    def attention_unit(q0, nq, kchs, c, qT, KQ):
        if True:
            if True:
                bo0, bo1, bz = balloc(), balloc(), balloc()
                nk_ = len(kchs)
                sb_ = [None] * nk_

                def smm(i):
                    sb_[i] = balloc()
                    mm(PS[sb_[i]][:, 0:nq], kchs[i][c], qT[:, c, q0:q0 + nq], True, True, kchs[i][3] + KQ,
                       [KP(sb_[i])])
                smm(0)
                for i in range(nk_):
                    if i + 1 < nk_:
                        smm(i + 1)
                    e_ = t_E[i % 2][:, 0:nq]
                    act(e_, PS[sb_[i]][:, 0:nq], AF.Exp, [KP(sb_[i])], K_E[i % 2], scale=1.0 / math.sqrt(128.0))
                    bfree(sb_[i])
                    v_ = kchs[i][2]
                    mm(PS[bo0][:, 0:nq], v_[:, 0:128], e_, i == 0, i == nk_ - 1, kchs[i][3] + K_E[i % 2], [KP(bo0)])
                    mm(PS[bo1][:, 0:nq], v_[:, 128:256], e_, i == 0, i == nk_ - 1, kchs[i][3] + K_E[i % 2],
                       [KP(bo1)])
                    mm(PS[bz][:, 0:nq], ones_b[:], e_, i == 0, i == nk_ - 1, K_E[i % 2] + [("c", "onesb")], [KP(bz)])
                E.op("dve", lambda e, bz=bz, nq=nq: e.reciprocal(out=t_rz[:, 0:nq], in_=PS[bz][:, 0:nq]), [KP(bz)],
                     K_RZ)
                bfree(bz)
                for jj, bo in enumerate((bo0, bo1)):
                    if c == 0:
                        tt(t_R0[:, jj, q0:q0 + nq], PS[bo][:, 0:nq], t_rz[:, 0:nq], ALU.mult, [KP(bo)] + K_RZ, K_R0,
                           partial=True)
                    else:
                        tt(t_t1[:, 0:nq], PS[bo][:, 0:nq], t_rz[:, 0:nq], ALU.mult, [KP(bo)] + K_RZ, K_T1)
                        stt(t_R0[:, jj, q0:q0 + nq], t_t1[:, 0:nq], sm[:, 5:6], t_R0[:, jj, q0:q0 + nq], ALU.mult,
                            ALU.add, K_T1 + K_R0 + [("c", "neglam")], K_R0, partial=True)
                    bfree(bo)

    def attention_b(h):
        for jj in range(2):
            act(t_dsq[:, jj, :], t_R0[:, jj, :], AF.Square, K_R0, K_DSQ, partial=True)
        b = balloc()
        mm(PS[b][:, :], ones_b[:], t_dsq[:, 0, :], True, False, K_DSQ + [("c", "onesb")], [KP(b)])
        mm(PS[b][:, :], ones_b[:], t_dsq[:, 1, :], False, True, K_DSQ + [("c", "onesb")], [KP(b)])
        act(t_rz, PS[b][:, :], AF.Ln, [KP(b)], K_RZ, scale=1.0 / 256.0, bias=EPS)
        bfree(b)
        act(t_rz, t_rz, AF.Exp, K_RZ, K_RZ, scale=-0.5)
        for jj in range(2):
            stt(mixT[:, 2 * h + jj, :], t_R0[:, jj, :], sm[:, 6 + jj:7 + jj], t_rz, ALU.mult, ALU.mult,
                K_R0 + K_RZ + [("c", "sg")], [KM(2 * h + jj)])

    def run_blocks(blocks, after_block=None):
        due = {}
        nb = len(blocks)
        for i, blk in enumerate(blocks):
            wb = wload(blk["wid"])
            bank = balloc()
            main_mm(wb, bank, blk["rhs"], blk["rkeys"], blk["n"])
            if blk.get("extra") is not None:
                blk["extra"](wb)
            for k, fn in enumerate(blk["stages"]):
                due.setdefault(i + k, []).append((i, k, fn, bank))
            for (_, _, fn, bk) in sorted(due.pop(i, []), key=lambda z: (getattr(z[2], "late", False), z[0], z[1])):
                fn(bk)
            if after_block is not None:
                after_block(i)
        for t in sorted(due.keys()):
            for (_, _, fn, bk) in sorted(due[t], key=lambda z: (z[0], z[1])):
                fn(bk)

    def hrhs(n0=0, n=512):
        return [hT[:, kc, n0:n0 + n] for kc in range(32)], [KH(kc) for kc in range(32)]

    def xt_reload_group(xrows, t4, g, s=0):
        if g == 0:
            dma("sp", xst[s], xrows[t4 * 128:(t4 + 1) * 128, :], [], KXST[s], ("xst", s))
        bank = balloc()
        for q in range(4):
            kc = 4 * g + q
            tr(PS[bank][:, q * 128:(q + 1) * 128], xst[s][:, kc * 128:(kc + 1) * 128],
               [KXST[s][kc // 2]], [KP(bank)])
        cp("dve" if g % 2 == 0 else "act", xT[:, 4 * g:4 * g + 4, t4 * 128:(t4 + 1) * 128],
           PS[bank][:, :].rearrange("p (c t) -> p c t", c=4), [KP(bank)], K1(4 * g, 4 * g + 4), partial=True)
        bfree(bank)

    def p2(sample, krow0, xres=None):
        rhs, rkeys = hrhs()
        blocks = []
        for h in range(NH):
            hp = h % 2
            QT, KT_, VH, KQ, KK, KV = QTs[hp], KTs[hp], VHs[hp], KQs[hp], KKs[hp], KVs[hp]
            for c in range(2):
                blocks.append(dict(wid=WIN0 + 6 * h + c, rhs=rhs, rkeys=rkeys, n=512, stages=[
                    (lambda bank: qk_stage0(bank, V_QG, sample)),
                    (lambda bank, c=c, QT=QT, KQ=KQ: qk_stage1(sample, 0, QT[:, c, :], KQ, True)),
                ]))
            for c in range(2):
                if sample:
                    nkd = None
                else:
                    nkd = nk[krow0:krow0 + 512, (2 * h + c) * 128:(2 * h + c + 1) * 128].rearrange(
                        "(c p) d -> p c d", p=128)
                blocks.append(dict(wid=WIN0 + 6 * h + 2 + c, rhs=rhs, rkeys=rkeys, n=512, stages=[
                    (lambda bank: qk_stage0(bank, V_KG, sample)),
                    (lambda bank, c=c, nkd=nkd, KT_=KT_, KK=KK: qk_stage1(sample, 0, KT_[:, c, :], KK, True,
                                                                          nk_dst=nkd)),
                ]))
            for jj in range(2):
                if sample:
                    nvd = None
                else:
                    nvd = nv[krow0:krow0 + 512, (2 * h + jj) * 128:(2 * h + jj + 1) * 128].rearrange(
                        "(c p) d -> p c d", p=128)

                def v0(bank):
                    cp("dve", t_vT, PS[bank][:, :], [KP(bank)], K_VT)
                    bfree(bank)

                def v1(bank, jj=jj, nvd=nvd, VH=VH, KV=KV):
                    kv_out(t_vT, K_VT, nvd, VH[:, :, jj * 128:(jj + 1) * 128], KV)
                st = [v0, v1]
                if jj == 1:
                    units = []
                    if sample:
                        kch = []
                        for i in range(4):
                            kch.append((KT_[:, 0, i * 128:(i + 1) * 128], KT_[:, 1, i * 128:(i + 1) * 128],
                                        VH[:, i, :], KK + KV))
                        for i in range(4):
                            kch.append((kTo[:, 2 * h, i * 128:(i + 1) * 128],
                                        kTo[:, 2 * h + 1, i * 128:(i + 1) * 128],
                                        vo[:, i, h * 256:(h + 1) * 256], KTO + KVO))
                        for i in range(2):
                            kch.append((kTc[:, 2 * h, i * 128:(i + 1) * 128],
                                        kTc[:, 2 * h + 1, i * 128:(i + 1) * 128],
                                        vc[:, i, h * 256:(h + 1) * 256], KTC + KVC))
                        for c in range(2):
                            units.append((0, 512, kch, c))
                    else:
                        for s_ in range(2):
                            kch = []
                            for i in range(2):
                                o = s_ * 256 + i * 128
                                kch.append((KT_[:, 0, o:o + 128], KT_[:, 1, o:o + 128], VH[:, 2 * s_ + i, :],
                                            KK + KV))
                            for c in range(2):
                                units.append((s_ * 256, 256, kch, c))
                    for (q0, nq, kch, c) in units:
                        def uf(bank, q0=q0, nq=nq, kch=kch, c=c, QT=QT, KQ=KQ):
                            attention_unit(q0, nq, kch, c, QT, KQ)
                        uf.late = True
                        st.append(uf)

                    def bf(bank, h=h):
                        attention_b(h)
                    bf.late = True
                    st.append(bf)
                blocks.append(dict(wid=WIN0 + 6 * h + 4 + jj, rhs=rhs, rkeys=rkeys, n=512, stages=st))
        for j in range(16):
            base = WIN0 + 48 + 3 * j
            hb = [None]

            def halo(which, hb=hb):
                if not sample:
                    return None

                def ex(wb, which=which, hb=hb):
                    if which == 0:
                        hb[0] = balloc()

                    def fn(e, hb=hb):
                        last = None
                        for kc in range(32):
                            last = e.matmul(PS[hb[0]][:, which * 2:which * 2 + 2],
                                            lhsT=WB[wb][:, kc * 128:(kc + 1) * 128],
                                            rhs=hhalo[:, kc * 2:kc * 2 + 2], start=(kc == 0), stop=(kc == 31))
                        return last
                    E.op("pe", fn, [("WB", wb), ("c", "hhalo")], [KP(hb[0])], partial=True)
                return ex

            def c0(bank):
                cp("act", t_t2, PS[bank][:, :], [KP(bank)], K_T2)
                bfree(bank)

            def u0(bank, j=j, hb=hb):
                tt(t_qg, PS[bank][:, :], t_t2, ALU.mult, [KP(bank)] + K_T2, K_QG)
                bfree(bank)
                w0, w1, w2 = (vcol(V_CONV + 3 * j + t) for t in range(3))
                act(t_rs, t_qg, AF.Copy, K_QG + CK, K_RS, scale=w1)
                ns = 1 if sample else 2
                z3 = t_qg.rearrange("p (s t) -> p s t", s=ns)
                a3 = t_rs.rearrange("p (s t) -> p s t", s=ns)
                L = 512 // ns
                stt(a3[:, :, 1:L], z3[:, :, 0:L - 1], w0, a3[:, :, 1:L], ALU.mult, ALU.add, K_QG + K_RS + CK, K_RS)
                stt(a3[:, :, 0:L - 1], z3[:, :, 1:L], w2, a3[:, :, 0:L - 1], ALU.mult, ALU.add, K_QG + K_RS + CK,
                    K_RS)
                if sample:
                    cp("act", sm[:, 62:64], PS[hb[0]][:, 0:2], [KP(hb[0])], [("sm", "hc")])
                    tt(sm[:, 62:64], PS[hb[0]][:, 2:4], sm[:, 62:64], ALU.mult, [KP(hb[0]), ("sm", "hc")],
                       [("sm", "hc")])
                    bfree(hb[0])
                    tt(sm[:, 62:63], sm[:, 62:63], vcol(V_MASK + 1), ALU.mult, [("sm", "hc")] + CK, [("sm", "hc")])
                    tt(sm[:, 63:64], sm[:, 63:64], vcol(V_MASK), ALU.mult, [("sm", "hc")] + CK, [("sm", "hc")])
                    stt(t_rs[:, 511:512], sm[:, 62:63], w2, t_rs[:, 511:512], ALU.mult, ALU.add,
                        [("sm", "hc")] + K_RS + CK, K_RS)
                    stt(t_rs[:, 0:1], sm[:, 63:64], w0, t_rs[:, 0:1], ALU.mult, ALU.add, [("sm", "hc")] + K_RS + CK,
                        K_RS)

            def b0(bank, j=j):
                tt(mixT[:, 16 + j, :], PS[bank][:, :], t_rs, ALU.mult, [KP(bank)] + K_RS, [KM(16 + j)])
                bfree(bank)
            blocks.append(dict(wid=base + 1, rhs=rhs, rkeys=rkeys, n=512, stages=[c0], extra=halo(0)))
            blocks.append(dict(wid=base + 2, rhs=rhs, rkeys=rkeys, n=512, stages=[u0], extra=halo(1)))
            blocks.append(dict(wid=base + 0, rhs=rhs, rkeys=rkeys, n=512, stages=[b0]))
        if P2LIM is not None:
            blocks = blocks[:P2LIM]
        XR0 = 60

        def hook(i):
            ada_some(1)
            if xres is not None and XR0 <= i < XR0 + 32:
                xt_reload_group(xres, (i - XR0) // 8, (i - XR0) % 8)
        run_blocks(blocks, after_block=hook)

    def s0(do_p1=True, own_p1=None):
        own_dst = (mixT, KM)

        def cache_loads(i):
            if i == 6:
                dma("pool", kTc, ckT.rearrange("p (c t) -> p c t", c=16), [], KTC, "kc", max_dma_last_dim=1024)
                dma("pool", vc, cv.rearrange("(c p) e -> p c e", p=128), [], KVC, "vc", max_dma_last_dim=8192)
            if own_p1 is not None:
                if i >= 2 and (i - 2) % 7 == 0 and (i - 2) // 7 < 4:
                    p1_piece(own_p1[0], own_p1[1], (i - 2) // 7, "a", sfix=0, dst=own_dst)
                if i >= 5 and (i - 5) % 7 == 0 and (i - 5) // 7 < 4:
                    p1_piece(own_p1[0], own_p1[1], (i - 5) // 7, "b", sfix=0, dst=own_dst)
        if do_p1:
            p1(xsx, 1)
        hh = hhalo[:].rearrange("p (c t) -> p c t", c=32)
        hsrc = hT
        E.op("dve", lambda e: e.tensor_copy(out=hh[:, :, 0:1], in_=hsrc[:, :, 0:1]), [KH(kc) for kc in range(32)],
             [("c", "hhalo")], partial=True)
        E.op("dve", lambda e: e.tensor_copy(out=hh[:, :, 1:2], in_=hsrc[:, :, 511:512]),
             [KH(kc) for kc in range(32)], [("c", "hhalo")], partial=True)
        rhs, rkeys = hrhs()
        blocks = []
        for h in range(NH):
            for c in range(2):
                blocks.append(dict(wid=WIN0 + 6 * h + 2 + c, rhs=rhs, rkeys=rkeys, n=512, stages=[
                    (lambda bank: qk_stage0(bank, V_KG, True)),
                    (lambda bank, h=h, c=c: qk_stage1(True, 512, kTo[:, 2 * h + c, :], KTO, True)),
                ]))
            for jj in range(2):
                def v0(bank):
                    cp("dve", t_vT, PS[bank][:, :], [KP(bank)], K_VT)
                    bfree(bank)

                def v1(bank, h=h, jj=jj):
                    kv_out(t_vT, K_VT, None, vo[:, :, h * 256 + jj * 128:h * 256 + (jj + 1) * 128], KVO)
                blocks.append(dict(wid=WIN0 + 6 * h + 4 + jj, rhs=rhs, rkeys=rkeys, n=512, stages=[v0, v1]))
        run_blocks(blocks, after_block=cache_loads)

    def p3(xrows, j, preloaded=False):
        for t4 in range(0 if preloaded else 4):
            s = t4 % 2
            dma("sp", xst[s], xrows[t4 * 128:(t4 + 1) * 128, :], [], KXST[s], ("xst", s))
            for g in range(8):
                bank = balloc()
                for q in range(4):
                    kc = 4 * g + q
                    tr(PS[bank][:, q * 128:(q + 1) * 128], xst[s][:, kc * 128:(kc + 1) * 128],
                       [KXST[s][kc // 2]], [KP(bank)])
                cp("dve" if g % 2 == 0 else "act", xT[:, 4 * g:4 * g + 4, t4 * 128:(t4 + 1) * 128],
                   PS[bank][:, :].rearrange("p (c t) -> p c t", c=4), [KP(bank)], K1(4 * g, 4 * g + 4), partial=True)
                bfree(bank)
        rhs = [mixT[:, kc, :] for kc in range(32)]
        rkeys = [KM(kc) for kc in range(32)]
        blocks = []
        for m in range(32):
            def ep(bank, m=m):
                stt(xT[:, m, :], PS[bank][:, :], dvv(2, m, j), xT[:, m, :], ALU.mult, ALU.add,
                    [KP(bank)] + K1(m) + CDVI(2), K1(m))
                bfree(bank)
            blocks.append(dict(wid=WOUT0 + m, rhs=rhs, rkeys=rkeys, n=512, stages=[ep]))
        ada_some(1000 if ada_next[0] < 96 else 0)
        run_blocks(blocks, after_block=lambda i: ada_some(1))
        ada_some(1000)

    def p4(j):
        b = balloc()
        for kc in range(32):
            s = kc % 2
            act(t_E[s], xT[:, kc, :], AF.Square, K1(kc), K_E[s])
            mm(PS[b][:, :], ones_b[:], t_E[s], kc == 0, kc == 31, K_E[s] + [("c", "onesb")], [KP(b)])
        act(t_rs, PS[b][:, :], AF.Ln, [KP(b)], K_RS, scale=1.0 / DM, bias=EPS)
        bfree(b)
        act(t_rs, t_rs, AF.Exp, K_RS, K_RS, scale=-0.5)
        tmp = [t_qg, t_t2]
        ktmp = [K_QG, K_T2]
        for kc in range(32):
            s = kc % 2
            stt(tmp[s], xT[:, kc, :], dvv(3, kc, j), t_rs, ALU.mult, ALU.mult, K1(kc) + K_RS + CDVI(3), ktmp[s])
            act(hT[:, kc, :], tmp[s], AF.Identity, ktmp[s] + CDVI(4), [KH(kc)], bias=dvv(4, kc, j))

    def p56(j, next_p1=None):
        rhs, rkeys = hrhs()
        arhs = [mixT[:, kc, :] for kc in range(32)]
        akeys = [KM(kc) for kc in range(32)]
        tmp = [t_qg, t_t2]
        ktmp = [K_QG, K_T2]
        cnt = [0]
        for g in range(4):
            blocks = []
            for m in range(32):
                def ep(bank, m=m):
                    s = cnt[0] % 2
                    cnt[0] += 1
                    act(tmp[s], PS[bank][:, :], AF.Relu, [KP(bank)], ktmp[s])
                    tt(mixT[:, m, :], PS[bank][:, :], tmp[s], ALU.mult, [KP(bank)] + ktmp[s], [KM(m)])
                    bfree(bank)
                blocks.append(dict(wid=MIN0 + g * 32 + m, rhs=rhs, rkeys=rkeys, n=512, stages=[ep]))
            for m in range(32):
                def ep2(bank, m=m):
                    stt(xT[:, m, :], PS[bank][:, :], dvv(5, m, j), xT[:, m, :], ALU.mult, ALU.add,
                        [KP(bank)] + K1(m) + CDVI(5), K1(m))
                    bfree(bank)
                blocks.append(dict(wid=MOUT0 + g * 32 + m, rhs=arhs, rkeys=akeys, n=512, stages=[ep2]))
            hook = None
            if g == 3 and next_p1 is not None:
                def hook(i, next_p1=next_p1):
                    if i >= 34 and (i - 34) % 8 == 0 and (i - 34) // 8 < 4:
                        p1_piece(next_p1[0], next_p1[1], (i - 34) // 8, "a")
                    if i >= 38 and (i - 38) % 8 == 0 and (i - 38) // 8 < 4:
                        p1_piece(next_p1[0], next_p1[1], (i - 38) // 8, "b")
            run_blocks(blocks, after_block=hook)

    def p7(yrows):
        for t4 in range(4):
            s = t4 % 2
            for g in range(8):
                bank = balloc()
                for q in range(4):
                    kc = 4 * g + q
                    tr(PS[bank][:, q * 128:(q + 1) * 128], xT[:, kc, t4 * 128:(t4 + 1) * 128], K1(kc), [KP(bank)])
                cp("dve" if g % 2 == 0 else "act", xst[s][:, g * 512:(g + 1) * 512], PS[bank][:, :], [KP(bank)],
                   KXST[s][2 * g:2 * g + 2], partial=True)
                bfree(bank)
            dma("sp", yrows[t4 * 128:(t4 + 1) * 128, :], xst[s], KXST[s], [], ("xst", s))

    def prompt_tile(r0, do_p1, next_p1):
        if do_p1:
            p1(xp[r0:r0 + 512, :], 0)
        p2(False, r0, xp[r0:r0 + 512, :])
        p3(xp[r0:r0 + 512, :], 0, preloaded=True)
        p4(0)
        p56(0, next_p1)
        p7(yp[r0:r0 + 512, :])

    def sample_tile():
        s0(False, own_p1=(xso, 1))
        swap_hm()
        p2(True, 0, xso)
        p3(xso, 1, preloaded=True)
        p4(1)
        p56(1)
        p7(ys)

    if "all" in STAGES:
        phase0()
        prompt_tile(0, True, (xp[512:1024, :], 0))
        prompt_tile(512, False, (xsx, 1))
        sample_tile()
    else:
        phase0()
        if "p1" in STAGES:
            p1(xp[0:512, :], 0)
        if "p2" in STAGES:
            p2(False, 0)
        if "p3" in STAGES:
            p3(xp[0:512, :], 0)
        if "p4" in STAGES:
            p4(0)
        if "p56" in STAGES:
            p56(0)
        if "p7" in STAGES:
            p7(yp[0:512, :])
        if "sample" in STAGES:
            p1(xsx, 1)
            sample_tile()

    sem_eng = {k: es.enter_context(nc.semaphore(f"s_{k}")) for k in ("pe", "act", "dve")}
    dma_sems = {}
    dma_cnt = {}
    cnt = {"pe": 0, "act": 0, "dve": 0}
    for o in E.ops:
        if o.dma is not None:
            if o.dma not in dma_sems:
                dma_sems[o.dma] = es.enter_context(nc.semaphore(f"d{len(dma_sems)}"))
                dma_cnt[o.dma] = 0
            dma_cnt[o.dma] += 16
            o.val = (dma_sems[o.dma], dma_cnt[o.dma])
        elif o.sig:
            cnt[o.eng] += 1
            o.val = (sem_eng[o.eng], cnt[o.eng])
    per = {k: [o for o in E.ops if o.eng == k] for k in ("pe", "act", "dve", "sp", "pool")}
    block = es.enter_context(nc.Block())

    def replay(eng, ops, final=False):
        waited = {}
        for o in ops:
            for p in o.deps:
                sem, val = p.val
                if waited.get(id(sem), 0) < val:
                    eng.wait_ge(sem, val)
                    waited[id(sem)] = val
            inst = o.fn(eng)
            if o.dma is not None:
                inst.then_inc(o.val[0], 16)
            elif o.sig:
                inst.then_inc(o.val[0], 1)
        if final:
            for k, sem in dma_sems.items():
                eng.wait_ge(sem, dma_cnt[k])

    @block.tensor
    def _(e):
        replay(e, per["pe"])

    @block.scalar
    def _(e):
        replay(e, per["act"])

    @block.vector
    def _(e):
        replay(e, per["dve"])

    @block.gpsimd
    def _(e):
        replay(e, per["pool"])

    @block.sync
    def _(e):
        replay(e, per["sp"], final=True)

    es.close()
    return nc


def _blocks(W):
    K, F = W.shape
    assert K == 4096
    return np.ascontiguousarray(W.reshape(32, 128, F // 128, 128).transpose(2, 1, 0, 3)).reshape(F // 128, 128, 4096)


def _shared(inp):
    f = np.float32
    w_in = np.asarray(inp["w_in"][0], f)
    order = []
    for h in range(8):
        order += [2 * h, 2 * h + 1, 16 + 2 * h, 16 + 2 * h + 1, 32 + 2 * h, 32 + 2 * h + 1]
    for j in range(16):
        order += [48 + j, 64 + j, 80 + j]
    wblk = np.empty((NBLK, 128, 4096), f)
    wblk[ADA0:ADA0 + 192] = _blocks(np.asarray(inp["w_ada"][0], f))
    wblk[WIN0:WIN0 + 96] = _blocks(w_in)[order]
    wblk[WOUT0:WOUT0 + 32] = _blocks(np.asarray(inp["w_out"][0], f))
    wblk[MIN0:MIN0 + 128] = _blocks(np.asarray(inp["w_mlp_in"][0], f))
    wmo = np.asarray(inp["w_mlp_out"][0], f)
    for g in range(4):
        wblk[MOUT0 + 32 * g:MOUT0 + 32 * (g + 1)] = _blocks(wmo[g * 4096:(g + 1) * 4096])
    vecs = np.zeros((128, NV), f)
    vecs[:, V_BADA:V_BADA + 192] = np.asarray(inp["b_ada"][0], f).reshape(192, 128).T
    vecs[:, V_GATT:V_GATT + 32] = np.asarray(inp["norm_attn_g"][0], f).reshape(32, 128).T
    vecs[:, V_GMLP:V_GMLP + 32] = np.asarray(inp["norm_mlp_g"][0], f).reshape(32, 128).T
    vecs[:, V_QG] = np.asarray(inp["q_norm_g"][0], f)
    vecs[:, V_KG] = np.asarray(inp["k_norm_g"][0], f)
    cw = np.asarray(inp["conv_w"][0], f)
    for j in range(16):
        for t in range(3):
            vecs[:, V_CONV + 3 * j + t] = cw[t, j * 128:(j + 1) * 128]
    vecs[:, V_SUB:V_SUB + 2] = np.asarray(inp["subln_g"][0], f).reshape(2, 128).T
    for i, nm in enumerate(("lambda_q1", "lambda_k1", "lambda_q2", "lambda_k2")):
        vecs[:, V_LAM + i] = np.asarray(inp[nm][0], f)
    consts = np.zeros((128, 384), f)
    consts[:, 0:128] = np.eye(128, dtype=f)
    consts[:, 128:256] = 1.0
    for m in range(128):
        if (m % 64) < 32:
            consts[m + 32, 256 + m] = -1.0
        else:
            consts[m - 32, 256 + m] = 1.0
    return wblk, vecs, consts


def _rope_tables(tok):
    f = np.float32
    inv = np.power(f(10000.0), -np.arange(0, 64, 2, dtype=f) / f(64)).astype(f)
    row = (tok // 64).astype(f)
    col = (tok % 64).astype(f)
    ang = np.empty((128, tok.shape[0]), f)
    for p in range(128):
        pos = row if p < 64 else col
        ang[p] = pos * inv[p % 32]
    return np.cos(ang).astype(f), np.sin(ang).astype(f)


def _core_inputs(inp, c, shared):
    f = np.float32
    wblk, vecs, consts = shared
    sbi, par = c // 2, c % 2
    xs = np.asarray(inp["x_sample"][sbi], f)
    own = slice(512 * par, 512 * par + 512)
    oth = slice(512 * (1 - par), 512 * (1 - par) + 512)
    tok = np.concatenate([np.arange(own.start, own.stop), np.arange(oth.start, oth.stop)])
    rc, rs = _rope_tables(tok)
    v = vecs.copy()
    v[:, V_MASK] = 1.0 if par == 1 else 0.0
    v[:, V_MASK + 1] = 1.0 if par == 0 else 0.0
    ck = np.asarray(inp["cache_k"][sbi, 0], f)
    cond = np.stack([np.asarray(inp["c_ctx"], f), np.asarray(inp["c"][sbi], f)], axis=0)
    condT = np.ascontiguousarray(cond.reshape(2, 32, 128).transpose(2, 1, 0)).reshape(128, 64)
    return {
        "wblk": wblk,
        "xp": np.ascontiguousarray(np.asarray(inp["x_prompt"][4 * c:4 * c + 4], f).reshape(1024, DM)),
        "xso": np.ascontiguousarray(xs[own]),
        "xsx": np.ascontiguousarray(xs[oth]),
        "ckT": np.ascontiguousarray(ck.transpose(3, 1, 2, 0)).reshape(128, 4096),
        "cv": np.ascontiguousarray(np.asarray(inp["cache_v"][sbi, 0], f).reshape(256, 2048)),
        "condT": condT,
        "vecs": v,
        "ropec": rc,
        "ropes": rs,
        "consts": consts,
    }


_NC = [None]


def kernel(**inputs):
    if _NC[0] is None:
        _NC[0] = build_program()
    nc = _NC[0]
    shared = _shared(inputs)
    in_maps = [_core_inputs(inputs, c, shared) for c in range(NCORES)]
    res = run_bass_kernel_spmd(nc, in_maps, core_ids=list(range(NCORES)))
    y_p = np.empty((32, 256, DM), np.float32)
    y_s = np.empty((4, 1024, DM), np.float32)
    new_k = np.empty((32, 1, 256, 8, 2, 128), np.float32)
    new_v = np.empty((32, 1, 256, 8, 256), np.float32)
    for c in range(NCORES):
        r = res.results[c]
        y_p[4 * c:4 * c + 4] = r["yp"].reshape(4, 256, DM)
        par = c % 2
        y_s[c // 2, 512 * par:512 * par + 512] = r["ys"]
        new_k[4 * c:4 * c + 4, 0] = r["nk"].reshape(4, 256, 8, 2, 128)
        new_v[4 * c:4 * c + 4, 0] = r["nv"].reshape(4, 256, 8, 256)
    return (y_p, y_s, new_k, new_v)
```

```python
import math
from contextlib import ExitStack

import numpy as np
import concourse.bass as bass
import concourse.mybir as mybir
from concourse.bass_utils import run_bass_kernel_spmd

F32 = mybir.dt.float32
BF16 = mybir.dt.bfloat16
AF = mybir.ActivationFunctionType
ALU = mybir.AluOpType

NCORES = 8
DM = 4096
NH = 8
EPS = 1e-6
LAM_INIT = 0.8 - 0.6 * math.exp(-0.3 * 0)
NW = 4
STAGES = {"all"}
NADA = 192
P2LIM = None
DBGQ = 0
WMOD = None

ADA0 = 0
WIN0 = 192
WOUT0 = WIN0 + 96
MIN0 = WOUT0 + 32
MOUT0 = MIN0 + 128
NBLK = MOUT0 + 128

V_BADA = 0
V_GATT = 192
V_GMLP = 224
V_QG = 256
V_KG = 257
V_CONV = 258
V_SUB = 306
V_LAM = 308
V_MASK = 312
NV = 320


class Op:
    __slots__ = ("eng", "fn", "deps", "sig", "dma", "val", "idx")


class Em:
    def __init__(self):
        self.ops = []
        self.kw = {}
        self.kr = {}

    def op(self, eng, fn, r=(), w=(), dma=None, partial=False):
        o = Op()
        o.eng, o.fn, o.dma, o.sig, o.val, o.idx = eng, fn, dma, dma is not None, None, len(self.ops)
        deps = {}
        psr = [k for k in r if k[0] == "ps" and k not in w]
        r = [k for k in r if k[0] != "ps"]
        for k in psr:
            for p in self.kw.get(k, ()):
                deps[p.idx] = p
            for p in self.kr.get(k, ()):
                deps[p.idx] = p
        for k in r:
            for p in self.kw.get(k, ()):
                deps[p.idx] = p
        for k in w:
            for p in self.kw.get(k, ()):
                deps[p.idx] = p
            for p in self.kr.get(k, ()):
                deps[p.idx] = p
        o.deps = [p for p in deps.values() if not (p.eng == "pe" and eng == "pe")]
        for p in o.deps:
            p.sig = True
        for k in r:
            lst = self.kr.setdefault(k, [])
            lst[:] = [q for q in lst if q.dma is not None or q.eng != eng]
            lst.append(o)
        for k in psr:
            self.kr[k] = []
            lst = self.kw.setdefault(k, [])
            lst[:] = [q for q in lst if q.dma is not None or q.eng != eng]
            lst.append(o)
        for k in w:
            self.kr[k] = []
            if partial:
                lst = self.kw.setdefault(k, [])
                lst[:] = [q for q in lst if q.dma is not None or q.eng != eng]
                lst.append(o)
            else:
                self.kw[k] = [o]
        self.ops.append(o)
        return o


def build_program():
    nc = bass.Bass("TRN2", target_bir_lowering=False)

    def din(name, shape):
        return nc.dram_tensor(name, shape, F32, kind="ExternalInput").ap()

    def dout(name, shape):
        return nc.dram_tensor(name, shape, F32, kind="ExternalOutput").ap()

    wblk = din("wblk", [WMOD or NBLK, 128, 4096])
    xp = din("xp", [1024, DM])
    xso = din("xso", [512, DM])
    xsx = din("xsx", [512, DM])
    ckT = din("ckT", [128, 4096])
    cv = din("cv", [256, 2048])
    condT = din("condT", [128, 64])
    vecs_d = din("vecs", [128, NV])
    ropec_d = din("ropec", [128, 1024])
    ropes_d = din("ropes", [128, 1024])
    consts_d = din("consts", [128, 384])
    yp = dout("yp", [1024, DM])
    ys = dout("ys", [512, DM])
    nk = dout("nk", [1024, 2048])
    nv = dout("nv", [1024, 2048])

    E = Em()
    es = ExitStack()

    def sb(name, shape, dt):
        return es.enter_context(nc.sbuf_tensor("s_" + name, shape, dt))

    R1 = sb("R1", [128, 16384], F32)
    RH = sb("RH", [128, 16384], BF16)
    RM = sb("RM", [128, 16384], BF16)
    RT = sb("RT", [128, 8192], F32)
    WB = [sb(f"WB{i}", [128, 4096], BF16) for i in range(NW)]
    ident = sb("ident", [128, 128], F32)
    ones_f = sb("ones_f", [128, 128], F32)
    ones_b = sb("ones_b", [128, 128], BF16)
    rot_b = sb("rot_b", [128, 128], BF16)
    vecs = sb("vecs", [128, NV], F32)
    ropec = sb("ropec", [128, 1024], F32)
    ropes = sb("ropes", [128, 1024], F32)
    cond_s = sb("cond_s", [128, 64], F32)
    scT = sb("scT", [128, 64], BF16)
    modT = sb("modT", [128, 384], F32)
    dv = sb("dv", [128, 6 * 64], F32)
    sm = sb("sm", [128, 64], F32)
    hhalo = sb("hhalo", [128, 64], BF16)
    PS = [es.enter_context(nc.psum_tensor(f"ps{i}", [128, 512], F32)) for i in range(8)]

    xT = R1[:].rearrange("p (c t) -> p c t", c=32)
    R1b = R1[:].bitcast(BF16)
    kTo = R1b[:, 0:8192].rearrange("p (c t) -> p c t", c=16)
    vo = R1b[:, 8192:16384].rearrange("p (c e) -> p c e", c=4)
    kTc = R1b[:, 16384:20480].rearrange("p (c t) -> p c t", c=16)
    vc = R1b[:, 20480:24576].rearrange("p (c e) -> p c e", c=2)
    hT = RH[:].rearrange("p (c t) -> p c t", c=32)
    mixT = RM[:].rearrange("p (c t) -> p c t", c=32)
    RTb = RT[:].bitcast(BF16)

    def K1(lo, hi=None):
        return [("R1", i) for i in range(lo, (lo + 1) if hi is None else hi)]

    KTO = K1(0, 8)
    KVO = K1(8, 16)
    KTC = K1(16, 20)
    KVC = K1(20, 24)

    HM = ["RH", "RM"]

    def KH(kc):
        return (HM[0], kc)

    def KM(kc):
        return (HM[1], kc)

    def swap_hm():
        nonlocal hT, mixT
        hT, mixT = mixT, hT
        HM.reverse()

    def rt_f32(slot, n):
        return RT[:, slot * 256: slot * 256 + n]

    def rt_b16(slot, n):
        return RTb[:, slot * 512: slot * 512 + n]

    def KT(lo, n):
        return [("RT", i) for i in range(lo, lo + n)]

    xst = [RT[:, 0:4096], RT[:, 4096:8192]]
    KXST = [KT(0, 16), KT(16, 16)]
    t_qT = rt_b16(0, 1024).rearrange("p (c t) -> p c t", c=2); K_QT = KT(0, 2)
    t_kT = rt_b16(2, 1024).rearrange("p (c t) -> p c t", c=2); K_KT = KT(2, 2)
    t_vh = rt_b16(4, 1024).rearrange("p (c e) -> p c e", c=4); K_VH = KT(4, 2)
    QTs = [t_qT, R1b[:, 24576:25600].rearrange("p (c t) -> p c t", c=2)]
    KTs = [t_kT, R1b[:, 25600:26624].rearrange("p (c t) -> p c t", c=2)]
    VHs = [t_vh, R1b[:, 26624:27648].rearrange("p (c e) -> p c e", c=4)]
    KQs = [K_QT, K1(24)]
    KKs = [K_KT, K1(25)]
    KVs = [K_VH, K1(26)]
    t_E = [rt_b16(6, 512), rt_b16(7, 512)]; K_E = [KT(6, 1), KT(7, 1)]
    t_R0 = rt_f32(8, 1024).rearrange("p (c t) -> p c t", c=2); K_R0 = KT(8, 4)
    t_rz = rt_f32(12, 512); K_RZ = KT(12, 2)
    t_t1 = rt_f32(14, 512); K_T1 = KT(14, 2)
    t_sq = rt_b16(16, 512); K_SQ = KT(16, 1)
    t_qgb = rt_b16(17, 512); K_QGB = KT(17, 1)
    t_qg = rt_f32(18, 512); K_QG = KT(18, 2)
    t_rs = rt_f32(20, 512); K_RS = KT(20, 2)
    t_t2 = rt_f32(22, 512); K_T2 = KT(22, 2)
    t_vT = rt_f32(24, 512); K_VT = KT(24, 2)
    t_st = [rt_f32(26, 512), rt_f32(28, 512)]; K_ST = [KT(26, 2), KT(28, 2)]
    t_dsq = rt_b16(30, 1024).rearrange("p (c t) -> p c t", c=2); K_DSQ = KT(30, 2)

    def vcol(c, n=1):
        return vecs[:, c:c + n]

    def dvv(idx, kc, j):
        c = idx * 64 + kc * 2 + j
        return dv[:, c:c + 1]

    live = [False] * 8
    freed_at = list(range(8))
    fseq = [8]

    def balloc():
        cand = [b for b in range(8) if not live[b]]
        if not cand:
            raise RuntimeError("no free PSUM bank")
        b = min(cand, key=lambda x: freed_at[x])
        live[b] = True
        return b

    def bfree(b):
        live[b] = False
        freed_at[b] = fseq[0]
        fseq[0] += 1

    def KP(b):
        return ("ps", b)

    def dma(q, out, in_, r, w, key, **kw):
        E.op(q, lambda e: e.dma_start(out=out, in_=in_, **kw), r, w, dma=key)

    def act(out, in_, func, r, w, scale=None, bias=None, partial=False):
        kw = {}
        if scale is not None:
            kw["scale"] = scale
        if bias is not None:
            kw["bias"] = bias
        E.op("act", lambda e: e.activation(out=out, in_=in_, func=func, **kw), r, w, partial=partial)

    def tt(out, in0, in1, op, r, w, partial=False):
        E.op("dve", lambda e: e.tensor_tensor(out=out, in0=in0, in1=in1, op=op), r, w, partial=partial)

    def ts(out, in0, s1, op0, r, w, s2=None, op1=None, partial=False):
        if op1 is None:
            E.op("dve", lambda e: e.tensor_scalar(out=out, in0=in0, scalar1=s1, scalar2=None, op0=op0), r, w,
                 partial=partial)
        else:
            E.op("dve", lambda e: e.tensor_scalar(out=out, in0=in0, scalar1=s1, scalar2=s2, op0=op0, op1=op1), r, w,
                 partial=partial)

    def stt(out, in0, scalar, in1, op0, op1, r, w, partial=False):
        E.op("dve", lambda e: e.scalar_tensor_tensor(out=out, in0=in0, scalar=scalar, in1=in1, op0=op0, op1=op1),
             r, w, partial=partial)

    def cp(eng, out, in_, r, w, partial=False):
        if eng == "dve":
            E.op("dve", lambda e: e.tensor_copy(out=out, in_=in_), r, w, partial=partial)
        else:
            E.op("act", lambda e: e.activation(out=out, in_=in_, func=AF.Copy), r, w, partial=partial)

    def mm(out, lhsT, rhs, start, stop, r, w):
        E.op("pe", lambda e: e.matmul(out, lhsT=lhsT, rhs=rhs, start=start, stop=stop), r, w, partial=True)

    def tr(out, in_, r, w):
        E.op("pe", lambda e: e.transpose(out, in_, ident[:]), r + [("c", "ident")], w, partial=True)

    CK = [("c", "k")]

    wcnt = [0]

    def wload(wid):
        b = wcnt[0] % NW
        wcnt[0] += 1
        dma("pool", WB[b][:], wblk[wid % WMOD if WMOD else wid], [], [("WB", b)], ("WB", b), max_dma_last_dim=8192)
        return b

    def main_mm(b, bank, rhs_list, rkeys, n):
        def fn(e):
            last = None
            for kc in range(32):
                last = e.matmul(PS[bank][:, 0:n], lhsT=WB[b][:, kc * 128:(kc + 1) * 128], rhs=rhs_list[kc],
                                start=(kc == 0), stop=(kc == 31))
            return last
        E.op("pe", fn, [("WB", b)] + rkeys, [KP(bank)], partial=True)

    def phase0():
        dma("sp", ident[:], consts_d[:, 0:128], [], [("c", "ident")], "c0a")
        dma("sp", ones_f[:], consts_d[:, 128:256], [], [("c", "onesf")], "c0b")
        dma("sp", vecs[:], vecs_d, [], CK, "c0c")
        dma("sp", ropec[:], ropec_d, [], [("c", "ropec")], "c1a")
        dma("sp", ropes[:], ropes_d, [], [("c", "ropes")], "c1b")
        dma("sp", cond_s[:], condT, [], [("c", "cond")], "c2")
        dma("pool", ones_b[:], consts_d[:, 128:256], [], [("c", "onesb")], "c3")
        dma("pool", rot_b[:], consts_d[:, 256:384], [], [("c", "rot")], "c4")
        act(scT[:], cond_s[:], AF.Silu, [("c", "cond")], [("c", "scT")])
        tt(sm[:, 0:1], vcol(V_LAM), vcol(V_LAM + 1), ALU.mult, CK, [("sm", 0)])
        tt(sm[:, 1:2], vcol(V_LAM + 2), vcol(V_LAM + 3), ALU.mult, CK, [("sm", 1)])
        b = balloc()
        E.op("pe", lambda e: e.matmul(PS[b][:, 0:2], lhsT=ones_f[:], rhs=sm[:, 0:2], start=True, stop=True),
             [("sm", 0), ("sm", 1), ("c", "onesf")], [KP(b)], partial=True)
        act(sm[:, 2:4], PS[b][:, 0:2], AF.Exp, [KP(b)], [("sm", 2)])
        bfree(b)
        tt(sm[:, 4:5], sm[:, 2:3], sm[:, 3:4], ALU.subtract, [("sm", 2)], [("sm", 4)])
        ts(sm[:, 5:6], sm[:, 4:5], -1.0, ALU.mult, [("sm", 4)], [("c", "neglam")], s2=-LAM_INIT, op1=ALU.add)
        ts(sm[:, 6:8], vcol(V_SUB, 2), 1.0 - LAM_INIT, ALU.mult, CK, [("c", "sg")])
        for oc in range(min(64, NADA)):
            ada_block(oc)
        derive(0)
        derive(1)

    def ada_block(oc):
        wb = wload(ADA0 + oc)
        bank = balloc()

        def fn(e, wb=wb, bank=bank):
            last = None
            for kc in range(32):
                last = e.matmul(PS[bank][:, 0:2], lhsT=WB[wb][:, kc * 128:(kc + 1) * 128],
                                rhs=scT[:, kc * 2:kc * 2 + 2], start=(kc == 0), stop=(kc == 31))
            return last
        E.op("pe", fn, [("WB", wb), ("c", "scT")], [KP(bank)], partial=True)
        ts(modT[:, oc * 2:oc * 2 + 2], PS[bank][:, 0:2], vcol(V_BADA + oc), ALU.add, [KP(bank)] + CK,
           [("c", "mod")], partial=True)
        bfree(bank)

    ada_next = [64]

    def ada_some(n=1):
        for _ in range(n):
            if ada_next[0] < NADA:
                oc = ada_next[0]
                ada_next[0] += 1
                ada_block(oc)
                if oc == 95:
                    derive(2)
                elif oc == 159:
                    derive(3)
                    derive(4)
                elif oc == 191:
                    derive(5)

    def derive(dst):
        m4 = modT[:].rearrange("p (i c j) -> p i c j", i=6, c=32)
        d4 = dv[:].rearrange("p (i c j) -> p i c j", i=6, c=32)
        for j in range(2):
            if dst in (0, 3):
                src, g0 = (1, V_GATT) if dst == 0 else (4, V_GMLP)
                stt(d4[:, dst, :, j], m4[:, src, :, j], 1.0, vcol(g0, 32), ALU.add, ALU.mult,
                    [("c", "mod")] + CK, [("c", "dv", dst)], partial=True)
            else:
                src = {1: 0, 2: 2, 4: 3, 5: 5}[dst]
                E.op("dve", lambda e, s_=src, d=dst, j=j: e.tensor_copy(out=d4[:, d, :, j], in_=m4[:, s_, :, j]),
                     [("c", "mod")], [("c", "dv", dst)], partial=True)

    def CDVI(*idx):
        return [("c", "dv", i) for i in idx]

    def p1(xrows, j):
        for t4 in range(4):
            p1_piece(xrows, j, t4)

    def p1_piece(xrows, j, t4, part="ab", sfix=None, dst=None):
        s = t4 % 2 if sfix is None else sfix
        hD, KD = (hT, KH) if dst is None else dst
        if "a" in part:
            dma("sp", xst[s], xrows[t4 * 128:(t4 + 1) * 128, :], [], KXST[s], ("xst", s))
            for q in range(8):
                E.op("dve", lambda e, q=q, s=s: e.bn_stats(out=sm[:, 8 + q * 6: 14 + q * 6],
                                                       in_=xst[s][:, q * 512:(q + 1) * 512]),
                     KXST[s], [("sm", "bn")], partial=True)
            E.op("dve", lambda e: e.bn_aggr(out=sm[:, 56:58], in_=sm[:, 8:56]), [("sm", "bn")], [("sm", "agg")])
            stt(sm[:, 58:59], sm[:, 56:57], sm[:, 56:57], sm[:, 57:58], ALU.mult, ALU.add, [("sm", "agg")],
                [("sm", "ms")])
            act(sm[:, 59:60], sm[:, 58:59], AF.Ln, [("sm", "ms")], [("sm", "ln")], bias=EPS)
            act(sm[:, 60:61], sm[:, 59:60], AF.Exp, [("sm", "ln")], [("sm", "rstd")], scale=-0.5)
            ts(xst[s][:, 0:2048], xst[s][:, 0:2048], sm[:, 60:61], ALU.mult, KXST[s] + [("sm", "rstd")],
               KXST[s][0:8], partial=True)
            act(xst[s][:, 2048:4096], xst[s][:, 2048:4096], AF.Copy, KXST[s] + [("sm", "rstd")], KXST[s][8:16],
                scale=sm[:, 60:61], partial=True)
        if "b" in part:
            for g in range(8):
                bank = balloc()
                for q in range(4):
                    kc = 4 * g + q
                    tr(PS[bank][:, q * 128:(q + 1) * 128], xst[s][:, kc * 128:(kc + 1) * 128],
                       [KXST[s][kc // 2]], [KP(bank)])
                for q in range(4):
                    kc = 4 * g + q
                    act(hD[:, kc, t4 * 128:(t4 + 1) * 128], PS[bank][:, q * 128:(q + 1) * 128], AF.Identity,
                        [KP(bank)] + CDVI(0, 1), [KD(kc)], scale=dvv(0, kc, j), bias=dvv(1, kc, j), partial=True)
                bfree(bank)

    def qk_stage0(bank, gcol, rope):
        if DBGQ == 1:
            cp("dve", t_qg, PS[bank][:, :], [KP(bank)], K_QG)
            bfree(bank)
            return
        if DBGQ == 3:
            act(t_sq, PS[bank][:, :], AF.Square, [KP(bank)], K_SQ)
            bfree(bank)
            return
        if DBGQ == 4:
            ts(t_qg, PS[bank][:, :], vcol(gcol), ALU.mult, [KP(bank)] + CK, K_QG)
            bfree(bank)
            return
        if DBGQ == 5:
            act(hT[:, 0, :], PS[bank][:, :], AF.Square, [KP(bank)], [KH(0)])
            bfree(bank)
            return
        act(t_sq, PS[bank][:, :], AF.Square, [KP(bank)], K_SQ)
        ts(t_qg, PS[bank][:, :], vcol(gcol), ALU.mult, [KP(bank)] + CK, K_QG)
        if rope:
            act(t_qgb, PS[bank][:, :], AF.Copy, [KP(bank)] + CK, K_QGB, scale=vcol(gcol))
        bfree(bank)

    def qk_stage1(rope, tcol0, out_bf, out_keys, out_partial, nk_dst=None):
        if DBGQ in (1, 2, 3, 4, 5):
            return
        b1 = balloc()
        mm(PS[b1][:, :], ones_b[:], t_sq, True, True, K_SQ + [("c", "onesb")], [KP(b1)])
        act(t_rs, PS[b1][:, :], AF.Ln, [KP(b1)], K_RS, scale=1.0 / 128.0, bias=EPS)
        bfree(b1)
        act(t_rs, t_rs, AF.Exp, K_RS, K_RS, scale=-0.5)
        if rope:
            b2 = balloc()
            mm(PS[b2][:, :], rot_b[:], t_qgb, True, True, K_QGB + [("c", "rot")], [KP(b2)])
            tt(t_t2, PS[b2][:, :], ropes[:, tcol0:tcol0 + 512], ALU.mult, [KP(b2), ("c", "ropes")], K_T2)
            bfree(b2)
            tt(t_qg, t_qg, ropec[:, tcol0:tcol0 + 512], ALU.mult, K_QG + [("c", "ropec")], K_QG)
            tt(t_qg, t_qg, t_t2, ALU.add, K_QG + K_T2, K_QG)
            tt(out_bf, t_qg, t_rs, ALU.mult, K_QG + K_RS, out_keys, partial=out_partial)
        else:
            tt(t_qg, t_qg, t_rs, ALU.mult, K_QG + K_RS, K_QG)
            cp("act", out_bf, t_qg, K_QG, out_keys, partial=out_partial)
            if nk_dst is not None:
                kv_out(t_qg, K_QG, nk_dst, None, None)

    stc = [0]

    def kv_out(src_f32, src_keys, dst, bf_out, bf_keys):
        b = balloc()
        for q in range(4):
            tr(PS[b][:, q * 128:(q + 1) * 128], src_f32[:, q * 128:(q + 1) * 128], src_keys, [KP(b)])
        if bf_out is not None:
            cp("act", bf_out, PS[b][:, :].rearrange("p (c e) -> p c e", c=4), [KP(b)], bf_keys, partial=True)
        if dst is not None:
            s = stc[0] % 2
            stc[0] += 1
            cp("dve", t_st[s], PS[b][:, :], [KP(b)], K_ST[s])
            dma("sp", dst, t_st[s].rearrange("p (c e) -> p c e", c=4), K_ST[s], [], ("st", s))
        bfree(b)

    def attention_unit(q0, nq, kchs, c, qT, KQ):
        if True:
            if True:
                bo0, bo1, bz = balloc(), balloc(), balloc()
                nk_ = len(kchs)
                sb_ = [None] * nk_

                def smm(i):
                    sb_[i] = balloc()
                    mm(PS[sb_[i]][:, 0:nq], kchs[i][c], qT[:, c, q0:q0 + nq], True, True, kchs[i][3] + KQ,
                       [KP(sb_[i])])
                smm(0)
                for i in range(nk_):
                    if i + 1 < nk_:
                        smm(i + 1)
                    e_ = t_E[i % 2][:, 0:nq]
                    act(e_, PS[sb_[i]][:, 0:nq], AF.Exp, [KP(sb_[i])], K_E[i % 2], scale=1.0 / math.sqrt(128.0))
                    bfree(sb_[i])
                    v_ = kchs[i][2]
                    mm(PS[bo0][:, 0:nq], v_[:, 0:128], e_, i == 0, i == nk_ - 1, kchs[i][3] + K_E[i % 2], [KP(bo0)])
                    mm(PS[bo1][:, 0:nq], v_[:, 128:256], e_, i == 0, i == nk_ - 1, kchs[i][3] + K_E[i % 2],
                       [KP(bo1)])
                    mm(PS[bz][:, 0:nq], ones_b[:], e_, i == 0, i == nk_ - 1, K_E[i % 2] + [("c", "onesb")], [KP(bz)])
                E.op("dve", lambda e, bz=bz, nq=nq: e.reciprocal(out=t_rz[:, 0:nq], in_=PS[bz][:, 0:nq]), [KP(bz)],
                     K_RZ)
                bfree(bz)
                for jj, bo in enumerate((bo0, bo1)):
                    if c == 0:
                        tt(t_R0[:, jj, q0:q0 + nq], PS[bo][:, 0:nq], t_rz[:, 0:nq], ALU.mult, [KP(bo)] + K_RZ, K_R0,
                           partial=True)
                    else:
                        tt(t_t1[:, 0:nq], PS[bo][:, 0:nq], t_rz[:, 0:nq], ALU.mult, [KP(bo)] + K_RZ, K_T1)
                        stt(t_R0[:, jj, q0:q0 + nq], t_t1[:, 0:nq], sm[:, 5:6], t_R0[:, jj, q0:q0 + nq], ALU.mult,
                            ALU.add, K_T1 + K_R0 + [("c", "neglam")], K_R0, partial=True)
                    bfree(bo)

    def attention_b(h):
        for jj in range(2):
            act(t_dsq[:, jj, :], t_R0[:, jj, :], AF.Square, K_R0, K_DSQ, partial=True)
        b = balloc()
        mm(PS[b][:, :], ones_b[:], t_dsq[:, 0, :], True, False, K_DSQ + [("c", "onesb")], [KP(b)])
        mm(PS[b][:, :], ones_b[:], t_dsq[:, 1, :], False, True, K_DSQ + [("c", "onesb")], [KP(b)])
        act(t_rz, PS[b][:, :], AF.Ln, [KP(b)], K_RZ, scale=1.0 / 256.0, bias=EPS)
        bfree(b)
        act(t_rz, t_rz, AF.Exp, K_RZ, K_RZ, scale=-0.5)
        for jj in range(2):
            stt(mixT[:, 2 * h + jj, :], t_R0[:, jj, :], sm[:, 6 + jj:7 + jj], t_rz, ALU.mult, ALU.mult,
                K_R0 + K_RZ + [("c", "sg")], [KM(2 * h + jj)])

    def run_blocks(blocks, after_block=None):
        due = {}
        nb = len(blocks)
        for i, blk in enumerate(blocks):
            wb = wload(blk["wid"])
            bank = balloc()
            main_mm(wb, bank, blk["rhs"], blk["rkeys"], blk["n"])
            if blk.get("extra") is not None:
                blk["extra"](wb)
            for k, fn in enumerate(blk["stages"]):
                due.setdefault(i + k, []).append((i, k, fn, bank))
            for (_, _, fn, bk) in sorted(due.pop(i, []), key=lambda z: (getattr(z[2], "late", False), z[0], z[1])):
                fn(bk)
            if after_block is not None:
                after_block(i)
        for t in sorted(due.keys()):
            for (_, _, fn, bk) in sorted(due[t], key=lambda z: (z[0], z[1])):
                fn(bk)

    def hrhs(n0=0, n=512):
        return [hT[:, kc, n0:n0 + n] for kc in range(32)], [KH(kc) for kc in range(32)]

    def xt_reload_group(xrows, t4, g, s=0):
        if g == 0:
            dma("sp", xst[s], xrows[t4 * 128:(t4 + 1) * 128, :], [], KXST[s], ("xst", s))
        bank = balloc()
        for q in range(4):
            kc = 4 * g + q
            tr(PS[bank][:, q * 128:(q + 1) * 128], xst[s][:, kc * 128:(kc + 1) * 128],
               [KXST[s][kc // 2]], [KP(bank)])
        cp("dve" if g % 2 == 0 else "act", xT[:, 4 * g:4 * g + 4, t4 * 128:(t4 + 1) * 128],
           PS[bank][:, :].rearrange("p (c t) -> p c t", c=4), [KP(bank)], K1(4 * g, 4 * g + 4), partial=True)
        bfree(bank)

    def p2(sample, krow0, xres=None):
        rhs, rkeys = hrhs()
        blocks = []
        for h in range(NH):
            hp = h % 2
            QT, KT_, VH, KQ, KK, KV = QTs[hp], KTs[hp], VHs[hp], KQs[hp], KKs[hp], KVs[hp]
            for c in range(2):
                blocks.append(dict(wid=WIN0 + 6 * h + c, rhs=rhs, rkeys=rkeys, n=512, stages=[
                    (lambda bank: qk_stage0(bank, V_QG, sample)),
                    (lambda bank, c=c, QT=QT, KQ=KQ: qk_stage1(sample, 0, QT[:, c, :], KQ, True)),
                ]))
            for c in range(2):
                if sample:
                    nkd = None
                else:
                    nkd = nk[krow0:krow0 + 512, (2 * h + c) * 128:(2 * h + c + 1) * 128].rearrange(
                        "(c p) d -> p c d", p=128)
                blocks.append(dict(wid=WIN0 + 6 * h + 2 + c, rhs=rhs, rkeys=rkeys, n=512, stages=[
                    (lambda bank: qk_stage0(bank, V_KG, sample)),
                    (lambda bank, c=c, nkd=nkd, KT_=KT_, KK=KK: qk_stage1(sample, 0, KT_[:, c, :], KK, True,
                                                                          nk_dst=nkd)),
                ]))
            for jj in range(2):
                if sample:
                    nvd = None
                else:
                    nvd = nv[krow0:krow0 + 512, (2 * h + jj) * 128:(2 * h + jj + 1) * 128].rearrange(
                        "(c p) d -> p c d", p=128)

                def v0(bank):
                    cp("dve", t_vT, PS[bank][:, :], [KP(bank)], K_VT)
                    bfree(bank)

                def v1(bank, jj=jj, nvd=nvd, VH=VH, KV=KV):
                    kv_out(t_vT, K_VT, nvd, VH[:, :, jj * 128:(jj + 1) * 128], KV)
                st = [v0, v1]
                if jj == 1:
                    units = []
                    if sample:
                        kch = []
                        for i in range(4):
                            kch.append((KT_[:, 0, i * 128:(i + 1) * 128], KT_[:, 1, i * 128:(i + 1) * 128],
                                        VH[:, i, :], KK + KV))
                        for i in range(4):
                            kch.append((kTo[:, 2 * h, i * 128:(i + 1) * 128],
                                        kTo[:, 2 * h + 1, i * 128:(i + 1) * 128],
                                        vo[:, i, h * 256:(h + 1) * 256], KTO + KVO))
                        for i in range(2):
                            kch.append((kTc[:, 2 * h, i * 128:(i + 1) * 128],
                                        kTc[:, 2 * h + 1, i * 128:(i + 1) * 128],
                                        vc[:, i, h * 256:(h + 1) * 256], KTC + KVC))
                        for c in range(2):
                            units.append((0, 512, kch, c))
                    else:
                        for s_ in range(2):
                            kch = []
                            for i in range(2):
                                o = s_ * 256 + i * 128
                                kch.append((KT_[:, 0, o:o + 128], KT_[:, 1, o:o + 128], VH[:, 2 * s_ + i, :],
                                            KK + KV))
                            for c in range(2):
                                units.append((s_ * 256, 256, kch, c))
                    for (q0, nq, kch, c) in units:
                        def uf(bank, q0=q0, nq=nq, kch=kch, c=c, QT=QT, KQ=KQ):
                            attention_unit(q0, nq, kch, c, QT, KQ)
                        uf.late = False
                        st.append(uf)

                    def bf(bank, h=h):
                        attention_b(h)
                    bf.late = False
                    st.append(bf)
                blocks.append(dict(wid=WIN0 + 6 * h + 4 + jj, rhs=rhs, rkeys=rkeys, n=512, stages=st))
        for j in range(16):
            base = WIN0 + 48 + 3 * j
            hb = [None]

            def halo(which, hb=hb):
                if not sample:
                    return None

                def ex(wb, which=which, hb=hb):
                    if which == 0:
                        hb[0] = balloc()

                    def fn(e, hb=hb):
                        last = None
                        for kc in range(32):
                            last = e.matmul(PS[hb[0]][:, which * 2:which * 2 + 2],
                                            lhsT=WB[wb][:, kc * 128:(kc + 1) * 128],
                                            rhs=hhalo[:, kc * 2:kc * 2 + 2], start=(kc == 0), stop=(kc == 31))
                        return last
                    E.op("pe", fn, [("WB", wb), ("c", "hhalo")], [KP(hb[0])], partial=True)
                return ex

            def c0(bank):
                cp("act", t_t2, PS[bank][:, :], [KP(bank)], K_T2)
                bfree(bank)

            def u0(bank, j=j, hb=hb):
                tt(t_qg, PS[bank][:, :], t_t2, ALU.mult, [KP(bank)] + K_T2, K_QG)
                bfree(bank)
                w0, w1, w2 = (vcol(V_CONV + 3 * j + t) for t in range(3))
                act(t_rs, t_qg, AF.Copy, K_QG + CK, K_RS, scale=w1)
                ns = 1 if sample else 2
                z3 = t_qg.rearrange("p (s t) -> p s t", s=ns)
                a3 = t_rs.rearrange("p (s t) -> p s t", s=ns)
                L = 512 // ns
                stt(a3[:, :, 1:L], z3[:, :, 0:L - 1], w0, a3[:, :, 1:L], ALU.mult, ALU.add, K_QG + K_RS + CK, K_RS)
                stt(a3[:, :, 0:L - 1], z3[:, :, 1:L], w2, a3[:, :, 0:L - 1], ALU.mult, ALU.add, K_QG + K_RS + CK,
                    K_RS)
                if sample:
                    cp("act", sm[:, 62:64], PS[hb[0]][:, 0:2], [KP(hb[0])], [("sm", "hc")])
                    tt(sm[:, 62:64], PS[hb[0]][:, 2:4], sm[:, 62:64], ALU.mult, [KP(hb[0]), ("sm", "hc")],
                       [("sm", "hc")])
                    bfree(hb[0])
                    tt(sm[:, 62:63], sm[:, 62:63], vcol(V_MASK + 1), ALU.mult, [("sm", "hc")] + CK, [("sm", "hc")])
                    tt(sm[:, 63:64], sm[:, 63:64], vcol(V_MASK), ALU.mult, [("sm", "hc")] + CK, [("sm", "hc")])
                    stt(t_rs[:, 511:512], sm[:, 62:63], w2, t_rs[:, 511:512], ALU.mult, ALU.add,
                        [("sm", "hc")] + K_RS + CK, K_RS)
                    stt(t_rs[:, 0:1], sm[:, 63:64], w0, t_rs[:, 0:1], ALU.mult, ALU.add, [("sm", "hc")] + K_RS + CK,
                        K_RS)

            def b0(bank, j=j):
                tt(mixT[:, 16 + j, :], PS[bank][:, :], t_rs, ALU.mult, [KP(bank)] + K_RS, [KM(16 + j)])
                bfree(bank)
            blocks.append(dict(wid=base + 1, rhs=rhs, rkeys=rkeys, n=512, stages=[c0], extra=halo(0)))
            blocks.append(dict(wid=base + 2, rhs=rhs, rkeys=rkeys, n=512, stages=[u0], extra=halo(1)))
            blocks.append(dict(wid=base + 0, rhs=rhs, rkeys=rkeys, n=512, stages=[b0]))
        if P2LIM is not None:
            blocks = blocks[:P2LIM]
        XR0 = 60

        def hook(i):
            ada_some(1)
            if xres is not None and XR0 <= i < XR0 + 32:
                xt_reload_group(xres, (i - XR0) // 8, (i - XR0) % 8)
        run_blocks(blocks, after_block=hook)

    def s0(do_p1=True, own_p1=None):
        own_dst = (mixT, KM)

        def cache_loads(i):
            if i == 6:
                dma("pool", kTc, ckT.rearrange("p (c t) -> p c t", c=16), [], KTC, "kc", max_dma_last_dim=1024)
                dma("pool", vc, cv.rearrange("(c p) e -> p c e", p=128), [], KVC, "vc", max_dma_last_dim=8192)
            if own_p1 is not None:
                if i >= 2 and (i - 2) % 7 == 0 and (i - 2) // 7 < 4:
                    p1_piece(own_p1[0], own_p1[1], (i - 2) // 7, "a", sfix=0, dst=own_dst)
                if i >= 5 and (i - 5) % 7 == 0 and (i - 5) // 7 < 4:
                    p1_piece(own_p1[0], own_p1[1], (i - 5) // 7, "b", sfix=0, dst=own_dst)
        if do_p1:
            p1(xsx, 1)
        hh = hhalo[:].rearrange("p (c t) -> p c t", c=32)
        hsrc = hT
        E.op("dve", lambda e: e.tensor_copy(out=hh[:, :, 0:1], in_=hsrc[:, :, 0:1]), [KH(kc) for kc in range(32)],
             [("c", "hhalo")], partial=True)
        E.op("dve", lambda e: e.tensor_copy(out=hh[:, :, 1:2], in_=hsrc[:, :, 511:512]),
             [KH(kc) for kc in range(32)], [("c", "hhalo")], partial=True)
        rhs, rkeys = hrhs()
        blocks = []
        for h in range(NH):
            for c in range(2):
                blocks.append(dict(wid=WIN0 + 6 * h + 2 + c, rhs=rhs, rkeys=rkeys, n=512, stages=[
                    (lambda bank: qk_stage0(bank, V_KG, True)),
                    (lambda bank, h=h, c=c: qk_stage1(True, 512, kTo[:, 2 * h + c, :], KTO, True)),
                ]))
            for jj in range(2):
                def v0(bank):
                    cp("dve", t_vT, PS[bank][:, :], [KP(bank)], K_VT)
                    bfree(bank)

                def v1(bank, h=h, jj=jj):
                    kv_out(t_vT, K_VT, None, vo[:, :, h * 256 + jj * 128:h * 256 + (jj + 1) * 128], KVO)
                blocks.append(dict(wid=WIN0 + 6 * h + 4 + jj, rhs=rhs, rkeys=rkeys, n=512, stages=[v0, v1]))
        run_blocks(blocks, after_block=cache_loads)

    def p3(xrows, j, preloaded=False):
        for t4 in range(0 if preloaded else 4):
            s = t4 % 2
            dma("sp", xst[s], xrows[t4 * 128:(t4 + 1) * 128, :], [], KXST[s], ("xst", s))
            for g in range(8):
                bank = balloc()
                for q in range(4):
                    kc = 4 * g + q
                    tr(PS[bank][:, q * 128:(q + 1) * 128], xst[s][:, kc * 128:(kc + 1) * 128],
                       [KXST[s][kc // 2]], [KP(bank)])
                cp("dve" if g % 2 == 0 else "act", xT[:, 4 * g:4 * g + 4, t4 * 128:(t4 + 1) * 128],
                   PS[bank][:, :].rearrange("p (c t) -> p c t", c=4), [KP(bank)], K1(4 * g, 4 * g + 4), partial=True)
                bfree(bank)
        rhs = [mixT[:, kc, :] for kc in range(32)]
        rkeys = [KM(kc) for kc in range(32)]
        blocks = []
        for m in range(32):
            def ep(bank, m=m):
                stt(xT[:, m, :], PS[bank][:, :], dvv(2, m, j), xT[:, m, :], ALU.mult, ALU.add,
                    [KP(bank)] + K1(m) + CDVI(2), K1(m))
                bfree(bank)
            blocks.append(dict(wid=WOUT0 + m, rhs=rhs, rkeys=rkeys, n=512, stages=[ep]))
        ada_some(1000 if ada_next[0] < 96 else 0)
        run_blocks(blocks, after_block=lambda i: ada_some(1))
        ada_some(1000)

    def p4(j):
        b = balloc()
        for kc in range(32):
            s = kc % 2
            act(t_E[s], xT[:, kc, :], AF.Square, K1(kc), K_E[s])
            mm(PS[b][:, :], ones_b[:], t_E[s], kc == 0, kc == 31, K_E[s] + [("c", "onesb")], [KP(b)])
        act(t_rs, PS[b][:, :], AF.Ln, [KP(b)], K_RS, scale=1.0 / DM, bias=EPS)
        bfree(b)
        act(t_rs, t_rs, AF.Exp, K_RS, K_RS, scale=-0.5)
        tmp = [t_qg, t_t2]
        ktmp = [K_QG, K_T2]
        for kc in range(32):
            s = kc % 2
            stt(tmp[s], xT[:, kc, :], dvv(3, kc, j), t_rs, ALU.mult, ALU.mult, K1(kc) + K_RS + CDVI(3), ktmp[s])
            act(hT[:, kc, :], tmp[s], AF.Identity, ktmp[s] + CDVI(4), [KH(kc)], bias=dvv(4, kc, j))

    def p56(j, next_p1=None):
        rhs, rkeys = hrhs()
        arhs = [mixT[:, kc, :] for kc in range(32)]
        akeys = [KM(kc) for kc in range(32)]
        tmp = [t_qg, t_t2]
        ktmp = [K_QG, K_T2]
        cnt = [0]
        for g in range(4):
            blocks = []
            for m in range(32):
                def ep(bank, m=m):
                    s = cnt[0] % 2
                    cnt[0] += 1
                    act(tmp[s], PS[bank][:, :], AF.Relu, [KP(bank)], ktmp[s])
                    tt(mixT[:, m, :], PS[bank][:, :], tmp[s], ALU.mult, [KP(bank)] + ktmp[s], [KM(m)])
                    bfree(bank)
                blocks.append(dict(wid=MIN0 + g * 32 + m, rhs=rhs, rkeys=rkeys, n=512, stages=[ep]))
            for m in range(32):
                def ep2(bank, m=m):
                    stt(xT[:, m, :], PS[bank][:, :], dvv(5, m, j), xT[:, m, :], ALU.mult, ALU.add,
                        [KP(bank)] + K1(m) + CDVI(5), K1(m))
                    bfree(bank)
                blocks.append(dict(wid=MOUT0 + g * 32 + m, rhs=arhs, rkeys=akeys, n=512, stages=[ep2]))
            hook = None
            if g == 3 and next_p1 is not None:
                def hook(i, next_p1=next_p1):
                    if i >= 34 and (i - 34) % 8 == 0 and (i - 34) // 8 < 4:
                        p1_piece(next_p1[0], next_p1[1], (i - 34) // 8, "a")
                    if i >= 38 and (i - 38) % 8 == 0 and (i - 38) // 8 < 4:
                        p1_piece(next_p1[0], next_p1[1], (i - 38) // 8, "b")
            run_blocks(blocks, after_block=hook)

    def p7(yrows):
        for t4 in range(4):
            s = t4 % 2
            for g in range(8):
                bank = balloc()
                for q in range(4):
                    kc = 4 * g + q
                    tr(PS[bank][:, q * 128:(q + 1) * 128], xT[:, kc, t4 * 128:(t4 + 1) * 128], K1(kc), [KP(bank)])
                cp("dve" if g % 2 == 0 else "act", xst[s][:, g * 512:(g + 1) * 512], PS[bank][:, :], [KP(bank)],
                   KXST[s][2 * g:2 * g + 2], partial=True)
                bfree(bank)
            dma("sp", yrows[t4 * 128:(t4 + 1) * 128, :], xst[s], KXST[s], [], ("xst", s))

    def prompt_tile(r0, do_p1, next_p1):
        if do_p1:
            p1(xp[r0:r0 + 512, :], 0)
        p2(False, r0, xp[r0:r0 + 512, :])
        p3(xp[r0:r0 + 512, :], 0, preloaded=True)
        p4(0)
        p56(0, next_p1)
        p7(yp[r0:r0 + 512, :])

    def sample_tile():
        s0(False, own_p1=(xso, 1))
        swap_hm()
        p2(True, 0, xso)
        p3(xso, 1, preloaded=True)
        p4(1)
        p56(1)
        p7(ys)

    if "all" in STAGES:
        phase0()
        prompt_tile(0, True, (xp[512:1024, :], 0))
        prompt_tile(512, False, (xsx, 1))
        sample_tile()
    else:
        phase0()
        if "p1" in STAGES:
            p1(xp[0:512, :], 0)
        if "p2" in STAGES:
            p2(False, 0)
        if "p3" in STAGES:
            p3(xp[0:512, :], 0)
        if "p4" in STAGES:
            p4(0)
        if "p56" in STAGES:
            p56(0)
        if "p7" in STAGES:
            p7(yp[0:512, :])
        if "sample" in STAGES:
            p1(xsx, 1)
            sample_tile()

    sem_eng = {k: es.enter_context(nc.semaphore(f"s_{k}")) for k in ("pe", "act", "dve")}
    dma_sems = {}
    dma_cnt = {}
    cnt = {"pe": 0, "act": 0, "dve": 0}
    for o in E.ops:
        if o.dma is not None:
            if o.dma not in dma_sems:
                dma_sems[o.dma] = es.enter_context(nc.semaphore(f"d{len(dma_sems)}"))
                dma_cnt[o.dma] = 0
            dma_cnt[o.dma] += 16
            o.val = (dma_sems[o.dma], dma_cnt[o.dma])
        elif o.sig:
            cnt[o.eng] += 1
            o.val = (sem_eng[o.eng], cnt[o.eng])
    per = {k: [o for o in E.ops if o.eng == k] for k in ("pe", "act", "dve", "sp", "pool")}
    block = es.enter_context(nc.Block())

    def replay(eng, ops, final=False):
        waited = {}
        for o in ops:
            for p in o.deps:
                sem, val = p.val
                if waited.get(id(sem), 0) < val:
                    eng.wait_ge(sem, val)
                    waited[id(sem)] = val
            inst = o.fn(eng)
            if o.dma is not None:
                inst.then_inc(o.val[0], 16)
            elif o.sig:
                inst.then_inc(o.val[0], 1)
        if final:
            for k, sem in dma_sems.items():
                eng.wait_ge(sem, dma_cnt[k])

    @block.tensor
    def _(e):
        replay(e, per["pe"])

    @block.scalar
    def _(e):
        replay(e, per["act"])

    @block.vector
    def _(e):
        replay(e, per["dve"])

    @block.gpsimd
    def _(e):
        replay(e, per["pool"])

    @block.sync
    def _(e):
        replay(e, per["sp"], final=True)

    es.close()
    return nc


def _blocks(W):
    K, F = W.shape
    assert K == 4096
    return np.ascontiguousarray(W.reshape(32, 128, F // 128, 128).transpose(2, 1, 0, 3)).reshape(F // 128, 128, 4096)


def _shared(inp):
    f = np.float32
    w_in = np.asarray(inp["w_in"][0], f)
    order = []
    for h in range(8):
        order += [2 * h, 2 * h + 1, 16 + 2 * h, 16 + 2 * h + 1, 32 + 2 * h, 32 + 2 * h + 1]
    for j in range(16):
        order += [48 + j, 64 + j, 80 + j]
    wblk = np.empty((NBLK, 128, 4096), f)
    wblk[ADA0:ADA0 + 192] = _blocks(np.asarray(inp["w_ada"][0], f))
    wblk[WIN0:WIN0 + 96] = _blocks(w_in)[order]
    wblk[WOUT0:WOUT0 + 32] = _blocks(np.asarray(inp["w_out"][0], f))
    wblk[MIN0:MIN0 + 128] = _blocks(np.asarray(inp["w_mlp_in"][0], f))
    wmo = np.asarray(inp["w_mlp_out"][0], f)
    for g in range(4):
        wblk[MOUT0 + 32 * g:MOUT0 + 32 * (g + 1)] = _blocks(wmo[g * 4096:(g + 1) * 4096])
    vecs = np.zeros((128, NV), f)
    vecs[:, V_BADA:V_BADA + 192] = np.asarray(inp["b_ada"][0], f).reshape(192, 128).T
    vecs[:, V_GATT:V_GATT + 32] = np.asarray(inp["norm_attn_g"][0], f).reshape(32, 128).T
    vecs[:, V_GMLP:V_GMLP + 32] = np.asarray(inp["norm_mlp_g"][0], f).reshape(32, 128).T
    vecs[:, V_QG] = np.asarray(inp["q_norm_g"][0], f)
    vecs[:, V_KG] = np.asarray(inp["k_norm_g"][0], f)
    cw = np.asarray(inp["conv_w"][0], f)
    for j in range(16):
        for t in range(3):
            vecs[:, V_CONV + 3 * j + t] = cw[t, j * 128:(j + 1) * 128]
    vecs[:, V_SUB:V_SUB + 2] = np.asarray(inp["subln_g"][0], f).reshape(2, 128).T
    for i, nm in enumerate(("lambda_q1", "lambda_k1", "lambda_q2", "lambda_k2")):
        vecs[:, V_LAM + i] = np.asarray(inp[nm][0], f)
    consts = np.zeros((128, 384), f)
    consts[:, 0:128] = np.eye(128, dtype=f)
    consts[:, 128:256] = 1.0
    for m in range(128):
        if (m % 64) < 32:
            consts[m + 32, 256 + m] = -1.0
        else:
            consts[m - 32, 256 + m] = 1.0
    return wblk, vecs, consts


def _rope_tables(tok):
    f = np.float32
    inv = np.power(f(10000.0), -np.arange(0, 64, 2, dtype=f) / f(64)).astype(f)
    row = (tok // 64).astype(f)
    col = (tok % 64).astype(f)
    ang = np.empty((128, tok.shape[0]), f)
    for p in range(128):
        pos = row if p < 64 else col
        ang[p] = pos * inv[p % 32]
    return np.cos(ang).astype(f), np.sin(ang).astype(f)


def _core_inputs(inp, c, shared):
    f = np.float32
    wblk, vecs, consts = shared
    sbi, par = c // 2, c % 2
    xs = np.asarray(inp["x_sample"][sbi], f)
    own = slice(512 * par, 512 * par + 512)
    oth = slice(512 * (1 - par), 512 * (1 - par) + 512)
    tok = np.concatenate([np.arange(own.start, own.stop), np.arange(oth.start, oth.stop)])
    rc, rs = _rope_tables(tok)
    v = vecs.copy()
    v[:, V_MASK] = 1.0 if par == 1 else 0.0
    v[:, V_MASK + 1] = 1.0 if par == 0 else 0.0
    ck = np.asarray(inp["cache_k"][sbi, 0], f)
    cond = np.stack([np.asarray(inp["c_ctx"], f), np.asarray(inp["c"][sbi], f)], axis=0)
    condT = np.ascontiguousarray(cond.reshape(2, 32, 128).transpose(2, 1, 0)).reshape(128, 64)
    return {
        "wblk": wblk,
        "xp": np.ascontiguousarray(np.asarray(inp["x_prompt"][4 * c:4 * c + 4], f).reshape(1024, DM)),
        "xso": np.ascontiguousarray(xs[own]),
        "xsx": np.ascontiguousarray(xs[oth]),
        "ckT": np.ascontiguousarray(ck.transpose(3, 1, 2, 0)).reshape(128, 4096),
        "cv": np.ascontiguousarray(np.asarray(inp["cache_v"][sbi, 0], f).reshape(256, 2048)),
        "condT": condT,
        "vecs": v,
        "ropec": rc,
        "ropes": rs,
        "consts": consts,
    }


_NC = [None]


def kernel(**inputs):
    if _NC[0] is None:
        _NC[0] = build_program()
    nc = _NC[0]
    shared = _shared(inputs)
    in_maps = [_core_inputs(inputs, c, shared) for c in range(NCORES)]
    res = run_bass_kernel_spmd(nc, in_maps, core_ids=list(range(NCORES)))
    y_p = np.empty((32, 256, DM), np.float32)
    y_s = np.empty((4, 1024, DM), np.float32)
    new_k = np.empty((32, 1, 256, 8, 2, 128), np.float32)
    new_v = np.empty((32, 1, 256, 8, 256), np.float32)
    for c in range(NCORES):
        r = res.results[c]
        y_p[4 * c:4 * c + 4] = r["yp"].reshape(4, 256, DM)
        par = c % 2
        y_s[c // 2, 512 * par:512 * par + 512] = r["ys"]
        new_k[4 * c:4 * c + 4, 0] = r["nk"].reshape(4, 256, 8, 2, 128)
        new_v[4 * c:4 * c + 4, 0] = r["nv"].reshape(4, 256, 8, 256)
    return (y_p, y_s, new_k, new_v)
```

```python
import math
from contextlib import ExitStack

import numpy as np
import concourse.bass as bass
import concourse.mybir as mybir
from concourse.bass_utils import run_bass_kernel_spmd

F32 = mybir.dt.float32
BF16 = mybir.dt.bfloat16
AF = mybir.ActivationFunctionType
ALU = mybir.AluOpType

NCORES = 8
DM = 4096
NH = 8
EPS = 1e-6
LAM_INIT = 0.8 - 0.6 * math.exp(-0.3 * 0)
NW = 4
STAGES = {"all"}
NADA = 192
P2LIM = None
DBGQ = 0
WMOD = None

ADA0 = 0
WIN0 = 192
WOUT0 = WIN0 + 96
MIN0 = WOUT0 + 32
MOUT0 = MIN0 + 128
NBLK = MOUT0 + 128

V_BADA = 0
V_GATT = 192
V_GMLP = 224
V_QG = 256
V_KG = 257
V_CONV = 258
V_SUB = 306
V_LAM = 308
V_MASK = 312
NV = 320


class Op:
    __slots__ = ("eng", "fn", "deps", "sig", "dma", "val", "idx")


class Em:
    def __init__(self):
        self.ops = []
        self.kw = {}
        self.kr = {}

    def op(self, eng, fn, r=(), w=(), dma=None, partial=False):
        o = Op()
        o.eng, o.fn, o.dma, o.sig, o.val, o.idx = eng, fn, dma, dma is not None, None, len(self.ops)
        deps = {}
        psr = [k for k in r if k[0] == "ps" and k not in w]
        r = [k for k in r if k[0] != "ps"]
        for k in psr:
            for p in self.kw.get(k, ()):
                deps[p.idx] = p
            for p in self.kr.get(k, ()):
                deps[p.idx] = p
        for k in r:
            for p in self.kw.get(k, ()):
                deps[p.idx] = p
        for k in w:
            for p in self.kw.get(k, ()):
                deps[p.idx] = p
            for p in self.kr.get(k, ()):
                deps[p.idx] = p
        o.deps = [p for p in deps.values() if not (p.eng == "pe" and eng == "pe")]
        for p in o.deps:
            p.sig = True
        for k in r:
            lst = self.kr.setdefault(k, [])
            lst[:] = [q for q in lst if q.dma is not None or q.eng != eng]
            lst.append(o)
        for k in psr:
            self.kr[k] = []
            lst = self.kw.setdefault(k, [])
            lst[:] = [q for q in lst if q.dma is not None or q.eng != eng]
            lst.append(o)
        for k in w:
            self.kr[k] = []
            if partial:
                lst = self.kw.setdefault(k, [])
                lst[:] = [q for q in lst if q.dma is not None or q.eng != eng]
                lst.append(o)
            else:
                self.kw[k] = [o]
        self.ops.append(o)
        return o


def build_program():
    nc = bass.Bass("TRN2", target_bir_lowering=False)

    def din(name, shape):
        return nc.dram_tensor(name, shape, F32, kind="ExternalInput").ap()

    def dout(name, shape):
        return nc.dram_tensor(name, shape, F32, kind="ExternalOutput").ap()

    wblk = din("wblk", [WMOD or NBLK, 128, 4096])
    xp = din("xp", [1024, DM])
    xso = din("xso", [512, DM])
    xsx = din("xsx", [512, DM])
    ckT = din("ckT", [128, 4096])
    cv = din("cv", [256, 2048])
    condT = din("condT", [128, 64])
    vecs_d = din("vecs", [128, NV])
    ropec_d = din("ropec", [128, 1024])
    ropes_d = din("ropes", [128, 1024])
    consts_d = din("consts", [128, 384])
    yp = dout("yp", [1024, DM])
    ys = dout("ys", [512, DM])
    nk = dout("nk", [1024, 2048])
    nv = dout("nv", [1024, 2048])

    E = Em()
    es = ExitStack()

    def sb(name, shape, dt):
        return es.enter_context(nc.sbuf_tensor("s_" + name, shape, dt))

    R1 = sb("R1", [128, 16384], F32)
    RH = sb("RH", [128, 16384], BF16)
    RM = sb("RM", [128, 16384], BF16)
    RT = sb("RT", [128, 8192], F32)
    WB = [sb(f"WB{i}", [128, 4096], BF16) for i in range(NW)]
    ident = sb("ident", [128, 128], F32)
    ones_f = sb("ones_f", [128, 128], F32)
    ones_b = sb("ones_b", [128, 128], BF16)
    rot_b = sb("rot_b", [128, 128], BF16)
    vecs = sb("vecs", [128, NV], F32)
    ropec = sb("ropec", [128, 1024], F32)
    ropes = sb("ropes", [128, 1024], F32)
    cond_s = sb("cond_s", [128, 64], F32)
    scT = sb("scT", [128, 64], BF16)
    modT = sb("modT", [128, 384], F32)
    dv = sb("dv", [128, 6 * 64], F32)
    sm = sb("sm", [128, 64], F32)
    hhalo = sb("hhalo", [128, 64], BF16)
    PS = [es.enter_context(nc.psum_tensor(f"ps{i}", [128, 512], F32)) for i in range(8)]

    xT = R1[:].rearrange("p (c t) -> p c t", c=32)
    R1b = R1[:].bitcast(BF16)
    kTo = R1b[:, 0:8192].rearrange("p (c t) -> p c t", c=16)
    vo = R1b[:, 8192:16384].rearrange("p (c e) -> p c e", c=4)
    kTc = R1b[:, 16384:20480].rearrange("p (c t) -> p c t", c=16)
    vc = R1b[:, 20480:24576].rearrange("p (c e) -> p c e", c=2)
    hT = RH[:].rearrange("p (c t) -> p c t", c=32)
    mixT = RM[:].rearrange("p (c t) -> p c t", c=32)
    RTb = RT[:].bitcast(BF16)

    def K1(lo, hi=None):
        return [("R1", i) for i in range(lo, (lo + 1) if hi is None else hi)]

    KTO = K1(0, 8)
    KVO = K1(8, 16)
    KTC = K1(16, 20)
    KVC = K1(20, 24)

    HM = ["RH", "RM"]

    def KH(kc):
        return (HM[0], kc)

    def KM(kc):
        return (HM[1], kc)

    def swap_hm():
        nonlocal hT, mixT
        hT, mixT = mixT, hT
        HM.reverse()

    def rt_f32(slot, n):
        return RT[:, slot * 256: slot * 256 + n]

    def rt_b16(slot, n):
        return RTb[:, slot * 512: slot * 512 + n]

    def KT(lo, n):
        return [("RT", i) for i in range(lo, lo + n)]

    xst = [RT[:, 0:4096], RT[:, 4096:8192]]
    KXST = [KT(0, 16), KT(16, 16)]
    t_qT = rt_b16(0, 1024).rearrange("p (c t) -> p c t", c=2); K_QT = KT(0, 2)
    t_kT = rt_b16(2, 1024).rearrange("p (c t) -> p c t", c=2); K_KT = KT(2, 2)
    t_vh = rt_b16(4, 1024).rearrange("p (c e) -> p c e", c=4); K_VH = KT(4, 2)
    QTs = [t_qT, R1b[:, 24576:25600].rearrange("p (c t) -> p c t", c=2)]
    KTs = [t_kT, R1b[:, 25600:26624].rearrange("p (c t) -> p c t", c=2)]
    VHs = [t_vh, R1b[:, 26624:27648].rearrange("p (c e) -> p c e", c=4)]
    KQs = [K_QT, K1(24)]
    KKs = [K_KT, K1(25)]
    KVs = [K_VH, K1(26)]
    t_E = [rt_b16(6, 512), rt_b16(7, 512)]; K_E = [KT(6, 1), KT(7, 1)]
    t_R0 = rt_f32(8, 1024).rearrange("p (c t) -> p c t", c=2); K_R0 = KT(8, 4)
    t_rz = rt_f32(12, 512); K_RZ = KT(12, 2)
    t_t1 = rt_f32(14, 512); K_T1 = KT(14, 2)
    t_sq = rt_b16(16, 512); K_SQ = KT(16, 1)
    t_qgb = rt_b16(17, 512); K_QGB = KT(17, 1)
    t_qg = rt_f32(18, 512); K_QG = KT(18, 2)
    t_rs = rt_f32(20, 512); K_RS = KT(20, 2)
    t_t2 = rt_f32(22, 512); K_T2 = KT(22, 2)
    t_vT = rt_f32(24, 512); K_VT = KT(24, 2)
    t_st = [rt_f32(26, 512), rt_f32(28, 512)]; K_ST = [KT(26, 2), KT(28, 2)]
    t_dsq = rt_b16(30, 1024).rearrange("p (c t) -> p c t", c=2); K_DSQ = KT(30, 2)
    t_kn = R1[:, 27 * 512:28 * 512]; K_KN = K1(27)

    def vcol(c, n=1):
        return vecs[:, c:c + n]

    def dvv(idx, kc, j):
        c = idx * 64 + kc * 2 + j
        return dv[:, c:c + 1]

    live = [False] * 8
    freed_at = list(range(8))
    fseq = [8]

    def balloc():
        cand = [b for b in range(8) if not live[b]]
        if not cand:
            raise RuntimeError("no free PSUM bank")
        b = min(cand, key=lambda x: freed_at[x])
        live[b] = True
        return b

    def bfree(b):
        live[b] = False
        freed_at[b] = fseq[0]
        fseq[0] += 1

    def KP(b):
        return ("ps", b)

    def dma(q, out, in_, r, w, key, **kw):
        E.op(q, lambda e: e.dma_start(out=out, in_=in_, **kw), r, w, dma=key)

    def act(out, in_, func, r, w, scale=None, bias=None, partial=False):
        kw = {}
        if scale is not None:
            kw["scale"] = scale
        if bias is not None:
            kw["bias"] = bias
        E.op("act", lambda e: e.activation(out=out, in_=in_, func=func, **kw), r, w, partial=partial)

    def tt(out, in0, in1, op, r, w, partial=False):
        E.op("dve", lambda e: e.tensor_tensor(out=out, in0=in0, in1=in1, op=op), r, w, partial=partial)

    def ts(out, in0, s1, op0, r, w, s2=None, op1=None, partial=False):
        if op1 is None:
            E.op("dve", lambda e: e.tensor_scalar(out=out, in0=in0, scalar1=s1, scalar2=None, op0=op0), r, w,
                 partial=partial)
        else:
            E.op("dve", lambda e: e.tensor_scalar(out=out, in0=in0, scalar1=s1, scalar2=s2, op0=op0, op1=op1), r, w,
                 partial=partial)

    def stt(out, in0, scalar, in1, op0, op1, r, w, partial=False):
        E.op("dve", lambda e: e.scalar_tensor_tensor(out=out, in0=in0, scalar=scalar, in1=in1, op0=op0, op1=op1),
             r, w, partial=partial)

    def cp(eng, out, in_, r, w, partial=False):
        if eng == "dve":
            E.op("dve", lambda e: e.tensor_copy(out=out, in_=in_), r, w, partial=partial)
        else:
            E.op("act", lambda e: e.activation(out=out, in_=in_, func=AF.Copy), r, w, partial=partial)

    def mm(out, lhsT, rhs, start, stop, r, w):
        E.op("pe", lambda e: e.matmul(out, lhsT=lhsT, rhs=rhs, start=start, stop=stop), r, w, partial=True)

    def tr(out, in_, r, w):
        E.op("pe", lambda e: e.transpose(out, in_, ident[:]), r + [("c", "ident")], w, partial=True)

    CK = [("c", "k")]

    wcnt = [0]

    def wload(wid):
        b = wcnt[0] % NW
        wcnt[0] += 1
        dma("pool", WB[b][:], wblk[wid % WMOD if WMOD else wid], [], [("WB", b)], ("WB", b), max_dma_last_dim=8192)
        return b

    def main_mm(b, bank, rhs_list, rkeys, n):
        def fn(e):
            last = None
            for kc in range(32):
                last = e.matmul(PS[bank][:, 0:n], lhsT=WB[b][:, kc * 128:(kc + 1) * 128], rhs=rhs_list[kc],
                                start=(kc == 0), stop=(kc == 31))
            return last
        E.op("pe", fn, [("WB", b)] + rkeys, [KP(bank)], partial=True)

    def phase0():
        dma("sp", ident[:], consts_d[:, 0:128], [], [("c", "ident")], "c0a")
        dma("sp", ones_f[:], consts_d[:, 128:256], [], [("c", "onesf")], "c0b")
        dma("sp", vecs[:], vecs_d, [], CK, "c0c")
        dma("sp", ropec[:], ropec_d, [], [("c", "ropec")], "c1a")
        dma("sp", ropes[:], ropes_d, [], [("c", "ropes")], "c1b")
        dma("sp", cond_s[:], condT, [], [("c", "cond")], "c2")
        dma("pool", ones_b[:], consts_d[:, 128:256], [], [("c", "onesb")], "c3")
        dma("pool", rot_b[:], consts_d[:, 256:384], [], [("c", "rot")], "c4")
        act(scT[:], cond_s[:], AF.Silu, [("c", "cond")], [("c", "scT")])
        tt(sm[:, 0:1], vcol(V_LAM), vcol(V_LAM + 1), ALU.mult, CK, [("sm", 0)])
        tt(sm[:, 1:2], vcol(V_LAM + 2), vcol(V_LAM + 3), ALU.mult, CK, [("sm", 1)])
        b = balloc()
        E.op("pe", lambda e: e.matmul(PS[b][:, 0:2], lhsT=ones_f[:], rhs=sm[:, 0:2], start=True, stop=True),
             [("sm", 0), ("sm", 1), ("c", "onesf")], [KP(b)], partial=True)
        act(sm[:, 2:4], PS[b][:, 0:2], AF.Exp, [KP(b)], [("sm", 2)])
        bfree(b)
        tt(sm[:, 4:5], sm[:, 2:3], sm[:, 3:4], ALU.subtract, [("sm", 2)], [("sm", 4)])
        ts(sm[:, 5:6], sm[:, 4:5], -1.0, ALU.mult, [("sm", 4)], [("c", "neglam")], s2=-LAM_INIT, op1=ALU.add)
        ts(sm[:, 6:8], vcol(V_SUB, 2), 1.0 - LAM_INIT, ALU.mult, CK, [("c", "sg")])
        for oc in range(min(64, NADA)):
            ada_block(oc)
        derive(0)
        derive(1)

    def ada_block(oc):
        wb = wload(ADA0 + oc)
        bank = balloc()

        def fn(e, wb=wb, bank=bank):
            last = None
            for kc in range(32):
                last = e.matmul(PS[bank][:, 0:2], lhsT=WB[wb][:, kc * 128:(kc + 1) * 128],
                                rhs=scT[:, kc * 2:kc * 2 + 2], start=(kc == 0), stop=(kc == 31))
            return last
        E.op("pe", fn, [("WB", wb), ("c", "scT")], [KP(bank)], partial=True)
        ts(modT[:, oc * 2:oc * 2 + 2], PS[bank][:, 0:2], vcol(V_BADA + oc), ALU.add, [KP(bank)] + CK,
           [("c", "mod")], partial=True)
        bfree(bank)

    ada_next = [64]

    def ada_some(n=1):
        for _ in range(n):
            if ada_next[0] < NADA:
                oc = ada_next[0]
                ada_next[0] += 1
                ada_block(oc)
                if oc == 95:
                    derive(2)
                elif oc == 159:
                    derive(3)
                    derive(4)
                elif oc == 191:
                    derive(5)

    def derive(dst):
        m4 = modT[:].rearrange("p (i c j) -> p i c j", i=6, c=32)
        d4 = dv[:].rearrange("p (i c j) -> p i c j", i=6, c=32)
        for j in range(2):
            if dst in (0, 3):
                src, g0 = (1, V_GATT) if dst == 0 else (4, V_GMLP)
                stt(d4[:, dst, :, j], m4[:, src, :, j], 1.0, vcol(g0, 32), ALU.add, ALU.mult,
                    [("c", "mod")] + CK, [("c", "dv", dst)], partial=True)
            else:
                src = {1: 0, 2: 2, 4: 3, 5: 5}[dst]
                E.op("dve", lambda e, s_=src, d=dst, j=j: e.tensor_copy(out=d4[:, d, :, j], in_=m4[:, s_, :, j]),
                     [("c", "mod")], [("c", "dv", dst)], partial=True)

    def CDVI(*idx):
        return [("c", "dv", i) for i in idx]

    def p1(xrows, j):
        for t4 in range(4):
            p1_piece(xrows, j, t4)

    def p1_piece(xrows, j, t4, part="ab", sfix=None, dst=None):
        s = t4 % 2 if sfix is None else sfix
        hD, KD = (hT, KH) if dst is None else dst
        if "a" in part:
            dma("sp", xst[s], xrows[t4 * 128:(t4 + 1) * 128, :], [], KXST[s], ("xst", s))
            for q in range(8):
                E.op("dve", lambda e, q=q, s=s: e.bn_stats(out=sm[:, 8 + q * 6: 14 + q * 6],
                                                       in_=xst[s][:, q * 512:(q + 1) * 512]),
                     KXST[s], [("sm", "bn")], partial=True)
            E.op("dve", lambda e: e.bn_aggr(out=sm[:, 56:58], in_=sm[:, 8:56]), [("sm", "bn")], [("sm", "agg")])
            stt(sm[:, 58:59], sm[:, 56:57], sm[:, 56:57], sm[:, 57:58], ALU.mult, ALU.add, [("sm", "agg")],
                [("sm", "ms")])
            act(sm[:, 59:60], sm[:, 58:59], AF.Ln, [("sm", "ms")], [("sm", "ln")], bias=EPS)
            act(sm[:, 60:61], sm[:, 59:60], AF.Exp, [("sm", "ln")], [("sm", "rstd")], scale=-0.5)
            ts(xst[s][:, 0:2048], xst[s][:, 0:2048], sm[:, 60:61], ALU.mult, KXST[s] + [("sm", "rstd")],
               KXST[s][0:8], partial=True)
            act(xst[s][:, 2048:4096], xst[s][:, 2048:4096], AF.Copy, KXST[s] + [("sm", "rstd")], KXST[s][8:16],
                scale=sm[:, 60:61], partial=True)
        if "b" in part:
            for g in range(8):
                bank = balloc()
                for q in range(4):
                    kc = 4 * g + q
                    tr(PS[bank][:, q * 128:(q + 1) * 128], xst[s][:, kc * 128:(kc + 1) * 128],
                       [KXST[s][kc // 2]], [KP(bank)])
                for q in range(4):
                    kc = 4 * g + q
                    act(hD[:, kc, t4 * 128:(t4 + 1) * 128], PS[bank][:, q * 128:(q + 1) * 128], AF.Identity,
                        [KP(bank)] + CDVI(0, 1), [KD(kc)], scale=dvv(0, kc, j), bias=dvv(1, kc, j), partial=True)
                bfree(bank)

    def qk_stage0(bank, gcol, rope):
        if DBGQ == 1:
            cp("dve", t_qg, PS[bank][:, :], [KP(bank)], K_QG)
            bfree(bank)
            return
        if DBGQ == 3:
            act(t_sq, PS[bank][:, :], AF.Square, [KP(bank)], K_SQ)
            bfree(bank)
            return
        if DBGQ == 4:
            ts(t_qg, PS[bank][:, :], vcol(gcol), ALU.mult, [KP(bank)] + CK, K_QG)
            bfree(bank)
            return
        if DBGQ == 5:
            act(hT[:, 0, :], PS[bank][:, :], AF.Square, [KP(bank)], [KH(0)])
            bfree(bank)
            return
        act(t_sq, PS[bank][:, :], AF.Square, [KP(bank)], K_SQ)
        ts(t_qg, PS[bank][:, :], vcol(gcol), ALU.mult, [KP(bank)] + CK, K_QG)
        if rope:
            act(t_qgb, PS[bank][:, :], AF.Copy, [KP(bank)] + CK, K_QGB, scale=vcol(gcol))
        bfree(bank)

    def qk_stage1(rope, tcol0, out_bf, out_keys, out_partial, nk_dst=None):
        if DBGQ in (1, 2, 3, 4, 5):
            return
        b1 = balloc()
        mm(PS[b1][:, :], ones_b[:], t_sq, True, True, K_SQ + [("c", "onesb")], [KP(b1)])
        act(t_rs, PS[b1][:, :], AF.Ln, [KP(b1)], K_RS, scale=1.0 / 128.0, bias=EPS)
        bfree(b1)
        act(t_rs, t_rs, AF.Exp, K_RS, K_RS, scale=-0.5)
        if rope:
            b2 = balloc()
            mm(PS[b2][:, :], rot_b[:], t_qgb, True, True, K_QGB + [("c", "rot")], [KP(b2)])
            tt(t_t2, PS[b2][:, :], ropes[:, tcol0:tcol0 + 512], ALU.mult, [KP(b2), ("c", "ropes")], K_T2)
            bfree(b2)
            tt(t_qg, t_qg, ropec[:, tcol0:tcol0 + 512], ALU.mult, K_QG + [("c", "ropec")], K_QG)
            tt(t_qg, t_qg, t_t2, ALU.add, K_QG + K_T2, K_QG)
            tt(out_bf, t_qg, t_rs, ALU.mult, K_QG + K_RS, out_keys, partial=out_partial)
        elif nk_dst is not None:
            tt(t_kn, t_qg, t_rs, ALU.mult, K_QG + K_RS, K_KN)
            cp("act", out_bf, t_kn, K_KN, out_keys, partial=out_partial)
        else:
            tt(t_qg, t_qg, t_rs, ALU.mult, K_QG + K_RS, K_QG)
            cp("act", out_bf, t_qg, K_QG, out_keys, partial=out_partial)

    stc = [0]

    def kv_out(src_f32, src_keys, dst, bf_out, bf_keys):
        b = balloc()
        for q in range(4):
            tr(PS[b][:, q * 128:(q + 1) * 128], src_f32[:, q * 128:(q + 1) * 128], src_keys, [KP(b)])
        if bf_out is not None:
            cp("act", bf_out, PS[b][:, :].rearrange("p (c e) -> p c e", c=4), [KP(b)], bf_keys, partial=True)
        if dst is not None:
            s = stc[0] % 2
            stc[0] += 1
            cp("dve", t_st[s], PS[b][:, :], [KP(b)], K_ST[s])
            dma("sp", dst, t_st[s].rearrange("p (c e) -> p c e", c=4), K_ST[s], [], ("st", s))
        bfree(b)

    def attention_unit(q0, nq, kchs, c, qT, KQ):
        if True:
            if True:
                bo0, bo1, bz = balloc(), balloc(), balloc()
                nk_ = len(kchs)
                sb_ = [None] * nk_

                def smm(i):
                    sb_[i] = balloc()
                    mm(PS[sb_[i]][:, 0:nq], kchs[i][c], qT[:, c, q0:q0 + nq], True, True, kchs[i][3] + KQ,
                       [KP(sb_[i])])
                smm(0)
                for i in range(nk_):
                    if i + 1 < nk_:
                        smm(i + 1)
                    e_ = t_E[i % 2][:, 0:nq]
                    act(e_, PS[sb_[i]][:, 0:nq], AF.Exp, [KP(sb_[i])], K_E[i % 2], scale=1.0 / math.sqrt(128.0))
                    bfree(sb_[i])
                    v_ = kchs[i][2]
                    mm(PS[bo0][:, 0:nq], v_[:, 0:128], e_, i == 0, i == nk_ - 1, kchs[i][3] + K_E[i % 2], [KP(bo0)])
                    mm(PS[bo1][:, 0:nq], v_[:, 128:256], e_, i == 0, i == nk_ - 1, kchs[i][3] + K_E[i % 2],
                       [KP(bo1)])
                    mm(PS[bz][:, 0:nq], ones_b[:], e_, i == 0, i == nk_ - 1, K_E[i % 2] + [("c", "onesb")], [KP(bz)])
                E.op("dve", lambda e, bz=bz, nq=nq: e.reciprocal(out=t_rz[:, 0:nq], in_=PS[bz][:, 0:nq]), [KP(bz)],
                     K_RZ)
                bfree(bz)
                for jj, bo in enumerate((bo0, bo1)):
                    if c == 0:
                        tt(t_R0[:, jj, q0:q0 + nq], PS[bo][:, 0:nq], t_rz[:, 0:nq], ALU.mult, [KP(bo)] + K_RZ, K_R0,
                           partial=True)
                    else:
                        tt(t_t1[:, 0:nq], PS[bo][:, 0:nq], t_rz[:, 0:nq], ALU.mult, [KP(bo)] + K_RZ, K_T1)
                        stt(t_R0[:, jj, q0:q0 + nq], t_t1[:, 0:nq], sm[:, 5:6], t_R0[:, jj, q0:q0 + nq], ALU.mult,
                            ALU.add, K_T1 + K_R0 + [("c", "neglam")], K_R0, partial=True)
                    bfree(bo)

    def attention_b(h):
        for jj in range(2):
            act(t_dsq[:, jj, :], t_R0[:, jj, :], AF.Square, K_R0, K_DSQ, partial=True)
        b = balloc()
        mm(PS[b][:, :], ones_b[:], t_dsq[:, 0, :], True, False, K_DSQ + [("c", "onesb")], [KP(b)])
        mm(PS[b][:, :], ones_b[:], t_dsq[:, 1, :], False, True, K_DSQ + [("c", "onesb")], [KP(b)])
        act(t_rz, PS[b][:, :], AF.Ln, [KP(b)], K_RZ, scale=1.0 / 256.0, bias=EPS)
        bfree(b)
        act(t_rz, t_rz, AF.Exp, K_RZ, K_RZ, scale=-0.5)
        for jj in range(2):
            stt(mixT[:, 2 * h + jj, :], t_R0[:, jj, :], sm[:, 6 + jj:7 + jj], t_rz, ALU.mult, ALU.mult,
                K_R0 + K_RZ + [("c", "sg")], [KM(2 * h + jj)])

    def run_blocks(blocks, after_block=None):
        due = {}
        nb = len(blocks)
        for i, blk in enumerate(blocks):
            wb = wload(blk["wid"])
            bank = balloc()
            main_mm(wb, bank, blk["rhs"], blk["rkeys"], blk["n"])
            if blk.get("extra") is not None:
                blk["extra"](wb)
            for k, fn in enumerate(blk["stages"]):
                due.setdefault(i + k, []).append((i, k, fn, bank))
            for (_, _, fn, bk) in sorted(due.pop(i, []), key=lambda z: (getattr(z[2], "late", False), z[0], z[1])):
                fn(bk)
            if after_block is not None:
                after_block(i)
        for t in sorted(due.keys()):
            for (_, _, fn, bk) in sorted(due[t], key=lambda z: (z[0], z[1])):
                fn(bk)

    def hrhs(n0=0, n=512):
        return [hT[:, kc, n0:n0 + n] for kc in range(32)], [KH(kc) for kc in range(32)]

    def xt_reload_group(xrows, t4, g, s=0):
        if g == 0:
            dma("sp", xst[s], xrows[t4 * 128:(t4 + 1) * 128, :], [], KXST[s], ("xst", s))
        bank = balloc()
        for q in range(4):
            kc = 4 * g + q
            tr(PS[bank][:, q * 128:(q + 1) * 128], xst[s][:, kc * 128:(kc + 1) * 128],
               [KXST[s][kc // 2]], [KP(bank)])
        cp("dve" if g % 2 == 0 else "act", xT[:, 4 * g:4 * g + 4, t4 * 128:(t4 + 1) * 128],
           PS[bank][:, :].rearrange("p (c t) -> p c t", c=4), [KP(bank)], K1(4 * g, 4 * g + 4), partial=True)
        bfree(bank)

    def p2(sample, krow0, xres=None):
        rhs, rkeys = hrhs()
        blocks = []
        for h in range(NH):
            hp = h % 2
            QT, KT_, VH, KQ, KK, KV = QTs[hp], KTs[hp], VHs[hp], KQs[hp], KKs[hp], KVs[hp]
            for c in range(2):
                blocks.append(dict(wid=WIN0 + 6 * h + c, rhs=rhs, rkeys=rkeys, n=512, stages=[
                    (lambda bank: qk_stage0(bank, V_QG, sample)),
                    (lambda bank, c=c, QT=QT, KQ=KQ: qk_stage1(sample, 0, QT[:, c, :], KQ, True)),
                ]))
            for c in range(2):
                if sample:
                    nkd = None
                else:
                    nkd = nk[krow0:krow0 + 512, (2 * h + c) * 128:(2 * h + c + 1) * 128].rearrange(
                        "(c p) d -> p c d", p=128)
                kst = [
                    (lambda bank: qk_stage0(bank, V_KG, sample)),
                    (lambda bank, c=c, nkd=nkd, KT_=KT_, KK=KK: qk_stage1(sample, 0, KT_[:, c, :], KK, True,
                                                                          nk_dst=nkd)),
                ]
                if nkd is not None:
                    kst.append(lambda bank, nkd=nkd: kv_out(t_kn, K_KN, nkd, None, None))
                blocks.append(dict(wid=WIN0 + 6 * h + 2 + c, rhs=rhs, rkeys=rkeys, n=512, stages=kst))
            for jj in range(2):
                if sample:
                    nvd = None
                else:
                    nvd = nv[krow0:krow0 + 512, (2 * h + jj) * 128:(2 * h + jj + 1) * 128].rearrange(
                        "(c p) d -> p c d", p=128)

                def v0(bank):
                    cp("dve", t_vT, PS[bank][:, :], [KP(bank)], K_VT)
                    bfree(bank)

                def v1(bank, jj=jj, nvd=nvd, VH=VH, KV=KV):
                    kv_out(t_vT, K_VT, nvd, VH[:, :, jj * 128:(jj + 1) * 128], KV)
                st = [v0, v1]
                if jj == 1:
                    units = []
                    if sample:
                        kch = []
                        for i in range(4):
                            kch.append((KT_[:, 0, i * 128:(i + 1) * 128], KT_[:, 1, i * 128:(i + 1) * 128],
                                        VH[:, i, :], KK + KV))
                        for i in range(4):
                            kch.append((kTo[:, 2 * h, i * 128:(i + 1) * 128],
                                        kTo[:, 2 * h + 1, i * 128:(i + 1) * 128],
                                        vo[:, i, h * 256:(h + 1) * 256], KTO + KVO))
                        for i in range(2):
                            kch.append((kTc[:, 2 * h, i * 128:(i + 1) * 128],
                                        kTc[:, 2 * h + 1, i * 128:(i + 1) * 128],
                                        vc[:, i, h * 256:(h + 1) * 256], KTC + KVC))
                        for c in range(2):
                            units.append((0, 512, kch, c))
                    else:
                        for s_ in range(2):
                            kch = []
                            for i in range(2):
                                o = s_ * 256 + i * 128
                                kch.append((KT_[:, 0, o:o + 128], KT_[:, 1, o:o + 128], VH[:, 2 * s_ + i, :],
                                            KK + KV))
                            for c in range(2):
                                units.append((s_ * 256, 256, kch, c))
                    for (q0, nq, kch, c) in units:
                        def uf(bank, q0=q0, nq=nq, kch=kch, c=c, QT=QT, KQ=KQ):
                            attention_unit(q0, nq, kch, c, QT, KQ)
                        uf.late = False
                        st.append(uf)

                    def bf(bank, h=h):
                        attention_b(h)
                    bf.late = False
                    st.append(bf)
                blocks.append(dict(wid=WIN0 + 6 * h + 4 + jj, rhs=rhs, rkeys=rkeys, n=512, stages=st))
        for j in range(16):
            base = WIN0 + 48 + 3 * j
            hb = [None]

            def halo(which, hb=hb):
                if not sample:
                    return None

                def ex(wb, which=which, hb=hb):
                    if which == 0:
                        hb[0] = balloc()

                    def fn(e, hb=hb):
                        last = None
                        for kc in range(32):
                            last = e.matmul(PS[hb[0]][:, which * 2:which * 2 + 2],
                                            lhsT=WB[wb][:, kc * 128:(kc + 1) * 128],
                                            rhs=hhalo[:, kc * 2:kc * 2 + 2], start=(kc == 0), stop=(kc == 31))
                        return last
                    E.op("pe", fn, [("WB", wb), ("c", "hhalo")], [KP(hb[0])], partial=True)
                return ex

            def c0(bank):
                cp("act", t_t2, PS[bank][:, :], [KP(bank)], K_T2)
                bfree(bank)

            def u0(bank, j=j, hb=hb):
                tt(t_qg, PS[bank][:, :], t_t2, ALU.mult, [KP(bank)] + K_T2, K_QG)
                bfree(bank)
                w0, w1, w2 = (vcol(V_CONV + 3 * j + t) for t in range(3))
                act(t_rs, t_qg, AF.Copy, K_QG + CK, K_RS, scale=w1)
                ns = 1 if sample else 2
                z3 = t_qg.rearrange("p (s t) -> p s t", s=ns)
                a3 = t_rs.rearrange("p (s t) -> p s t", s=ns)
                L = 512 // ns
                stt(a3[:, :, 1:L], z3[:, :, 0:L - 1], w0, a3[:, :, 1:L], ALU.mult, ALU.add, K_QG + K_RS + CK, K_RS)
                stt(a3[:, :, 0:L - 1], z3[:, :, 1:L], w2, a3[:, :, 0:L - 1], ALU.mult, ALU.add, K_QG + K_RS + CK,
                    K_RS)
                if sample:
                    cp("act", sm[:, 62:64], PS[hb[0]][:, 0:2], [KP(hb[0])], [("sm", "hc")])
                    tt(sm[:, 62:64], PS[hb[0]][:, 2:4], sm[:, 62:64], ALU.mult, [KP(hb[0]), ("sm", "hc")],
                       [("sm", "hc")])
                    bfree(hb[0])
                    tt(sm[:, 62:63], sm[:, 62:63], vcol(V_MASK + 1), ALU.mult, [("sm", "hc")] + CK, [("sm", "hc")])
                    tt(sm[:, 63:64], sm[:, 63:64], vcol(V_MASK), ALU.mult, [("sm", "hc")] + CK, [("sm", "hc")])
                    stt(t_rs[:, 511:512], sm[:, 62:63], w2, t_rs[:, 511:512], ALU.mult, ALU.add,
                        [("sm", "hc")] + K_RS + CK, K_RS)
                    stt(t_rs[:, 0:1], sm[:, 63:64], w0, t_rs[:, 0:1], ALU.mult, ALU.add, [("sm", "hc")] + K_RS + CK,
                        K_RS)

            def b0(bank, j=j):
                tt(mixT[:, 16 + j, :], PS[bank][:, :], t_rs, ALU.mult, [KP(bank)] + K_RS, [KM(16 + j)])
                bfree(bank)
            blocks.append(dict(wid=base + 1, rhs=rhs, rkeys=rkeys, n=512, stages=[c0], extra=halo(0)))
            blocks.append(dict(wid=base + 2, rhs=rhs, rkeys=rkeys, n=512, stages=[u0], extra=halo(1)))
            blocks.append(dict(wid=base + 0, rhs=rhs, rkeys=rkeys, n=512, stages=[b0]))
        if P2LIM is not None:
            blocks = blocks[:P2LIM]
        XR0 = 60

        def hook(i):
            ada_some(1)
            if xres is not None and XR0 <= i < XR0 + 32:
                xt_reload_group(xres, (i - XR0) // 8, (i - XR0) % 8)
        run_blocks(blocks, after_block=hook)

    def s0(do_p1=True, own_p1=None):
        own_dst = (mixT, KM)

        def cache_loads(i):
            if i == 6:
                dma("pool", kTc, ckT.rearrange("p (c t) -> p c t", c=16), [], KTC, "kc", max_dma_last_dim=1024)
                dma("pool", vc, cv.rearrange("(c p) e -> p c e", p=128), [], KVC, "vc", max_dma_last_dim=8192)
            if own_p1 is not None:
                if i >= 2 and (i - 2) % 7 == 0 and (i - 2) // 7 < 4:
                    p1_piece(own_p1[0], own_p1[1], (i - 2) // 7, "a", sfix=0, dst=own_dst)
                if i >= 5 and (i - 5) % 7 == 0 and (i - 5) // 7 < 4:
                    p1_piece(own_p1[0], own_p1[1], (i - 5) // 7, "b", sfix=0, dst=own_dst)
        if do_p1:
            p1(xsx, 1)
        hh = hhalo[:].rearrange("p (c t) -> p c t", c=32)
        hsrc = hT
        E.op("dve", lambda e: e.tensor_copy(out=hh[:, :, 0:1], in_=hsrc[:, :, 0:1]), [KH(kc) for kc in range(32)],
             [("c", "hhalo")], partial=True)
        E.op("dve", lambda e: e.tensor_copy(out=hh[:, :, 1:2], in_=hsrc[:, :, 511:512]),
             [KH(kc) for kc in range(32)], [("c", "hhalo")], partial=True)
        rhs, rkeys = hrhs()
        blocks = []
        for h in range(NH):
            for c in range(2):
                blocks.append(dict(wid=WIN0 + 6 * h + 2 + c, rhs=rhs, rkeys=rkeys, n=512, stages=[
                    (lambda bank: qk_stage0(bank, V_KG, True)),
                    (lambda bank, h=h, c=c: qk_stage1(True, 512, kTo[:, 2 * h + c, :], KTO, True)),
                ]))
            for jj in range(2):
                def v0(bank):
                    cp("dve", t_vT, PS[bank][:, :], [KP(bank)], K_VT)
                    bfree(bank)

                def v1(bank, h=h, jj=jj):
                    kv_out(t_vT, K_VT, None, vo[:, :, h * 256 + jj * 128:h * 256 + (jj + 1) * 128], KVO)
                blocks.append(dict(wid=WIN0 + 6 * h + 4 + jj, rhs=rhs, rkeys=rkeys, n=512, stages=[v0, v1]))
        run_blocks(blocks, after_block=cache_loads)

    def p3(xrows, j, preloaded=False):
        for t4 in range(0 if preloaded else 4):
            s = t4 % 2
            dma("sp", xst[s], xrows[t4 * 128:(t4 + 1) * 128, :], [], KXST[s], ("xst", s))
            for g in range(8):
                bank = balloc()
                for q in range(4):
                    kc = 4 * g + q
                    tr(PS[bank][:, q * 128:(q + 1) * 128], xst[s][:, kc * 128:(kc + 1) * 128],
                       [KXST[s][kc // 2]], [KP(bank)])
                cp("dve" if g % 2 == 0 else "act", xT[:, 4 * g:4 * g + 4, t4 * 128:(t4 + 1) * 128],
                   PS[bank][:, :].rearrange("p (c t) -> p c t", c=4), [KP(bank)], K1(4 * g, 4 * g + 4), partial=True)
                bfree(bank)
        rhs = [mixT[:, kc, :] for kc in range(32)]
        rkeys = [KM(kc) for kc in range(32)]
        blocks = []
        for m in range(32):
            def ep(bank, m=m):
                stt(xT[:, m, :], PS[bank][:, :], dvv(2, m, j), xT[:, m, :], ALU.mult, ALU.add,
                    [KP(bank)] + K1(m) + CDVI(2), K1(m))
                bfree(bank)
            blocks.append(dict(wid=WOUT0 + m, rhs=rhs, rkeys=rkeys, n=512, stages=[ep]))
        ada_some(1000 if ada_next[0] < 96 else 0)
        run_blocks(blocks, after_block=lambda i: ada_some(1))
        ada_some(1000)

    def p4(j):
        b = balloc()
        for kc in range(32):
            s = kc % 2
            act(t_E[s], xT[:, kc, :], AF.Square, K1(kc), K_E[s])
            mm(PS[b][:, :], ones_b[:], t_E[s], kc == 0, kc == 31, K_E[s] + [("c", "onesb")], [KP(b)])
        act(t_rs, PS[b][:, :], AF.Ln, [KP(b)], K_RS, scale=1.0 / DM, bias=EPS)
        bfree(b)
        act(t_rs, t_rs, AF.Exp, K_RS, K_RS, scale=-0.5)
        tmp = [t_qg, t_t2]
        ktmp = [K_QG, K_T2]
        for kc in range(32):
            s = kc % 2
            stt(tmp[s], xT[:, kc, :], dvv(3, kc, j), t_rs, ALU.mult, ALU.mult, K1(kc) + K_RS + CDVI(3), ktmp[s])
            act(hT[:, kc, :], tmp[s], AF.Identity, ktmp[s] + CDVI(4), [KH(kc)], bias=dvv(4, kc, j))

    def p56(j, next_p1=None):
        rhs, rkeys = hrhs()
        arhs = [mixT[:, kc, :] for kc in range(32)]
        akeys = [KM(kc) for kc in range(32)]
        tmp = [t_qg, t_t2]
        ktmp = [K_QG, K_T2]
        cnt = [0]
        for g in range(4):
            blocks = []
            for m in range(32):
                def ep(bank, m=m):
                    s = cnt[0] % 2
                    cnt[0] += 1
                    act(tmp[s], PS[bank][:, :], AF.Relu, [KP(bank)], ktmp[s])
                    tt(mixT[:, m, :], PS[bank][:, :], tmp[s], ALU.mult, [KP(bank)] + ktmp[s], [KM(m)])
                    bfree(bank)
                blocks.append(dict(wid=MIN0 + g * 32 + m, rhs=rhs, rkeys=rkeys, n=512, stages=[ep]))
            for m in range(32):
                def ep2(bank, m=m):
                    stt(xT[:, m, :], PS[bank][:, :], dvv(5, m, j), xT[:, m, :], ALU.mult, ALU.add,
                        [KP(bank)] + K1(m) + CDVI(5), K1(m))
                    bfree(bank)
                blocks.append(dict(wid=MOUT0 + g * 32 + m, rhs=arhs, rkeys=akeys, n=512, stages=[ep2]))
            hook = None
            if g == 3 and next_p1 is not None:
                def hook(i, next_p1=next_p1):
                    if i >= 34 and (i - 34) % 8 == 0 and (i - 34) // 8 < 4:
                        p1_piece(next_p1[0], next_p1[1], (i - 34) // 8, "a")
                    if i >= 38 and (i - 38) % 8 == 0 and (i - 38) // 8 < 4:
                        p1_piece(next_p1[0], next_p1[1], (i - 38) // 8, "b")
            run_blocks(blocks, after_block=hook)

    def p7(yrows):
        for t4 in range(4):
            s = t4 % 2
            for g in range(8):
                bank = balloc()
                for q in range(4):
                    kc = 4 * g + q
                    tr(PS[bank][:, q * 128:(q + 1) * 128], xT[:, kc, t4 * 128:(t4 + 1) * 128], K1(kc), [KP(bank)])
                cp("dve" if g % 2 == 0 else "act", xst[s][:, g * 512:(g + 1) * 512], PS[bank][:, :], [KP(bank)],
                   KXST[s][2 * g:2 * g + 2], partial=True)
                bfree(bank)
            dma("sp", yrows[t4 * 128:(t4 + 1) * 128, :], xst[s], KXST[s], [], ("xst", s))

    def prompt_tile(r0, do_p1, next_p1):
        if do_p1:
            p1(xp[r0:r0 + 512, :], 0)
        p2(False, r0, xp[r0:r0 + 512, :])
        p3(xp[r0:r0 + 512, :], 0, preloaded=True)
        p4(0)
        p56(0, next_p1)
        p7(yp[r0:r0 + 512, :])

    def sample_tile():
        s0(False, own_p1=(xso, 1))
        swap_hm()
        p2(True, 0, xso)
        p3(xso, 1, preloaded=True)
        p4(1)
        p56(1)
        p7(ys)

    if "all" in STAGES:
        phase0()
        prompt_tile(0, True, (xp[512:1024, :], 0))
        prompt_tile(512, False, (xsx, 1))
        sample_tile()
    else:
        phase0()
        if "p1" in STAGES:
            p1(xp[0:512, :], 0)
        if "p2" in STAGES:
            p2(False, 0)
        if "p3" in STAGES:
            p3(xp[0:512, :], 0)
        if "p4" in STAGES:
            p4(0)
        if "p56" in STAGES:
            p56(0)
        if "p7" in STAGES:
            p7(yp[0:512, :])
        if "sample" in STAGES:
            p1(xsx, 1)
            sample_tile()

    sem_eng = {k: es.enter_context(nc.semaphore(f"s_{k}")) for k in ("pe", "act", "dve")}
    dma_sems = {}
    dma_cnt = {}
    cnt = {"pe": 0, "act": 0, "dve": 0}
    for o in E.ops:
        if o.dma is not None:
            if o.dma not in dma_sems:
                dma_sems[o.dma] = es.enter_context(nc.semaphore(f"d{len(dma_sems)}"))
                dma_cnt[o.dma] = 0
            dma_cnt[o.dma] += 16
            o.val = (dma_sems[o.dma], dma_cnt[o.dma])
        elif o.sig:
            cnt[o.eng] += 1
            o.val = (sem_eng[o.eng], cnt[o.eng])
    per = {k: [o for o in E.ops if o.eng == k] for k in ("pe", "act", "dve", "sp", "pool")}
    block = es.enter_context(nc.Block())

    def replay(eng, ops, final=False):
        waited = {}
        for o in ops:
            for p in o.deps:
                sem, val = p.val
                if waited.get(id(sem), 0) < val:
                    eng.wait_ge(sem, val)
                    waited[id(sem)] = val
            inst = o.fn(eng)
            if o.dma is not None:
                inst.then_inc(o.val[0], 16)
            elif o.sig:
                inst.then_inc(o.val[0], 1)
        if final:
            for k, sem in dma_sems.items():
                eng.wait_ge(sem, dma_cnt[k])

    @block.tensor
    def _(e):
        replay(e, per["pe"])

    @block.scalar
    def _(e):
        replay(e, per["act"])

    @block.vector
    def _(e):
        replay(e, per["dve"])

    @block.gpsimd
    def _(e):
        replay(e, per["pool"])

    @block.sync
    def _(e):
        replay(e, per["sp"], final=True)

    es.close()
    return nc


def _blocks(W):
    K, F = W.shape
    assert K == 4096
    return np.ascontiguousarray(W.reshape(32, 128, F // 128, 128).transpose(2, 1, 0, 3)).reshape(F // 128, 128, 4096)


def _shared(inp):
    f = np.float32
    w_in = np.asarray(inp["w_in"][0], f)
    order = []
    for h in range(8):
        order += [2 * h, 2 * h + 1, 16 + 2 * h, 16 + 2 * h + 1, 32 + 2 * h, 32 + 2 * h + 1]
    for j in range(16):
        order += [48 + j, 64 + j, 80 + j]
    wblk = np.empty((NBLK, 128, 4096), f)
    wblk[ADA0:ADA0 + 192] = _blocks(np.asarray(inp["w_ada"][0], f))
    wblk[WIN0:WIN0 + 96] = _blocks(w_in)[order]
    wblk[WOUT0:WOUT0 + 32] = _blocks(np.asarray(inp["w_out"][0], f))
    wblk[MIN0:MIN0 + 128] = _blocks(np.asarray(inp["w_mlp_in"][0], f))
    wmo = np.asarray(inp["w_mlp_out"][0], f)
    for g in range(4):
        wblk[MOUT0 + 32 * g:MOUT0 + 32 * (g + 1)] = _blocks(wmo[g * 4096:(g + 1) * 4096])
    vecs = np.zeros((128, NV), f)
    vecs[:, V_BADA:V_BADA + 192] = np.asarray(inp["b_ada"][0], f).reshape(192, 128).T
    vecs[:, V_GATT:V_GATT + 32] = np.asarray(inp["norm_attn_g"][0], f).reshape(32, 128).T
    vecs[:, V_GMLP:V_GMLP + 32] = np.asarray(inp["norm_mlp_g"][0], f).reshape(32, 128).T
    vecs[:, V_QG] = np.asarray(inp["q_norm_g"][0], f)
    vecs[:, V_KG] = np.asarray(inp["k_norm_g"][0], f)
    cw = np.asarray(inp["conv_w"][0], f)
    for j in range(16):
        for t in range(3):
            vecs[:, V_CONV + 3 * j + t] = cw[t, j * 128:(j + 1) * 128]
    vecs[:, V_SUB:V_SUB + 2] = np.asarray(inp["subln_g"][0], f).reshape(2, 128).T
    for i, nm in enumerate(("lambda_q1", "lambda_k1", "lambda_q2", "lambda_k2")):
        vecs[:, V_LAM + i] = np.asarray(inp[nm][0], f)
    consts = np.zeros((128, 384), f)
    consts[:, 0:128] = np.eye(128, dtype=f)
    consts[:, 128:256] = 1.0
    for m in range(128):
        if (m % 64) < 32:
            consts[m + 32, 256 + m] = -1.0
        else:
            consts[m - 32, 256 + m] = 1.0
    return wblk, vecs, consts


def _rope_tables(tok):
    f = np.float32
    inv = np.power(f(10000.0), -np.arange(0, 64, 2, dtype=f) / f(64)).astype(f)
    row = (tok // 64).astype(f)
    col = (tok % 64).astype(f)
    ang = np.empty((128, tok.shape[0]), f)
    for p in range(128):
        pos = row if p < 64 else col
        ang[p] = pos * inv[p % 32]
    return np.cos(ang).astype(f), np.sin(ang).astype(f)


def _core_inputs(inp, c, shared):
    f = np.float32
    wblk, vecs, consts = shared
    sbi, par = c // 2, c % 2
    xs = np.asarray(inp["x_sample"][sbi], f)
    own = slice(512 * par, 512 * par + 512)
    oth = slice(512 * (1 - par), 512 * (1 - par) + 512)
    tok = np.concatenate([np.arange(own.start, own.stop), np.arange(oth.start, oth.stop)])
    rc, rs = _rope_tables(tok)
    v = vecs.copy()
    v[:, V_MASK] = 1.0 if par == 1 else 0.0
    v[:, V_MASK + 1] = 1.0 if par == 0 else 0.0
    ck = np.asarray(inp["cache_k"][sbi, 0], f)
    cond = np.stack([np.asarray(inp["c_ctx"], f), np.asarray(inp["c"][sbi], f)], axis=0)
    condT = np.ascontiguousarray(cond.reshape(2, 32, 128).transpose(2, 1, 0)).reshape(128, 64)
    return {
        "wblk": wblk,
        "xp": np.ascontiguousarray(np.asarray(inp["x_prompt"][4 * c:4 * c + 4], f).reshape(1024, DM)),
        "xso": np.ascontiguousarray(xs[own]),
        "xsx": np.ascontiguousarray(xs[oth]),
        "ckT": np.ascontiguousarray(ck.transpose(3, 1, 2, 0)).reshape(128, 4096),
        "cv": np.ascontiguousarray(np.asarray(inp["cache_v"][sbi, 0], f).reshape(256, 2048)),
        "condT": condT,
        "vecs": v,
        "ropec": rc,
        "ropes": rs,
        "consts": consts,
    }


_NC = [None]


def kernel(**inputs):
    if _NC[0] is None:
        _NC[0] = build_program()
    nc = _NC[0]
    shared = _shared(inputs)
    in_maps = [_core_inputs(inputs, c, shared) for c in range(NCORES)]
    res = run_bass_kernel_spmd(nc, in_maps, core_ids=list(range(NCORES)))
    y_p = np.empty((32, 256, DM), np.float32)
    y_s = np.empty((4, 1024, DM), np.float32)
    new_k = np.empty((32, 1, 256, 8, 2, 128), np.float32)
    new_v = np.empty((32, 1, 256, 8, 256), np.float32)
    for c in range(NCORES):
        r = res.results[c]
        y_p[4 * c:4 * c + 4] = r["yp"].reshape(4, 256, DM)
        par = c % 2
        y_s[c // 2, 512 * par:512 * par + 512] = r["ys"]
        new_k[4 * c:4 * c + 4, 0] = r["nk"].reshape(4, 256, 8, 2, 128)
        new_v[4 * c:4 * c + 4, 0] = r["nv"].reshape(4, 256, 8, 256)
    return (y_p, y_s, new_k, new_v)
```

```python
import math
from contextlib import ExitStack

import numpy as np
import concourse.bass as bass
import concourse.mybir as mybir
from concourse.bass_utils import run_bass_kernel_spmd

F32 = mybir.dt.float32
BF16 = mybir.dt.bfloat16
AF = mybir.ActivationFunctionType
ALU = mybir.AluOpType

NCORES = 8
DM = 4096
NH = 8
EPS = 1e-6
LAM_INIT = 0.8 - 0.6 * math.exp(-0.3 * 0)
NW = 4
STAGES = {"all"}
NADA = 192
P2LIM = None
DBGQ = 0
WMOD = None

ADA0 = 0
WIN0 = 192
WOUT0 = WIN0 + 96
MIN0 = WOUT0 + 32
MOUT0 = MIN0 + 128
NBLK = MOUT0 + 128

V_BADA = 0
V_GATT = 192
V_GMLP = 224
V_QG = 256
V_KG = 257
V_CONV = 258
V_SUB = 306
V_LAM = 308
V_MASK = 312
NV = 320


class Op:
    __slots__ = ("eng", "fn", "deps", "sig", "dma", "val", "idx")


class Em:
    def __init__(self):
        self.ops = []
        self.kw = {}
        self.kr = {}

    def op(self, eng, fn, r=(), w=(), dma=None, partial=False):
        o = Op()
        o.eng, o.fn, o.dma, o.sig, o.val, o.idx = eng, fn, dma, dma is not None, None, len(self.ops)
        deps = {}
        psr = [k for k in r if k[0] == "ps" and k not in w]
        r = [k for k in r if k[0] != "ps"]
        for k in psr:
            for p in self.kw.get(k, ()):
                deps[p.idx] = p
            for p in self.kr.get(k, ()):
                deps[p.idx] = p
        for k in r:
            for p in self.kw.get(k, ()):
                deps[p.idx] = p
        for k in w:
            for p in self.kw.get(k, ()):
                deps[p.idx] = p
            for p in self.kr.get(k, ()):
                deps[p.idx] = p
        o.deps = [p for p in deps.values() if not (p.eng == "pe" and eng == "pe")]
        for p in o.deps:
            p.sig = True
        for k in r:
            lst = self.kr.setdefault(k, [])
            lst[:] = [q for q in lst if q.dma is not None or q.eng != eng]
            lst.append(o)
        for k in psr:
            self.kr[k] = []
            lst = self.kw.setdefault(k, [])
            lst[:] = [q for q in lst if q.dma is not None or q.eng != eng]
            lst.append(o)
        for k in w:
            self.kr[k] = []
            if partial:
                lst = self.kw.setdefault(k, [])
                lst[:] = [q for q in lst if q.dma is not None or q.eng != eng]
                lst.append(o)
            else:
                self.kw[k] = [o]
        self.ops.append(o)
        return o


def build_program():
    nc = bass.Bass("TRN2", target_bir_lowering=False)

    def din(name, shape):
        return nc.dram_tensor(name, shape, F32, kind="ExternalInput").ap()

    def dout(name, shape):
        return nc.dram_tensor(name, shape, F32, kind="ExternalOutput").ap()

    wblk = din("wblk", [WMOD or NBLK, 128, 4096])
    xp = din("xp", [1024, DM])
    xso = din("xso", [512, DM])
    xsx = din("xsx", [512, DM])
    ckT = din("ckT", [128, 4096])
    cv = din("cv", [256, 2048])
    condT = din("condT", [128, 64])
    vecs_d = din("vecs", [128, NV])
    ropec_d = din("ropec", [128, 1024])
    ropes_d = din("ropes", [128, 1024])
    consts_d = din("consts", [128, 384])
    yp = dout("yp", [1024, DM])
    ys = dout("ys", [512, DM])
    nk = dout("nk", [1024, 2048])
    nv = dout("nv", [1024, 2048])

    E = Em()
    es = ExitStack()

    def sb(name, shape, dt):
        return es.enter_context(nc.sbuf_tensor("s_" + name, shape, dt))

    R1 = sb("R1", [128, 16384], F32)
    RH = sb("RH", [128, 16384], BF16)
    RM = sb("RM", [128, 16384], BF16)
    RT = sb("RT", [128, 8192], F32)
    WB = [sb(f"WB{i}", [128, 4096], BF16) for i in range(NW)]
    ident = sb("ident", [128, 128], F32)
    ones_f = sb("ones_f", [128, 128], F32)
    ones_b = sb("ones_b", [128, 128], BF16)
    rot_b = sb("rot_b", [128, 128], BF16)
    vecs = sb("vecs", [128, NV], F32)
    ropec = sb("ropec", [128, 1024], F32)
    ropes = sb("ropes", [128, 1024], F32)
    cond_s = sb("cond_s", [128, 64], F32)
    scT = sb("scT", [128, 64], BF16)
    modT = sb("modT", [128, 384], F32)
    dv = sb("dv", [128, 6 * 64], F32)
    sm = sb("sm", [128, 64], F32)
    hhalo = sb("hhalo", [128, 64], BF16)
    PS = [es.enter_context(nc.psum_tensor(f"ps{i}", [128, 512], F32)) for i in range(8)]

    xT = R1[:].rearrange("p (c t) -> p c t", c=32)
    R1b = R1[:].bitcast(BF16)
    kTo = R1b[:, 0:8192].rearrange("p (c t) -> p c t", c=16)
    vo = R1b[:, 8192:16384].rearrange("p (c e) -> p c e", c=4)
    kTc = R1b[:, 16384:20480].rearrange("p (c t) -> p c t", c=16)
    vc = R1b[:, 20480:24576].rearrange("p (c e) -> p c e", c=2)
    hT = RH[:].rearrange("p (c t) -> p c t", c=32)
    mixT = RM[:].rearrange("p (c t) -> p c t", c=32)
    RTb = RT[:].bitcast(BF16)

    def K1(lo, hi=None):
        return [("R1", i) for i in range(lo, (lo + 1) if hi is None else hi)]

    KTO = K1(0, 8)
    KVO = K1(8, 16)
    KTC = K1(16, 20)
    KVC = K1(20, 24)

    HM = ["RH", "RM"]

    def KH(kc):
        return (HM[0], kc)

    def KM(kc):
        return (HM[1], kc)

    def swap_hm():
        nonlocal hT, mixT
        hT, mixT = mixT, hT
        HM.reverse()

    def rt_f32(slot, n):
        return RT[:, slot * 256: slot * 256 + n]

    def rt_b16(slot, n):
        return RTb[:, slot * 512: slot * 512 + n]

    def KT(lo, n):
        return [("RT", i) for i in range(lo, lo + n)]

    xst = [RT[:, 0:4096], RT[:, 4096:8192]]
    KXST = [KT(0, 16), KT(16, 16)]
    t_qT = rt_b16(0, 1024).rearrange("p (c t) -> p c t", c=2); K_QT = KT(0, 2)
    t_kT = rt_b16(2, 1024).rearrange("p (c t) -> p c t", c=2); K_KT = KT(2, 2)
    t_vh = rt_b16(4, 1024).rearrange("p (c e) -> p c e", c=4); K_VH = KT(4, 2)
    QTs = [t_qT, R1b[:, 24576:25600].rearrange("p (c t) -> p c t", c=2)]
    KTs = [t_kT, R1b[:, 25600:26624].rearrange("p (c t) -> p c t", c=2)]
    VHs = [t_vh, R1b[:, 26624:27648].rearrange("p (c e) -> p c e", c=4)]
    KQs = [K_QT, K1(24)]
    KKs = [K_KT, K1(25)]
    KVs = [K_VH, K1(26)]
    t_E = [rt_b16(6, 512), rt_b16(7, 512)]; K_E = [KT(6, 1), KT(7, 1)]
    t_R0 = rt_f32(8, 1024).rearrange("p (c t) -> p c t", c=2); K_R0 = KT(8, 4)
    t_rz = rt_f32(12, 512); K_RZ = KT(12, 2)
    t_t1 = rt_f32(14, 512); K_T1 = KT(14, 2)
    t_sq = rt_b16(16, 512); K_SQ = KT(16, 1)
    t_qgb = rt_b16(17, 512); K_QGB = KT(17, 1)
    t_qg = rt_f32(18, 512); K_QG = KT(18, 2)
    t_rs = rt_f32(20, 512); K_RS = KT(20, 2)
    t_t2 = rt_f32(22, 512); K_T2 = KT(22, 2)
    t_vT = rt_f32(24, 512); K_VT = KT(24, 2)
    t_st = [rt_f32(26, 512), rt_f32(28, 512)]; K_ST = [KT(26, 2), KT(28, 2)]
    t_dsq = rt_b16(30, 1024).rearrange("p (c t) -> p c t", c=2); K_DSQ = KT(30, 2)
    t_kn = R1[:, 27 * 512:28 * 512]; K_KN = K1(27)

    def vcol(c, n=1):
        return vecs[:, c:c + n]

    def dvv(idx, kc, j):
        c = idx * 64 + kc * 2 + j
        return dv[:, c:c + 1]

    live = [False] * 8
    freed_at = list(range(8))
    fseq = [8]

    def balloc():
        cand = [b for b in range(8) if not live[b]]
        if not cand:
            raise RuntimeError("no free PSUM bank")
        b = min(cand, key=lambda x: freed_at[x])
        live[b] = True
        return b

    def bfree(b):
        live[b] = False
        freed_at[b] = fseq[0]
        fseq[0] += 1

    def KP(b):
        return ("ps", b)

    def dma(q, out, in_, r, w, key, **kw):
        E.op(q, lambda e: e.dma_start(out=out, in_=in_, **kw), r, w, dma=key)

    def act(out, in_, func, r, w, scale=None, bias=None, partial=False):
        kw = {}
        if scale is not None:
            kw["scale"] = scale
        if bias is not None:
            kw["bias"] = bias
        E.op("act", lambda e: e.activation(out=out, in_=in_, func=func, **kw), r, w, partial=partial)

    def tt(out, in0, in1, op, r, w, partial=False):
        E.op("dve", lambda e: e.tensor_tensor(out=out, in0=in0, in1=in1, op=op), r, w, partial=partial)

    def ts(out, in0, s1, op0, r, w, s2=None, op1=None, partial=False):
        if op1 is None:
            E.op("dve", lambda e: e.tensor_scalar(out=out, in0=in0, scalar1=s1, scalar2=None, op0=op0), r, w,
                 partial=partial)
        else:
            E.op("dve", lambda e: e.tensor_scalar(out=out, in0=in0, scalar1=s1, scalar2=s2, op0=op0, op1=op1), r, w,
                 partial=partial)

    def stt(out, in0, scalar, in1, op0, op1, r, w, partial=False):
        E.op("dve", lambda e: e.scalar_tensor_tensor(out=out, in0=in0, scalar=scalar, in1=in1, op0=op0, op1=op1),
             r, w, partial=partial)

    def cp(eng, out, in_, r, w, partial=False):
        if eng == "dve":
            E.op("dve", lambda e: e.tensor_copy(out=out, in_=in_), r, w, partial=partial)
        else:
            E.op("act", lambda e: e.activation(out=out, in_=in_, func=AF.Copy), r, w, partial=partial)

    def mm(out, lhsT, rhs, start, stop, r, w):
        E.op("pe", lambda e: e.matmul(out, lhsT=lhsT, rhs=rhs, start=start, stop=stop), r, w, partial=True)

    def tr(out, in_, r, w):
        E.op("pe", lambda e: e.transpose(out, in_, ident[:]), r + [("c", "ident")], w, partial=True)

    CK = [("c", "k")]

    wcnt = [0]

    def wload(wid):
        b = wcnt[0] % NW
        wcnt[0] += 1
        dma("pool", WB[b][:], wblk[wid % WMOD if WMOD else wid], [], [("WB", b)], ("WB", b), max_dma_last_dim=8192)
        return b

    def main_mm(b, bank, rhs_list, rkeys, n):
        def fn(e):
            last = None
            for kc in range(32):
                last = e.matmul(PS[bank][:, 0:n], lhsT=WB[b][:, kc * 128:(kc + 1) * 128], rhs=rhs_list[kc],
                                start=(kc == 0), stop=(kc == 31))
            return last
        E.op("pe", fn, [("WB", b)] + rkeys, [KP(bank)], partial=True)

    def phase0():
        dma("sp", ident[:], consts_d[:, 0:128], [], [("c", "ident")], "c0a")
        dma("sp", ones_f[:], consts_d[:, 128:256], [], [("c", "onesf")], "c0b")
        dma("sp", vecs[:], vecs_d, [], CK, "c0c")
        dma("sp", ropec[:], ropec_d, [], [("c", "ropec")], "c1a")
        dma("sp", ropes[:], ropes_d, [], [("c", "ropes")], "c1b")
        dma("sp", cond_s[:], condT, [], [("c", "cond")], "c2")
        dma("pool", ones_b[:], consts_d[:, 128:256], [], [("c", "onesb")], "c3")
        dma("pool", rot_b[:], consts_d[:, 256:384], [], [("c", "rot")], "c4")
        act(scT[:], cond_s[:], AF.Silu, [("c", "cond")], [("c", "scT")])
        tt(sm[:, 0:1], vcol(V_LAM), vcol(V_LAM + 1), ALU.mult, CK, [("sm", 0)])
        tt(sm[:, 1:2], vcol(V_LAM + 2), vcol(V_LAM + 3), ALU.mult, CK, [("sm", 1)])
        b = balloc()
        E.op("pe", lambda e: e.matmul(PS[b][:, 0:2], lhsT=ones_f[:], rhs=sm[:, 0:2], start=True, stop=True),
             [("sm", 0), ("sm", 1), ("c", "onesf")], [KP(b)], partial=True)
        act(sm[:, 2:4], PS[b][:, 0:2], AF.Exp, [KP(b)], [("sm", 2)])
        bfree(b)
        tt(sm[:, 4:5], sm[:, 2:3], sm[:, 3:4], ALU.subtract, [("sm", 2)], [("sm", 4)])
        ts(sm[:, 5:6], sm[:, 4:5], -1.0, ALU.mult, [("sm", 4)], [("c", "neglam")], s2=-LAM_INIT, op1=ALU.add)
        ts(sm[:, 6:8], vcol(V_SUB, 2), 1.0 - LAM_INIT, ALU.mult, CK, [("c", "sg")])
        for oc in range(min(64, NADA)):
            ada_block(oc)
        derive(0)
        derive(1)

    def ada_block(oc):
        wb = wload(ADA0 + oc)
        bank = balloc()

        def fn(e, wb=wb, bank=bank):
            last = None
            for kc in range(32):
                last = e.matmul(PS[bank][:, 0:2], lhsT=WB[wb][:, kc * 128:(kc + 1) * 128],
                                rhs=scT[:, kc * 2:kc * 2 + 2], start=(kc == 0), stop=(kc == 31))
            return last
        E.op("pe", fn, [("WB", wb), ("c", "scT")], [KP(bank)], partial=True)
        ts(modT[:, oc * 2:oc * 2 + 2], PS[bank][:, 0:2], vcol(V_BADA + oc), ALU.add, [KP(bank)] + CK,
           [("c", "mod")], partial=True)
        bfree(bank)

    ada_next = [64]

    def ada_some(n=1):
        for _ in range(n):
            if ada_next[0] < NADA:
                oc = ada_next[0]
                ada_next[0] += 1
                ada_block(oc)
                if oc == 95:
                    derive(2)
                elif oc == 159:
                    derive(3)
                    derive(4)
                if oc >= 160:
                    derive5_chunk(oc - 160)

    def derive5_chunk(k):
        m4 = modT[:].rearrange("p (i c j) -> p i c j", i=6, c=32)
        d4 = dv[:].rearrange("p (i c j) -> p i c j", i=6, c=32)
        E.op("dve", lambda e: e.tensor_copy(out=d4[:, 5, k, :], in_=m4[:, 5, k, :]), [("c", "mod")],
             [("c", "dv", 5)], partial=True)

    def derive(dst):
        m4 = modT[:].rearrange("p (i c j) -> p i c j", i=6, c=32)
        d4 = dv[:].rearrange("p (i c j) -> p i c j", i=6, c=32)
        for j in range(2):
            if dst in (0, 3):
                src, g0 = (1, V_GATT) if dst == 0 else (4, V_GMLP)
                stt(d4[:, dst, :, j], m4[:, src, :, j], 1.0, vcol(g0, 32), ALU.add, ALU.mult,
                    [("c", "mod")] + CK, [("c", "dv", dst)], partial=True)
            else:
                src = {1: 0, 2: 2, 4: 3, 5: 5}[dst]
                E.op("dve", lambda e, s_=src, d=dst, j=j: e.tensor_copy(out=d4[:, d, :, j], in_=m4[:, s_, :, j]),
                     [("c", "mod")], [("c", "dv", dst)], partial=True)

    def CDVI(*idx):
        return [("c", "dv", i) for i in idx]

    def p1(xrows, j):
        for t4 in range(4):
            p1_piece(xrows, j, t4)

    def p1_piece(xrows, j, t4, part="ab", sfix=None, dst=None):
        s = t4 % 2 if sfix is None else sfix
        hD, KD = (hT, KH) if dst is None else dst
        if "a" in part:
            dma("sp", xst[s], xrows[t4 * 128:(t4 + 1) * 128, :], [], KXST[s], ("xst", s))
            for q in range(8):
                E.op("dve", lambda e, q=q, s=s: e.bn_stats(out=sm[:, 8 + q * 6: 14 + q * 6],
                                                       in_=xst[s][:, q * 512:(q + 1) * 512]),
                     KXST[s], [("sm", "bn")], partial=True)
            E.op("dve", lambda e: e.bn_aggr(out=sm[:, 56:58], in_=sm[:, 8:56]), [("sm", "bn")], [("sm", "agg")])
            stt(sm[:, 58:59], sm[:, 56:57], sm[:, 56:57], sm[:, 57:58], ALU.mult, ALU.add, [("sm", "agg")],
                [("sm", "ms")])
            act(sm[:, 59:60], sm[:, 58:59], AF.Ln, [("sm", "ms")], [("sm", "ln")], bias=EPS)
            act(sm[:, 60:61], sm[:, 59:60], AF.Exp, [("sm", "ln")], [("sm", "rstd")], scale=-0.5)
            ts(xst[s][:, 0:2048], xst[s][:, 0:2048], sm[:, 60:61], ALU.mult, KXST[s] + [("sm", "rstd")],
               KXST[s][0:8], partial=True)
            act(xst[s][:, 2048:4096], xst[s][:, 2048:4096], AF.Copy, KXST[s] + [("sm", "rstd")], KXST[s][8:16],
                scale=sm[:, 60:61], partial=True)
        if "b" in part:
            for g in range(8):
                bank = balloc()
                for q in range(4):
                    kc = 4 * g + q
                    tr(PS[bank][:, q * 128:(q + 1) * 128], xst[s][:, kc * 128:(kc + 1) * 128],
                       [KXST[s][kc // 2]], [KP(bank)])
                for q in range(4):
                    kc = 4 * g + q
                    act(hD[:, kc, t4 * 128:(t4 + 1) * 128], PS[bank][:, q * 128:(q + 1) * 128], AF.Identity,
                        [KP(bank)] + CDVI(0, 1), [KD(kc)], scale=dvv(0, kc, j), bias=dvv(1, kc, j), partial=True)
                bfree(bank)

    def qk_stage0(bank, gcol, rope):
        if DBGQ == 1:
            cp("dve", t_qg, PS[bank][:, :], [KP(bank)], K_QG)
            bfree(bank)
            return
        if DBGQ == 3:
            act(t_sq, PS[bank][:, :], AF.Square, [KP(bank)], K_SQ)
            bfree(bank)
            return
        if DBGQ == 4:
            ts(t_qg, PS[bank][:, :], vcol(gcol), ALU.mult, [KP(bank)] + CK, K_QG)
            bfree(bank)
            return
        if DBGQ == 5:
            act(hT[:, 0, :], PS[bank][:, :], AF.Square, [KP(bank)], [KH(0)])
            bfree(bank)
            return
        act(t_sq, PS[bank][:, :], AF.Square, [KP(bank)], K_SQ)
        ts(t_qg, PS[bank][:, :], vcol(gcol), ALU.mult, [KP(bank)] + CK, K_QG)
        if rope:
            act(t_qgb, PS[bank][:, :], AF.Copy, [KP(bank)] + CK, K_QGB, scale=vcol(gcol))
        bfree(bank)

    def qk_stage1(rope, tcol0, out_bf, out_keys, out_partial, nk_dst=None):
        if DBGQ in (1, 2, 3, 4, 5):
            return
        b1 = balloc()
        mm(PS[b1][:, :], ones_b[:], t_sq, True, True, K_SQ + [("c", "onesb")], [KP(b1)])
        act(t_rs, PS[b1][:, :], AF.Ln, [KP(b1)], K_RS, scale=1.0 / 128.0, bias=EPS)
        bfree(b1)
        act(t_rs, t_rs, AF.Exp, K_RS, K_RS, scale=-0.5)
        if rope:
            b2 = balloc()
            mm(PS[b2][:, :], rot_b[:], t_qgb, True, True, K_QGB + [("c", "rot")], [KP(b2)])
            tt(t_t2, PS[b2][:, :], ropes[:, tcol0:tcol0 + 512], ALU.mult, [KP(b2), ("c", "ropes")], K_T2)
            bfree(b2)
            tt(t_qg, t_qg, ropec[:, tcol0:tcol0 + 512], ALU.mult, K_QG + [("c", "ropec")], K_QG)
            tt(t_qg, t_qg, t_t2, ALU.add, K_QG + K_T2, K_QG)
            tt(out_bf, t_qg, t_rs, ALU.mult, K_QG + K_RS, out_keys, partial=out_partial)
        elif nk_dst is not None:
            tt(t_kn, t_qg, t_rs, ALU.mult, K_QG + K_RS, K_KN)
            cp("act", out_bf, t_kn, K_KN, out_keys, partial=out_partial)
        else:
            tt(t_qg, t_qg, t_rs, ALU.mult, K_QG + K_RS, K_QG)
            cp("act", out_bf, t_qg, K_QG, out_keys, partial=out_partial)

    stc = [0]

    def kv_out(src_f32, src_keys, dst, bf_out, bf_keys):
        b = balloc()
        for q in range(4):
            tr(PS[b][:, q * 128:(q + 1) * 128], src_f32[:, q * 128:(q + 1) * 128], src_keys, [KP(b)])
        if bf_out is not None:
            cp("act", bf_out, PS[b][:, :].rearrange("p (c e) -> p c e", c=4), [KP(b)], bf_keys, partial=True)
        if dst is not None:
            s = stc[0] % 2
            stc[0] += 1
            cp("dve", t_st[s], PS[b][:, :], [KP(b)], K_ST[s])
            dma("sp", dst, t_st[s].rearrange("p (c e) -> p c e", c=4), K_ST[s], [], ("st", s))
        bfree(b)

    def attention_unit(q0, nq, kchs, c, qT, KQ):
        if True:
            if True:
                bo0, bo1, bz = balloc(), balloc(), balloc()
                nk_ = len(kchs)
                sb_ = [None] * nk_

                def smm(i):
                    sb_[i] = balloc()
                    mm(PS[sb_[i]][:, 0:nq], kchs[i][c], qT[:, c, q0:q0 + nq], True, True, kchs[i][3] + KQ,
                       [KP(sb_[i])])
                smm(0)
                for i in range(nk_):
                    if i + 1 < nk_:
                        smm(i + 1)
                    e_ = t_E[i % 2][:, 0:nq]
                    act(e_, PS[sb_[i]][:, 0:nq], AF.Exp, [KP(sb_[i])], K_E[i % 2], scale=1.0 / math.sqrt(128.0))
                    bfree(sb_[i])
                    v_ = kchs[i][2]
                    mm(PS[bo0][:, 0:nq], v_[:, 0:128], e_, i == 0, i == nk_ - 1, kchs[i][3] + K_E[i % 2], [KP(bo0)])
                    mm(PS[bo1][:, 0:nq], v_[:, 128:256], e_, i == 0, i == nk_ - 1, kchs[i][3] + K_E[i % 2],
                       [KP(bo1)])
                    mm(PS[bz][:, 0:nq], ones_b[:], e_, i == 0, i == nk_ - 1, K_E[i % 2] + [("c", "onesb")], [KP(bz)])
                E.op("dve", lambda e, bz=bz, nq=nq: e.reciprocal(out=t_rz[:, 0:nq], in_=PS[bz][:, 0:nq]), [KP(bz)],
                     K_RZ)
                bfree(bz)
                for jj, bo in enumerate((bo0, bo1)):
                    if c == 0:
                        tt(t_R0[:, jj, q0:q0 + nq], PS[bo][:, 0:nq], t_rz[:, 0:nq], ALU.mult, [KP(bo)] + K_RZ, K_R0,
                           partial=True)
                    else:
                        tt(t_t1[:, 0:nq], PS[bo][:, 0:nq], t_rz[:, 0:nq], ALU.mult, [KP(bo)] + K_RZ, K_T1)
                        stt(t_R0[:, jj, q0:q0 + nq], t_t1[:, 0:nq], sm[:, 5:6], t_R0[:, jj, q0:q0 + nq], ALU.mult,
                            ALU.add, K_T1 + K_R0 + [("c", "neglam")], K_R0, partial=True)
                    bfree(bo)

    def attention_b(h):
        for jj in range(2):
            act(t_dsq[:, jj, :], t_R0[:, jj, :], AF.Square, K_R0, K_DSQ, partial=True)
        b = balloc()
        mm(PS[b][:, :], ones_b[:], t_dsq[:, 0, :], True, False, K_DSQ + [("c", "onesb")], [KP(b)])
        mm(PS[b][:, :], ones_b[:], t_dsq[:, 1, :], False, True, K_DSQ + [("c", "onesb")], [KP(b)])
        act(t_rz, PS[b][:, :], AF.Ln, [KP(b)], K_RZ, scale=1.0 / 256.0, bias=EPS)
        bfree(b)
        act(t_rz, t_rz, AF.Exp, K_RZ, K_RZ, scale=-0.5)
        for jj in range(2):
            stt(mixT[:, 2 * h + jj, :], t_R0[:, jj, :], sm[:, 6 + jj:7 + jj], t_rz, ALU.mult, ALU.mult,
                K_R0 + K_RZ + [("c", "sg")], [KM(2 * h + jj)])

    def run_blocks(blocks, after_block=None):
        due = {}
        nb = len(blocks)
        for i, blk in enumerate(blocks):
            wb = wload(blk["wid"])
            bank = balloc()
            main_mm(wb, bank, blk["rhs"], blk["rkeys"], blk["n"])
            if blk.get("extra") is not None:
                blk["extra"](wb)
            for k, fn in enumerate(blk["stages"]):
                due.setdefault(i + k, []).append((i, k, fn, bank))
            for (_, _, fn, bk) in sorted(due.pop(i, []), key=lambda z: (getattr(z[2], "late", False), z[0], z[1])):
                fn(bk)
            if after_block is not None:
                after_block(i)
        for t in sorted(due.keys()):
            for (_, _, fn, bk) in sorted(due[t], key=lambda z: (z[0], z[1])):
                fn(bk)

    def hrhs(n0=0, n=512):
        return [hT[:, kc, n0:n0 + n] for kc in range(32)], [KH(kc) for kc in range(32)]

    def xt_reload_group(xrows, t4, g, s=0):
        if g == 0:
            dma("sp", xst[s], xrows[t4 * 128:(t4 + 1) * 128, :], [], KXST[s], ("xst", s))
        bank = balloc()
        for q in range(4):
            kc = 4 * g + q
            tr(PS[bank][:, q * 128:(q + 1) * 128], xst[s][:, kc * 128:(kc + 1) * 128],
               [KXST[s][kc // 2]], [KP(bank)])
        cp("dve" if g % 2 == 0 else "act", xT[:, 4 * g:4 * g + 4, t4 * 128:(t4 + 1) * 128],
           PS[bank][:, :].rearrange("p (c t) -> p c t", c=4), [KP(bank)], K1(4 * g, 4 * g + 4), partial=True)
        bfree(bank)

    def p2(sample, krow0, xres=None):
        rhs, rkeys = hrhs()
        blocks = []
        for h in range(NH):
            hp = h % 2
            QT, KT_, VH, KQ, KK, KV = QTs[hp], KTs[hp], VHs[hp], KQs[hp], KKs[hp], KVs[hp]
            for c in range(2):
                blocks.append(dict(wid=WIN0 + 6 * h + c, rhs=rhs, rkeys=rkeys, n=512, stages=[
                    (lambda bank: qk_stage0(bank, V_QG, sample)),
                    (lambda bank, c=c, QT=QT, KQ=KQ: qk_stage1(sample, 0, QT[:, c, :], KQ, True)),
                ]))
            for c in range(2):
                if sample:
                    nkd = None
                else:
                    nkd = nk[krow0:krow0 + 512, (2 * h + c) * 128:(2 * h + c + 1) * 128].rearrange(
                        "(c p) d -> p c d", p=128)
                kst = [
                    (lambda bank: qk_stage0(bank, V_KG, sample)),
                    (lambda bank, c=c, nkd=nkd, KT_=KT_, KK=KK: qk_stage1(sample, 0, KT_[:, c, :], KK, True,
                                                                          nk_dst=nkd)),
                ]
                if nkd is not None:
                    kst.append(lambda bank, nkd=nkd: kv_out(t_kn, K_KN, nkd, None, None))
                blocks.append(dict(wid=WIN0 + 6 * h + 2 + c, rhs=rhs, rkeys=rkeys, n=512, stages=kst))
            for jj in range(2):
                if sample:
                    nvd = None
                else:
                    nvd = nv[krow0:krow0 + 512, (2 * h + jj) * 128:(2 * h + jj + 1) * 128].rearrange(
                        "(c p) d -> p c d", p=128)

                def v0(bank):
                    cp("dve", t_vT, PS[bank][:, :], [KP(bank)], K_VT)
                    bfree(bank)

                def v1(bank, jj=jj, nvd=nvd, VH=VH, KV=KV):
                    kv_out(t_vT, K_VT, nvd, VH[:, :, jj * 128:(jj + 1) * 128], KV)
                st = [v0, v1]
                if jj == 1:
                    units = []
                    if sample:
                        kch = []
                        for i in range(4):
                            kch.append((KT_[:, 0, i * 128:(i + 1) * 128], KT_[:, 1, i * 128:(i + 1) * 128],
                                        VH[:, i, :], KK + KV))
                        for i in range(4):
                            kch.append((kTo[:, 2 * h, i * 128:(i + 1) * 128],
                                        kTo[:, 2 * h + 1, i * 128:(i + 1) * 128],
                                        vo[:, i, h * 256:(h + 1) * 256], KTO + KVO))
                        for i in range(2):
                            kch.append((kTc[:, 2 * h, i * 128:(i + 1) * 128],
                                        kTc[:, 2 * h + 1, i * 128:(i + 1) * 128],
                                        vc[:, i, h * 256:(h + 1) * 256], KTC + KVC))
                        for c in range(2):
                            units.append((0, 512, kch, c))
                    else:
                        for s_ in range(2):
                            kch = []
                            for i in range(2):
                                o = s_ * 256 + i * 128
                                kch.append((KT_[:, 0, o:o + 128], KT_[:, 1, o:o + 128], VH[:, 2 * s_ + i, :],
                                            KK + KV))
                            for c in range(2):
                                units.append((s_ * 256, 256, kch, c))
                    for (q0, nq, kch, c) in units:
                        def uf(bank, q0=q0, nq=nq, kch=kch, c=c, QT=QT, KQ=KQ):
                            attention_unit(q0, nq, kch, c, QT, KQ)
                        uf.late = False
                        st.append(uf)

                    def bf(bank, h=h):
                        attention_b(h)
                    bf.late = False
                    st.append(bf)
                blocks.append(dict(wid=WIN0 + 6 * h + 4 + jj, rhs=rhs, rkeys=rkeys, n=512, stages=st))
        for j in range(16):
            base = WIN0 + 48 + 3 * j
            hb = [None]

            def halo(which, hb=hb):
                if not sample:
                    return None

                def ex(wb, which=which, hb=hb):
                    if which == 0:
                        hb[0] = balloc()

                    def fn(e, hb=hb):
                        last = None
                        for kc in range(32):
                            last = e.matmul(PS[hb[0]][:, which * 2:which * 2 + 2],
                                            lhsT=WB[wb][:, kc * 128:(kc + 1) * 128],
                                            rhs=hhalo[:, kc * 2:kc * 2 + 2], start=(kc == 0), stop=(kc == 31))
                        return last
                    E.op("pe", fn, [("WB", wb), ("c", "hhalo")], [KP(hb[0])], partial=True)
                return ex

            def c0(bank):
                cp("act", t_t2, PS[bank][:, :], [KP(bank)], K_T2)
                bfree(bank)

            def u0(bank, j=j, hb=hb):
                tt(t_qg, PS[bank][:, :], t_t2, ALU.mult, [KP(bank)] + K_T2, K_QG)
                bfree(bank)
                w0, w1, w2 = (vcol(V_CONV + 3 * j + t) for t in range(3))
                act(t_rs, t_qg, AF.Copy, K_QG + CK, K_RS, scale=w1)
                ns = 1 if sample else 2
                z3 = t_qg.rearrange("p (s t) -> p s t", s=ns)
                a3 = t_rs.rearrange("p (s t) -> p s t", s=ns)
                L = 512 // ns
                stt(a3[:, :, 1:L], z3[:, :, 0:L - 1], w0, a3[:, :, 1:L], ALU.mult, ALU.add, K_QG + K_RS + CK, K_RS)
                stt(a3[:, :, 0:L - 1], z3[:, :, 1:L], w2, a3[:, :, 0:L - 1], ALU.mult, ALU.add, K_QG + K_RS + CK,
                    K_RS)
                if sample:
                    cp("act", sm[:, 62:64], PS[hb[0]][:, 0:2], [KP(hb[0])], [("sm", "hc")])
                    tt(sm[:, 62:64], PS[hb[0]][:, 2:4], sm[:, 62:64], ALU.mult, [KP(hb[0]), ("sm", "hc")],
                       [("sm", "hc")])
                    bfree(hb[0])
                    tt(sm[:, 62:63], sm[:, 62:63], vcol(V_MASK + 1), ALU.mult, [("sm", "hc")] + CK, [("sm", "hc")])
                    tt(sm[:, 63:64], sm[:, 63:64], vcol(V_MASK), ALU.mult, [("sm", "hc")] + CK, [("sm", "hc")])
                    stt(t_rs[:, 511:512], sm[:, 62:63], w2, t_rs[:, 511:512], ALU.mult, ALU.add,
                        [("sm", "hc")] + K_RS + CK, K_RS)
                    stt(t_rs[:, 0:1], sm[:, 63:64], w0, t_rs[:, 0:1], ALU.mult, ALU.add, [("sm", "hc")] + K_RS + CK,
                        K_RS)

            def b0(bank, j=j):
                tt(mixT[:, 16 + j, :], PS[bank][:, :], t_rs, ALU.mult, [KP(bank)] + K_RS, [KM(16 + j)])
                bfree(bank)
            blocks.append(dict(wid=base + 1, rhs=rhs, rkeys=rkeys, n=512, stages=[c0], extra=halo(0)))
            blocks.append(dict(wid=base + 2, rhs=rhs, rkeys=rkeys, n=512, stages=[u0], extra=halo(1)))
            blocks.append(dict(wid=base + 0, rhs=rhs, rkeys=rkeys, n=512, stages=[b0]))
        if P2LIM is not None:
            blocks = blocks[:P2LIM]
        XR0 = 60

        def hook(i):
            ada_some(1)
            if xres is not None and XR0 <= i < XR0 + 32:
                xt_reload_group(xres, (i - XR0) // 8, (i - XR0) % 8)
        run_blocks(blocks, after_block=hook)

    def s0(do_p1=True, own_p1=None):
        own_dst = (mixT, KM)

        def cache_loads(i):
            if i == 6:
                dma("pool", kTc, ckT.rearrange("p (c t) -> p c t", c=16), [], KTC, "kc", max_dma_last_dim=1024)
                dma("pool", vc, cv.rearrange("(c p) e -> p c e", p=128), [], KVC, "vc", max_dma_last_dim=8192)
            if own_p1 is not None:
                if i >= 2 and (i - 2) % 7 == 0 and (i - 2) // 7 < 4:
                    p1_piece(own_p1[0], own_p1[1], (i - 2) // 7, "a", sfix=0, dst=own_dst)
                if i >= 5 and (i - 5) % 7 == 0 and (i - 5) // 7 < 4:
                    p1_piece(own_p1[0], own_p1[1], (i - 5) // 7, "b", sfix=0, dst=own_dst)
        if do_p1:
            p1(xsx, 1)
        hh = hhalo[:].rearrange("p (c t) -> p c t", c=32)
        hsrc = hT
        E.op("dve", lambda e: e.tensor_copy(out=hh[:, :, 0:1], in_=hsrc[:, :, 0:1]), [KH(kc) for kc in range(32)],
             [("c", "hhalo")], partial=True)
        E.op("dve", lambda e: e.tensor_copy(out=hh[:, :, 1:2], in_=hsrc[:, :, 511:512]),
             [KH(kc) for kc in range(32)], [("c", "hhalo")], partial=True)
        rhs, rkeys = hrhs()
        blocks = []
        for h in range(NH):
            for c in range(2):
                blocks.append(dict(wid=WIN0 + 6 * h + 2 + c, rhs=rhs, rkeys=rkeys, n=512, stages=[
                    (lambda bank: qk_stage0(bank, V_KG, True)),
                    (lambda bank, h=h, c=c: qk_stage1(True, 512, kTo[:, 2 * h + c, :], KTO, True)),
                ]))
            for jj in range(2):
                def v0(bank):
                    cp("dve", t_vT, PS[bank][:, :], [KP(bank)], K_VT)
                    bfree(bank)

                def v1(bank, h=h, jj=jj):
                    kv_out(t_vT, K_VT, None, vo[:, :, h * 256 + jj * 128:h * 256 + (jj + 1) * 128], KVO)
                blocks.append(dict(wid=WIN0 + 6 * h + 4 + jj, rhs=rhs, rkeys=rkeys, n=512, stages=[v0, v1]))
        run_blocks(blocks, after_block=cache_loads)

    def p3(xrows, j, preloaded=False):
        for t4 in range(0 if preloaded else 4):
            s = t4 % 2
            dma("sp", xst[s], xrows[t4 * 128:(t4 + 1) * 128, :], [], KXST[s], ("xst", s))
            for g in range(8):
                bank = balloc()
                for q in range(4):
                    kc = 4 * g + q
                    tr(PS[bank][:, q * 128:(q + 1) * 128], xst[s][:, kc * 128:(kc + 1) * 128],
                       [KXST[s][kc // 2]], [KP(bank)])
                cp("dve" if g % 2 == 0 else "act", xT[:, 4 * g:4 * g + 4, t4 * 128:(t4 + 1) * 128],
                   PS[bank][:, :].rearrange("p (c t) -> p c t", c=4), [KP(bank)], K1(4 * g, 4 * g + 4), partial=True)
                bfree(bank)
        rhs = [mixT[:, kc, :] for kc in range(32)]
        rkeys = [KM(kc) for kc in range(32)]
        blocks = []
        for m in range(32):
            def ep(bank, m=m):
                stt(xT[:, m, :], PS[bank][:, :], dvv(2, m, j), xT[:, m, :], ALU.mult, ALU.add,
                    [KP(bank)] + K1(m) + CDVI(2), K1(m))
                bfree(bank)
            blocks.append(dict(wid=WOUT0 + m, rhs=rhs, rkeys=rkeys, n=512, stages=[ep]))
        ada_some(1000 if ada_next[0] < 96 else 0)
        run_blocks(blocks, after_block=lambda i: ada_some(1) if ada_next[0] < 160 else None)

    def p4(j):
        b = balloc()
        for kc in range(32):
            s = kc % 2
            act(t_E[s], xT[:, kc, :], AF.Square, K1(kc), K_E[s])
            mm(PS[b][:, :], ones_b[:], t_E[s], kc == 0, kc == 31, K_E[s] + [("c", "onesb")], [KP(b)])
        act(t_rs, PS[b][:, :], AF.Ln, [KP(b)], K_RS, scale=1.0 / DM, bias=EPS)
        bfree(b)
        act(t_rs, t_rs, AF.Exp, K_RS, K_RS, scale=-0.5)
        tmp = [t_qg, t_t2]
        ktmp = [K_QG, K_T2]
        for kc in range(32):
            s = kc % 2
            stt(tmp[s], xT[:, kc, :], dvv(3, kc, j), t_rs, ALU.mult, ALU.mult, K1(kc) + K_RS + CDVI(3), ktmp[s])
            act(hT[:, kc, :], tmp[s], AF.Identity, ktmp[s] + CDVI(4), [KH(kc)], bias=dvv(4, kc, j))

    def p56(j, next_p1=None):
        rhs, rkeys = hrhs()
        arhs = [mixT[:, kc, :] for kc in range(32)]
        akeys = [KM(kc) for kc in range(32)]
        tmp = [t_qg, t_t2]
        ktmp = [K_QG, K_T2]
        cnt = [0]
        for g in range(4):
            blocks = []
            for m in range(32):
                def ep(bank, m=m):
                    s = cnt[0] % 2
                    cnt[0] += 1
                    act(tmp[s], PS[bank][:, :], AF.Relu, [KP(bank)], ktmp[s])
                    tt(mixT[:, m, :], PS[bank][:, :], tmp[s], ALU.mult, [KP(bank)] + ktmp[s], [KM(m)])
                    bfree(bank)
                blocks.append(dict(wid=MIN0 + g * 32 + m, rhs=rhs, rkeys=rkeys, n=512, stages=[ep]))
            for m in range(32):
                def ep2(bank, m=m):
                    stt(xT[:, m, :], PS[bank][:, :], dvv(5, m, j), xT[:, m, :], ALU.mult, ALU.add,
                        [KP(bank)] + K1(m) + CDVI(5), K1(m))
                    bfree(bank)
                blocks.append(dict(wid=MOUT0 + g * 32 + m, rhs=arhs, rkeys=akeys, n=512, stages=[ep2]))
            hook = None
            if g == 0 and ada_next[0] < NADA:
                def hook(i):
                    if i % 2 == 0:
                        ada_some(1)
            if g == 3 and next_p1 is not None:
                def hook(i, next_p1=next_p1):
                    if i >= 34 and (i - 34) % 8 == 0 and (i - 34) // 8 < 4:
                        p1_piece(next_p1[0], next_p1[1], (i - 34) // 8, "a")
                    if i >= 38 and (i - 38) % 8 == 0 and (i - 38) // 8 < 4:
                        p1_piece(next_p1[0], next_p1[1], (i - 38) // 8, "b")
            run_blocks(blocks, after_block=hook)
            ada_some(1000)

    def p7(yrows):
        for t4 in range(4):
            s = t4 % 2
            for g in range(8):
                bank = balloc()
                for q in range(4):
                    kc = 4 * g + q
                    tr(PS[bank][:, q * 128:(q + 1) * 128], xT[:, kc, t4 * 128:(t4 + 1) * 128], K1(kc), [KP(bank)])
                cp("dve" if g % 2 == 0 else "act", xst[s][:, g * 512:(g + 1) * 512], PS[bank][:, :], [KP(bank)],
                   KXST[s][2 * g:2 * g + 2], partial=True)
                bfree(bank)
            dma("sp", yrows[t4 * 128:(t4 + 1) * 128, :], xst[s], KXST[s], [], ("xst", s))

    def prompt_tile(r0, do_p1, next_p1):
        if do_p1:
            p1(xp[r0:r0 + 512, :], 0)
        p2(False, r0, xp[r0:r0 + 512, :])
        p3(xp[r0:r0 + 512, :], 0, preloaded=True)
        p4(0)
        p56(0, next_p1)
        p7(yp[r0:r0 + 512, :])

    def sample_tile():
        s0(False, own_p1=(xso, 1))
        swap_hm()
        p2(True, 0, xso)
        p3(xso, 1, preloaded=True)
        p4(1)
        p56(1)
        p7(ys)

    if "all" in STAGES:
        phase0()
        prompt_tile(0, True, (xp[512:1024, :], 0))
        prompt_tile(512, False, (xsx, 1))
        sample_tile()
    else:
        phase0()
        if "p1" in STAGES:
            p1(xp[0:512, :], 0)
        if "p2" in STAGES:
            p2(False, 0)
        if "p3" in STAGES:
            p3(xp[0:512, :], 0)
        if "p4" in STAGES:
            p4(0)
        if "p56" in STAGES:
            p56(0)
        if "p7" in STAGES:
            p7(yp[0:512, :])
        if "sample" in STAGES:
            p1(xsx, 1)
            sample_tile()

    sem_eng = {k: es.enter_context(nc.semaphore(f"s_{k}")) for k in ("pe", "act", "dve")}
    dma_sems = {}
    dma_cnt = {}
    cnt = {"pe": 0, "act": 0, "dve": 0}
    for o in E.ops:
        if o.dma is not None:
            if o.dma not in dma_sems:
                dma_sems[o.dma] = es.enter_context(nc.semaphore(f"d{len(dma_sems)}"))
                dma_cnt[o.dma] = 0
            dma_cnt[o.dma] += 16
            o.val = (dma_sems[o.dma], dma_cnt[o.dma])
        elif o.sig:
            cnt[o.eng] += 1
            o.val = (sem_eng[o.eng], cnt[o.eng])
    per = {k: [o for o in E.ops if o.eng == k] for k in ("pe", "act", "dve", "sp", "pool")}
    block = es.enter_context(nc.Block())

    def replay(eng, ops, final=False):
        waited = {}
        for o in ops:
            for p in o.deps:
                sem, val = p.val
                if waited.get(id(sem), 0) < val:
                    eng.wait_ge(sem, val)
                    waited[id(sem)] = val
            inst = o.fn(eng)
            if o.dma is not None:
                inst.then_inc(o.val[0], 16)
            elif o.sig:
                inst.then_inc(o.val[0], 1)
        if final:
            for k, sem in dma_sems.items():
                eng.wait_ge(sem, dma_cnt[k])

    @block.tensor
    def _(e):
        replay(e, per["pe"])

    @block.scalar
    def _(e):
        replay(e, per["act"])

    @block.vector
    def _(e):
        replay(e, per["dve"])

    @block.gpsimd
    def _(e):
        replay(e, per["pool"])

    @block.sync
    def _(e):
        replay(e, per["sp"], final=True)

    es.close()
    return nc


def _blocks(W):
    K, F = W.shape
    assert K == 4096
    return np.ascontiguousarray(W.reshape(32, 128, F // 128, 128).transpose(2, 1, 0, 3)).reshape(F // 128, 128, 4096)


def _shared(inp):
    f = np.float32
    w_in = np.asarray(inp["w_in"][0], f)
    order = []
    for h in range(8):
        order += [2 * h, 2 * h + 1, 16 + 2 * h, 16 + 2 * h + 1, 32 + 2 * h, 32 + 2 * h + 1]
    for j in range(16):
        order += [48 + j, 64 + j, 80 + j]
    wblk = np.empty((NBLK, 128, 4096), f)
    wblk[ADA0:ADA0 + 192] = _blocks(np.asarray(inp["w_ada"][0], f))
    wblk[WIN0:WIN0 + 96] = _blocks(w_in)[order]
    wblk[WOUT0:WOUT0 + 32] = _blocks(np.asarray(inp["w_out"][0], f))
    wblk[MIN0:MIN0 + 128] = _blocks(np.asarray(inp["w_mlp_in"][0], f))
    wmo = np.asarray(inp["w_mlp_out"][0], f)
    for g in range(4):
        wblk[MOUT0 + 32 * g:MOUT0 + 32 * (g + 1)] = _blocks(wmo[g * 4096:(g + 1) * 4096])
    vecs = np.zeros((128, NV), f)
    vecs[:, V_BADA:V_BADA + 192] = np.asarray(inp["b_ada"][0], f).reshape(192, 128).T
    vecs[:, V_GATT:V_GATT + 32] = np.asarray(inp["norm_attn_g"][0], f).reshape(32, 128).T
    vecs[:, V_GMLP:V_GMLP + 32] = np.asarray(inp["norm_mlp_g"][0], f).reshape(32, 128).T
    vecs[:, V_QG] = np.asarray(inp["q_norm_g"][0], f)
    vecs[:, V_KG] = np.asarray(inp["k_norm_g"][0], f)
    cw = np.asarray(inp["conv_w"][0], f)
    for j in range(16):
        for t in range(3):
            vecs[:, V_CONV + 3 * j + t] = cw[t, j * 128:(j + 1) * 128]
    vecs[:, V_SUB:V_SUB + 2] = np.asarray(inp["subln_g"][0], f).reshape(2, 128).T
    for i, nm in enumerate(("lambda_q1", "lambda_k1", "lambda_q2", "lambda_k2")):
        vecs[:, V_LAM + i] = np.asarray(inp[nm][0], f)
    consts = np.zeros((128, 384), f)
    consts[:, 0:128] = np.eye(128, dtype=f)
    consts[:, 128:256] = 1.0
    for m in range(128):
        if (m % 64) < 32:
            consts[m + 32, 256 + m] = -1.0
        else:
            consts[m - 32, 256 + m] = 1.0
    return wblk, vecs, consts


def _rope_tables(tok):
    f = np.float32
    inv = np.power(f(10000.0), -np.arange(0, 64, 2, dtype=f) / f(64)).astype(f)
    row = (tok // 64).astype(f)
    col = (tok % 64).astype(f)
    ang = np.empty((128, tok.shape[0]), f)
    for p in range(128):
        pos = row if p < 64 else col
        ang[p] = pos * inv[p % 32]
    return np.cos(ang).astype(f), np.sin(ang).astype(f)


def _core_inputs(inp, c, shared):
    f = np.float32
    wblk, vecs, consts = shared
    sbi, par = c // 2, c % 2
    xs = np.asarray(inp["x_sample"][sbi], f)
    own = slice(512 * par, 512 * par + 512)
    oth = slice(512 * (1 - par), 512 * (1 - par) + 512)
    tok = np.concatenate([np.arange(own.start, own.stop), np.arange(oth.start, oth.stop)])
    rc, rs = _rope_tables(tok)
    v = vecs.copy()
    v[:, V_MASK] = 1.0 if par == 1 else 0.0
    v[:, V_MASK + 1] = 1.0 if par == 0 else 0.0
    ck = np.asarray(inp["cache_k"][sbi, 0], f)
    cond = np.stack([np.asarray(inp["c_ctx"], f), np.asarray(inp["c"][sbi], f)], axis=0)
    condT = np.ascontiguousarray(cond.reshape(2, 32, 128).transpose(2, 1, 0)).reshape(128, 64)
    return {
        "wblk": wblk,
        "xp": np.ascontiguousarray(np.asarray(inp["x_prompt"][4 * c:4 * c + 4], f).reshape(1024, DM)),
        "xso": np.ascontiguousarray(xs[own]),
        "xsx": np.ascontiguousarray(xs[oth]),
        "ckT": np.ascontiguousarray(ck.transpose(3, 1, 2, 0)).reshape(128, 4096),
        "cv": np.ascontiguousarray(np.asarray(inp["cache_v"][sbi, 0], f).reshape(256, 2048)),
        "condT": condT,
        "vecs": v,
        "ropec": rc,
        "ropes": rs,
        "consts": consts,
    }


_NC = [None]


def kernel(**inputs):
    if _NC[0] is None:
        _NC[0] = build_program()
    nc = _NC[0]
    shared = _shared(inputs)
    in_maps = [_core_inputs(inputs, c, shared) for c in range(NCORES)]
    res = run_bass_kernel_spmd(nc, in_maps, core_ids=list(range(NCORES)))
    y_p = np.empty((32, 256, DM), np.float32)
    y_s = np.empty((4, 1024, DM), np.float32)
    new_k = np.empty((32, 1, 256, 8, 2, 128), np.float32)
    new_v = np.empty((32, 1, 256, 8, 256), np.float32)
    for c in range(NCORES):
        r = res.results[c]
        y_p[4 * c:4 * c + 4] = r["yp"].reshape(4, 256, DM)
        par = c % 2
        y_s[c // 2, 512 * par:512 * par + 512] = r["ys"]
        new_k[4 * c:4 * c + 4, 0] = r["nk"].reshape(4, 256, 8, 2, 128)
        new_v[4 * c:4 * c + 4, 0] = r["nv"].reshape(4, 256, 8, 256)
    return (y_p, y_s, new_k, new_v)
```
